# Optimizing a Trainium2 kernel written in Bass

```python
import jax, jax.numpy as jnp
from jax import lax
import numpy as np

D_MODEL = 1024
BATCH = 32
SEQ = 256
DEPTH = 4
DEC_BATCH = 4
DEC_SEQ = 1024
PAST_LEN = 256

GRID_W = 64
EPS = 1e-6
H_A = 8
Q_LORA = 256
KV_LORA = 128
NOPE = 64
ROPE = 32
V_A = 64
D_A = H_A * V_A
ROPE_FREQ = ROPE // 4
ROPE_THETA = 10000.0
Q_BLOCK = 128
H_B = 4
DK_B = 128
DV_B = 128
D_B = H_B * DV_B
CONV_K = 3
CHUNK = 64
D_MIX = D_A + D_B
IN_SIZES = (Q_LORA, KV_LORA, ROPE, D_A, H_B * DK_B, H_B * DK_B, H_B * DV_B, D_B, 2 * H_B, 2 * H_B)
IN_DIM = Q_LORA + KV_LORA + ROPE + D_A + 2 * H_B * DK_B + H_B * DV_B + D_B + 4 * H_B

kernel_name = "hybrid_mla_gdn_diffusion_step"

F32 = jnp.float32


def rmsnorm(x, w):
    xf = x.astype(F32)
    y = xf * lax.rsqrt(jnp.mean(xf * xf, axis=-1, keepdims=True) + EPS)
    return (y * w.astype(F32)).astype(x.dtype)


def l2norm(x):
    xf = x.astype(F32)
    return (xf * lax.rsqrt(jnp.sum(xf * xf, axis=-1, keepdims=True) + EPS)).astype(x.dtype)


def axial_rope_angles(n):
    rows = n // GRID_W
    t = jnp.arange(rows * GRID_W)
    row = (t // GRID_W).astype(F32)
    col = (t % GRID_W).astype(F32)
    inv = ROPE_THETA ** (-jnp.arange(ROPE_FREQ, dtype=F32) / ROPE_FREQ)
    ang = jnp.stack([row[:, None] * inv, col[:, None] * inv], axis=1)
    return jnp.cos(ang), jnp.sin(ang)


def apply_rope(x, cos, sin):
    xf = x.astype(F32).reshape(x.shape[:-1] + (2, 2, ROPE_FREQ))
    x1, x2 = xf[..., 0, :], xf[..., 1, :]
    out = jnp.stack([x1 * cos - x2 * sin, x2 * cos + x1 * sin], axis=-2)
    return out.reshape(x.shape).astype(x.dtype)


def mla_attention(q_nope, q_pe, k_nope, k_pe, v):
    B, Sq, H, _ = q_nope.shape
    nblk = Sq // Q_BLOCK
    scale = (NOPE + ROPE) ** -0.5

    def blocks(t):
        return jnp.moveaxis(t.reshape((B, nblk, Q_BLOCK) + t.shape[2:]), 1, 0)

    def one(qs):
        qn, qp = qs
        s = (jnp.einsum('bqhd,bkhd->bhqk', qn, k_nope, preferred_element_type=F32)
             + jnp.einsum('bqhr,bkr->bhqk', qp, k_pe, preferred_element_type=F32)) * scale
        p = jax.nn.softmax(s, axis=-1).astype(v.dtype)
        return jnp.einsum('bhqk,bkhd->bqhd', p, v)

    o = lax.map(one, (blocks(q_nope), blocks(q_pe)))
    return jnp.moveaxis(o, 0, 1).reshape(B, Sq, H * V_A)


def short_conv(x, w):
    pad = CONV_K // 2
    y = lax.conv_general_dilated(x, w[:, None, :].astype(x.dtype), window_strides=(1,),
                                 padding=((pad, pad),), dimension_numbers=('NWC', 'WIO', 'NWC'),
                                 feature_group_count=x.shape[-1])
    return jax.nn.silu(y)


def chunk_gated_delta(q, k, v, g, beta, s0):
    B, S, H, DK = q.shape
    DV = v.shape[-1]
    N = S // CHUNK

    def chunks(t):
        t = t.astype(F32).reshape((B, N, CHUNK) + t.shape[2:])
        return jnp.moveaxis(jnp.moveaxis(t, 1, 0), 2, 3)

    qc = chunks(q) * (DK ** -0.5)
    kc, vc, gch, bc = chunks(k), chunks(v), chunks(g), chunks(beta)
    gc = jnp.cumsum(gch, axis=-1)
    idx = jnp.arange(CHUNK)
    strict = idx[:, None] > idx[None, :]
    incl = idx[:, None] >= idx[None, :]
    diff = gc[..., :, None] - gc[..., None, :]
    decay = jnp.where(incl, jnp.exp(jnp.where(incl, diff, 0.0)), 0.0)
    kb = kc * bc[..., None]
    vb = vc * bc[..., None]
    lower = jnp.where(strict, jnp.einsum('nbhid,nbhjd->nbhij', kb, kc) * decay, 0.0)
    a_mat = lower + jnp.eye(CHUNK, dtype=F32)
    rhs = jnp.concatenate([vb, kb * jnp.exp(gc)[..., None]], axis=-1)
    sol = lax.linalg.triangular_solve(a_mat, rhs, left_side=True, lower=True, unit_diagonal=True)
    u, wk = sol[..., :DV], sol[..., DV:]
    intra = jnp.einsum('nbhid,nbhjd->nbhij', qc, kc) * decay
    qg = qc * jnp.exp(gc)[..., None]
    kd = kc * jnp.exp(gc[..., -1:] - gc)[..., None]
    glast = jnp.exp(gc[..., -1])

    def step(state, xs):
        u_c, w_c, intra_c, qg_c, kd_c, gl_c = xs
        v_new = u_c - jnp.einsum('bhcd,bhde->bhce', w_c, state)
        o = jnp.einsum('bhcd,bhde->bhce', qg_c, state) + jnp.einsum('bhij,bhje->bhie', intra_c, v_new)
        state = state * gl_c[..., None, None] + jnp.einsum('bhcd,bhce->bhde', kd_c, v_new)
        return state, o

    s_fin, o = lax.scan(step, s0.astype(F32), (u, wk, intra, qg, kd, glast))
    o = jnp.moveaxis(jnp.moveaxis(o, 3, 2), 0, 1).reshape(B, S, H, DV)
    return o.astype(v.dtype), s_fin


def mixer_layer(x, cond, norm_w, w_ada, b_ada, w_in, q_a_norm, w_qb, kv_a_norm, w_kvb,
                conv_w, a_log, dt_bias, o_norm, w_out, rope, ctx_ckv, ctx_kpe, s0):
    B, S, _ = x.shape
    mod = jnp.dot(jax.nn.silu(cond), w_ada) + b_ada
    shift, scale, gate = jnp.split(mod[:, None, :], 3, axis=-1)
    h = rmsnorm(x, norm_w) * (1.0 + scale) + shift
    z = jnp.einsum('bsd,de->bse', h, w_in)
    offs = np.cumsum(IN_SIZES)[:-1].tolist()
    q_lat, kv_lat, k_pe, g_a, q_b, k_b, v_b, g_b, a_b, b_b = jnp.split(z, offs, axis=-1)

    q = jnp.einsum('bsr,rf->bsf', rmsnorm(q_lat, q_a_norm), w_qb).reshape(B, S, H_A, NOPE + ROPE)
    q_nope, q_pe = q[..., :NOPE], q[..., NOPE:]
    ckv = rmsnorm(kv_lat, kv_a_norm)
    if rope is not None:
        cos, sin = rope
        q_pe = apply_rope(q_pe, cos[:, None], sin[:, None])
        k_pe_lat = apply_rope(k_pe, cos, sin)
        ckv_all = jnp.concatenate([ckv, ctx_ckv.astype(ckv.dtype)], axis=1)
        kpe_all = jnp.concatenate([k_pe_lat, ctx_kpe.astype(k_pe.dtype)], axis=1)
    else:
        ckv_all, kpe_all = ckv, k_pe
    kv = jnp.einsum('bsr,rf->bsf', ckv_all, w_kvb).reshape(B, -1, H_A, NOPE + V_A)
    o_a = mla_attention(q_nope, q_pe, kv[..., :NOPE], kpe_all, kv[..., NOPE:]) * jax.nn.silu(g_a)

    qkv = short_conv(jnp.concatenate([q_b, k_b, v_b], axis=-1), conv_w)
    qc, kc, vc = jnp.split(qkv, 3, axis=-1)
    qc = l2norm(qc.reshape(B, S, H_B, DK_B))
    kc = l2norm(kc.reshape(B, S, H_B, DK_B))
    vc = vc.reshape(B, S, H_B, DV_B)
    g = -jnp.exp(a_log.astype(F32)) * jax.nn.softplus(a_b.astype(F32).reshape(B, S, 2, H_B) + dt_bias.astype(F32))
    beta = jax.nn.sigmoid(b_b.astype(F32).reshape(B, S, 2, H_B))
    o_f, s_f = chunk_gated_delta(qc, kc, vc, g[:, :, 0], beta[:, :, 0], s0[:, 0])
    flip = lambda t: jnp.flip(t, axis=1)
    o_r, s_r = chunk_gated_delta(flip(qc), flip(kc), flip(vc), flip(g[:, :, 1]), flip(beta[:, :, 1]), s0[:, 1])
    o_b = o_f + flip(o_r)
    o_b = rmsnorm(o_b, o_norm).reshape(B, S, D_B) * jax.nn.silu(g_b)

    out = jnp.einsum('bse,ed->bsd', jnp.concatenate([o_a, o_b], axis=-1), w_out)
    x = x + gate * out
    return x, ckv, k_pe, jnp.stack([s_f, s_r], axis=1)


def setup_inputs(seed: int = 0) -> dict:
    key = jax.random.key(seed)
    ks = jax.random.split(key, 24)
    nrm = lambda k, shape, s: jax.random.normal(k, shape, F32) * s
    dt = jnp.exp(jax.random.uniform(ks[20], (DEPTH, 2, H_B), F32, np.log(1e-3), np.log(1e-1)))
    return {
        "x_prompt": nrm(ks[0], (BATCH, SEQ, D_MODEL), 1.0),
        "x_sample": nrm(ks[1], (DEC_BATCH, DEC_SEQ, D_MODEL), 1.0),
        "cache_ckv": nrm(ks[2], (DEC_BATCH, DEPTH, PAST_LEN, KV_LORA), 1.0),
        "cache_kpe": nrm(ks[3], (DEC_BATCH, DEPTH, PAST_LEN, ROPE), 1.0),
        "state_gdn": nrm(ks[4], (DEC_BATCH, DEPTH, 2, H_B, DK_B, DV_B), 0.1),
        "c": nrm(ks[5], (DEC_BATCH, D_MODEL), 1.0),
        "c_ctx": nrm(ks[6], (D_MODEL,), 1.0),
        "norm_w": 1.0 + nrm(ks[7], (DEPTH, D_MODEL), 0.02),
        "w_ada": nrm(ks[8], (DEPTH, D_MODEL, 3 * D_MODEL), D_MODEL ** -0.5),
        "b_ada": nrm(ks[9], (DEPTH, 3 * D_MODEL), 0.02),
        "w_in": nrm(ks[10], (DEPTH, D_MODEL, IN_DIM), D_MODEL ** -0.5),
        "q_a_norm": 1.0 + nrm(ks[11], (DEPTH, Q_LORA), 0.02),
        "w_qb": nrm(ks[12], (DEPTH, Q_LORA, H_A * (NOPE + ROPE)), Q_LORA ** -0.5),
        "kv_a_norm": 1.0 + nrm(ks[13], (DEPTH, KV_LORA), 0.02),
        "w_kvb": nrm(ks[14], (DEPTH, KV_LORA, H_A * (NOPE + V_A)), KV_LORA ** -0.5),
        "conv_w": nrm(ks[15], (DEPTH, CONV_K, 3 * H_B * DK_B), CONV_K ** -0.5),
        "a_log": jnp.log(jax.random.uniform(ks[16], (DEPTH, 2, H_B), F32, 1.0, 16.0)),
        "dt_bias": dt + jnp.log(-jnp.expm1(-dt)),
        "o_norm": 1.0 + nrm(ks[17], (DEPTH, DV_B), 0.02),
        "w_out": nrm(ks[18], (DEPTH, D_MIX, D_MODEL), D_MIX ** -0.5),
        "final_norm": 1.0 + nrm(ks[19], (D_MODEL,), 0.02),
    }


def reference(x_prompt, x_sample, cache_ckv, cache_kpe, state_gdn, c, c_ctx, norm_w, w_ada, b_ada,
              w_in, q_a_norm, w_qb, kv_a_norm, w_kvb, conv_w, a_log, dt_bias, o_norm, w_out, final_norm):
    rope = axial_rope_angles(x_sample.shape[1])
    s_zero = jnp.zeros((x_prompt.shape[0], 2, H_B, DK_B, DV_B), F32)
    cond_ctx = c_ctx[None, :]
    xp, xs = x_prompt, x_sample
    ckv_list, kpe_list, st_list = [], [], []
    for l in range(DEPTH):
        w = (norm_w[l], w_ada[l], b_ada[l], w_in[l], q_a_norm[l], w_qb[l], kv_a_norm[l], w_kvb[l],
             conv_w[l], a_log[l], dt_bias[l], o_norm[l], w_out[l])
        xp, ckv_l, kpe_l, st_l = mixer_layer(xp, cond_ctx, *w, None, None, None, s_zero)
        ckv_list.append(ckv_l)
        kpe_list.append(kpe_l)
        st_list.append(st_l)
        xs, _, _, _ = mixer_layer(xs, c, *w, rope, cache_ckv[:, l], cache_kpe[:, l], state_gdn[:, l])
    y_prompt = rmsnorm(xp, final_norm)
    y_sample = rmsnorm(xs, final_norm)
    new_ckv = jnp.stack(ckv_list, axis=1)
    new_kpe = jnp.stack(kpe_list, axis=1)
    new_state = jnp.stack(st_list, axis=1)
    return (y_prompt, y_sample, new_ckv, new_kpe, new_state)
```

```python
import numpy as np
from contextlib import ExitStack
import concourse.bass as bass
import concourse.mybir as mybir
from concourse.bass_utils import run_bass_kernel_spmd

F32 = mybir.dt.float32
BF16 = mybir.dt.bfloat16
ALU = mybir.AluOpType
AF = mybir.ActivationFunctionType

NL = 4
EPS = 1e-6
BIG = 30000.0


class Buf:
    __slots__ = ("name", "t", "w", "r", "ps")

    def __init__(self, name, t=None):
        self.name = name
        self.t = t
        self.w = None
        self.r = []
        self.ps = False

    def __getitem__(self, idx):
        return self.t[idx]


def alias(new_bufs, old_bufs):
    evs = []
    for o in old_bufs:
        if o.w is not None:
            evs.append(o.w)
        evs.extend(o.r)
    for n in new_bufs:
        n.w = None
        n.r = list(evs)

    def __getitem__(self, idx):
        return self.t[idx]


class KB:
    def __init__(self, nc, stack, n_dma_sems=8):
        self.nc = nc
        self.stack = stack
        self.eng = {"pe": nc.tensor, "act": nc.scalar, "dve": nc.vector, "pool": nc.gpsimd, "sp": nc.sync}
        self.sem = {}
        self.cnt = {}
        for e in ["pe", "act", "dve", "pool"]:
            self.sem[e] = stack.enter_context(nc.semaphore("s_" + e))
            self.cnt[e] = 0
        self.dsem = []
        for i in range(n_dma_sems):
            k = "d%d" % i
            self.sem[k] = stack.enter_context(nc.semaphore("s_" + k))
            self.cnt[k] = 0
            self.dsem.append(k)
        self.dnext = 0
        self.waited = {}
        self.nops = 0

    def sb(self, name, shape, dt):
        t = self.stack.enter_context(self.nc.sbuf_tensor("sb_" + name, list(shape), dt))
        return Buf(name, t)

    def ps(self, name, shape, dt=F32):
        t = self.stack.enter_context(self.nc.psum_tensor("ps_" + name, list(shape), dt))
        b = Buf(name, t)
        b.ps = True
        return b

    def _need(self, e, ev):
        if ev is None:
            return
        k, v = ev
        if k == e == "pe":
            return
        if v > self.cnt[k]:
            raise RuntimeError("wait on unsignaled op: %s needs %s=%d (have %d)" % (e, k, v, self.cnt[k]))
        if self.waited.get((e, k), 0) >= v:
            return
        self.waited[(e, k)] = v
        self.eng[e].wait_ge(self.sem[k], v)

    def _deps(self, e, rd, wr):
        for b in rd:
            self._need(e, b.w)
            if b.ps:
                for ev in b.r:
                    if ev[0] != e:
                        self._need(e, ev)
        for b in wr:
            self._need(e, b.w)
            for ev in b.r:
                self._need(e, ev)

    def op(self, e, fn, rd=(), wr=(), sig=True):
        self._deps(e, rd, wr)
        ins = fn()
        self.nops += 1
        if sig:
            self.cnt[e] += 1
            ins.then_inc(self.sem[e], 1)
            ev = (e, self.cnt[e])
        else:
            ev = (e, self.cnt[e] + 1)
        for b in wr:
            b.w = ev
            b.r = []
        for b in rd:
            b.r = [x for x in b.r if x[0] != e] + [ev]
        return ins

    def dma(self, out_ap, in_ap, rd=(), wr=(), q="sp"):
        k = self.dsem[self.dnext]
        self.dnext = (self.dnext + 1) % len(self.dsem)
        if self.cnt[k] > 0:
            self._need(q, (k, self.cnt[k]))
        self._deps(q, rd, wr)
        ins = self.eng[q].dma_start(out=out_ap, in_=in_ap)
        self.cnt[k] += 16
        ins.then_inc(self.sem[k], 16)
        ev = (k, self.cnt[k])
        for b in wr:
            b.w = ev
            b.r = []
        for b in rd:
            b.r = b.r + [ev]
        self.nops += 1
        return ins

    def finish(self):
        for k in self.dsem:
            if self.cnt[k] > 0:
                self._need("sp", (k, self.cnt[k]))


def build(n_layers=NL, dbg=False):
    nc = bass.Bass("TRN2", target_bir_lowering=False)

    def din(name, shape, dt=F32):
        return nc.dram_tensor(name, list(shape), dt, kind="ExternalInput").ap()

    def dout(name, shape, dt=F32):
        return nc.dram_tensor(name, list(shape), dt, kind="ExternalOutput").ap()

    xT_d = din("xT", [128, 8, 2048])
    cond_d = din("cond", [128, 8, 2])
    normw_d = din("normw", [128, NL, 8])
    wada_d = din("wada", [NL, 24, 128, 8, 128])
    bada_d = din("bada", [128, NL, 24])
    wtok_d = din("wtok", [NL, 1024, 1456])
    wfeat_d = din("wfeat", [NL, 1024, 1536])
    wqb_d = din("wqb", [NL, 256, 768])
    wkn_d = din("wkn", [NL, 128, 512])
    wv_d = din("wv", [NL, 128, 512])
    wout_d = din("wout", [NL, 8, 128, 8, 128])
    qan_d = din("qan", [128, NL, 256])
    kvn_d = din("kvn", [128, NL, 128])
    onb_d = din("onb", [128, NL, 128])
    convw_d = din("convw", [128, NL, 36])
    alog_d = din("alog", [128, NL, 8])
    dtb_d = din("dtb", [128, NL, 8])
    fnorm_d = din("fnorm", [128, 8])
    cckv_d = din("cckv", [NL, 256, 128])
    ckpe_d = din("ckpe", [NL, 256, 32])
    sgdn_d = din("sgdn", [NL, 8, 128, 128])
    ident_d = din("ident", [128, 128])
    umask_d = din("umask", [128, 2, 128])
    mask_d = din("masks", [128, 4, 128])
    rope_d = din("rope", [128, 8, 32])
    lmask_d = din("lmask", [128, 14, 128])

    yT_d = dout("yT", [128, 8, 2048])
    nckv_d = dout("nckv", [NL, 1024, 128])
    nkpe_d = dout("nkpe", [NL, 1024, 32])
    nst_d = dout("nst", [4, NL, 8, 128, 128])
    xs_d = nc.dram_tensor("xscr", [128, 8, 2048], F32).ap()
    dbg_d = dout("dbg", [128, 2048]) if dbg else None

    class StopBuild(Exception):
        pass

    with ExitStack() as st:
        k = KB(nc, st)
        V, A, G, T = nc.vector, nc.scalar, nc.gpsimd, nc.tensor

        def dve(fn, rd, wr):
            return k.op("dve", fn, rd, wr)

        def act(fn, rd, wr):
            return k.op("act", fn, rd, wr)

        def pool(fn, rd, wr):
            return k.op("pool", fn, rd, wr)

        def pe(fn, rd, wr, sig=True):
            return k.op("pe", fn, rd, wr, sig)

        ident_f = k.sb("ident_f", [128, 128], F32)
        ident_b = k.sb("ident_b", [128, 128], BF16)
        ones_f = k.sb("ones_f", [128, 128], F32)
        ones_b = k.sb("ones_b", [128, 128], BF16)
        mhalf = k.sb("mhalf", [128, 16], F32)
        umask = k.sb("umask", [128, 2, 128], F32)
        masks = k.sb("masks", [128, 4, 128], F32)
        rope = k.sb("rope", [128, 8, 32], F32)
        lmask = k.sb("lmask", [128, 14, 128], BF16)
        cond = k.sb("cond", [128, 8, 2], F32)
        scT = k.sb("scT", [128, 8, 2], F32)
        normw = k.sb("normw", [128, NL, 8], F32)
        bada = k.sb("bada", [128, NL, 24], F32)
        qan = k.sb("qan", [128, 256], F32)
        kvn = k.sb("kvn", [128, 128], F32)
        onb = k.sb("onb", [128, 128], F32)
        convw = k.sb("convw", [128, NL, 36], F32)
        nA = k.sb("nA", [128, NL, 8], F32)
        dtb = k.sb("dtb", [128, NL, 8], F32)
        fnorm = k.sb("fnorm", [128, 8], F32)
        modTs = [k.sb("modT%d" % i, [128, 24, 2], F32) for i in range(2)]
        amods = [k.sb("amod%d" % i, [128, 8, 2], F32) for i in range(2)]

        stg = [k.sb("stg%d" % i, [128, 1024], F32) for i in range(2)]
        wtok = k.sb("wtok", [128, 8, 1456], BF16)
        wfeat = k.sb("wfeat", [128, 8, 1536], BF16)
        wqb = k.sb("wqb", [128, 2, 768], BF16)
        wkn = k.sb("wkn", [128, 512], BF16)
        wv = k.sb("wv", [128, 512], BF16)
        woutm = [k.sb("woutm%d" % i, [128, 8, 128], BF16) for i in range(2)]

        mixT = k.sb("mixT", [128, 8, 1024], BF16)
        gb2 = k.sb("gb2", [128, 8, 512], BF16)
        xab = k.sb("xab", [128, 8, 16], F32)
        gall = k.sb("gall", [128, 8, 8], F32)
        svall = k.sb("svall", [128, 6, 8, 8], F32)
        ball = k.sb("ball", [128, 8, 8], F32)

        arX = k.sb("arX", [128, 8192], F32)
        arY = k.sb("arY", [128, 6144], F32)
        arZ = k.sb("arZ", [128, 7424], F32)

        def view(ar, name, off, nwords, dt, shape, p1=128):
            ap = ar.t[0:p1, off:off + nwords]
            if dt == BF16:
                ap = ap.bitcast(BF16)
            if len(shape) == 3:
                ap = ap.rearrange("p (a b) -> p a b", b=shape[2])
            elif len(shape) == 4:
                ap = ap.rearrange("p (a b c) -> p a b c", b=shape[2], c=shape[3])
            assert tuple(ap.shape) == tuple(shape), (name, ap.shape, shape)
            return Buf(name, ap)

        xblk = view(arX, "xblk", 0, 4096, F32, [128, 8, 512])
        xblk2 = view(arX, "xblk2", 4096, 4096, F32, [128, 8, 512])
        xall = view(arX, "xall", 0, 8192, F32, [128, 8, 1024])
        xm = [Buf("xall%d" % m_, xall.t[:, m_, :]) for m_ in range(8)]
        qT = view(arX, "qT", 0, 2048, BF16, [96, 4, 1024], p1=96)
        kT = view(arX, "kT", 2048, 2560, BF16, [96, 4, 1280], p1=96)
        vext = view(arX, "vext", 4608, 2600, BF16, [128, 10, 8, 65])
        zT = view(arX, "zT", 0, 6192, BF16, [128, 12, 1032])
        oacc = view(arX, "oacc", 0, 4096, F32, [128, 8, 512])
        ktok = view(arX, "ktok", 4096, 2048, BF16, [128, 8, 512])
        vtok = view(arX, "vtok", 6144, 2048, BF16, [128, 8, 512])
        hT = view(arY, "hT", 0, 4096, BF16, [128, 8, 1024])
        qlnT = view(arY, "qlnT", 4096, 1024, BF16, [128, 2, 1024])
        ckvT = view(arY, "ckvT", 5120, 640, BF16, [128, 1280])
        kpe = view(arY, "kpe", 5760, 320, F32, [128, 10, 32])
        cT = view(arY, "cT", 0, 6144, BF16, [128, 12, 1024])

        class Rot:
            def __init__(s, bufs):
                s.b = bufs
                s.i = 0

            def get(s):
                b = s.b[s.i]
                s.i = (s.i + 1) % len(s.b)
                return b

        def rot(name, shape, dt, n):
            return Rot([k.sb("%s%d" % (name, i), shape, dt) for i in range(n)])

        zo = [0]

        def zview(name, nwords, dt, shape):
            b = view(arZ, name, zo[0], nwords, dt, shape)
            zo[0] += nwords
            return b

        ga2 = zview("ga2", 2048, BF16, [128, 8, 512])
        oa = zview("oa", 2048, BF16, [128, 8, 512])
        ptb = Rot([zview("ptb%d" % i, 256, BF16, [128, 512]) for i in range(3)])
        qsb = Rot([zview("qsb%d" % i, 384, BF16, [128, 8, 96]) for i in range(2)])
        ksb = Rot([zview("ksb%d" % i, 384, BF16, [128, 8, 96]) for i in range(2)])
        Z_att = [ga2, oa] + ptb.b + qsb.b + ksb.b
        assert zo[0] <= 7424
        arW = k.sb("arW", [128, 4736], F32)
        wo = [0]

        def wview(name, nwords, dt, shape):
            b = view(arW, name, wo[0], nwords, dt, shape)
            wo[0] += nwords
            return b

        zA = Rot([wview("zA%d" % i, 432, F32, [128, 432]) for i in range(2)])
        th = Rot([wview("th%d" % i, 512, F32, [128, 512]) for i in range(3)])
        tb = Rot([wview("tb%d" % i, 256, BF16, [128, 512]) for i in range(2)])
        ckvs = Rot([wview("ckvs%d" % i, 128, F32, [128, 128]) for i in range(2)])
        ckvb = Rot([wview("ckvb%d" % i, 192, BF16, [128, 384]) for i in range(2)])
        dg = Rot([wview("dg%d" % i, 192, BF16, [128, 3, 128]) for i in range(2)])
        ctx_f = wview("ctx_f", 256, F32, [128, 2, 128])
        rsb = wview("rsb", 512, F32, [128, 512])
        W_small = zA.b + th.b + tb.b + ckvs.b + ckvb.b + dg.b + [ctx_f, rsb]
        assert wo[0] <= 4736, wo[0]

        def gdn_set(mk_f32, mk_bf16, tag):
            S = {}
            for nm in ("gE", "gEt", "gUg", "gu"):
                S[nm] = Rot([mk_f32(nm + tag)])
            for nm in ("gP", "gPt", "gXs", "gXst", "gY", "gin", "gvb", "gkb", "gkd", "gwT", "gvn"):
                S[nm] = Rot([mk_bf16(nm + tag)])
            S["gT"] = Rot([mk_bf16("gT%d%s" % (i, tag)) for i in range(2)])
            S["gTt"] = Rot([mk_bf16("gTt%d%s" % (i, tag)) for i in range(2)])
            S["all"] = [b for r in S.values() for b in r.b]
            return S

        zo[0] = 0
        set0 = gdn_set(lambda n: zview(n, 512, F32, [128, 4, 128]), lambda n: zview(n, 256, BF16, [128, 4, 128]), "a")
        Sf = [zview("Sf%d" % d, 512, F32, [128, 4, 128]) for d in range(2)]
        Sb = [zview("Sb%d" % d, 256, BF16, [128, 4, 128]) for d in range(2)]
        assert zo[0] <= 7424, zo[0]
        yo = [4096]

        def yview(name):
            b = view(arY, name, yo[0], 512, F32, [128, 4, 128])
            yo[0] += 512
            return b

        wo[0] = 0
        set1 = gdn_set(yview, lambda n: wview(n, 256, BF16, [128, 4, 128]), "b")
        assert yo[0] <= 6144 and wo[0] <= 4736
        Z_gdn = set0["all"] + Sf + Sb
        W_gdn = [b for b in set1["all"] if not b.name.startswith(("gE", "gUg", "gu"))]
        Y_gdn = [b for b in set1["all"] if b.name.startswith(("gE", "gUg", "gu"))]
        gsets = [set0, set1]

        stt = rot("stt", [128, 16], F32, 4)
        gst = rot("gst", [128, 8, 4], F32, 4)

        pp = [k.ps("pp%d" % i, [128, 512], F32) for i in range(4)]
        pacc = [k.ps("pacc%d" % i, [128, 512], F32) for i in range(2)]
        ptr = [k.ps("ptr%d" % i, [128, 1024], BF16) for i in range(2)]
        rr = {"pp": 0, "pacc": 0, "ptr": 0, "pp6": 0}
        pm_ap = ptr[0].t[:, 512:1024].bitcast(F32)

        pp6 = pp + pacc

        def nxt(kind):
            lst = {"pp": pp, "pacc": pacc, "ptr": ptr, "pp6": pp6}[kind]
            b = lst[rr[kind] % len(lst)]
            rr[kind] += 1
            return b

        k.dma(ident_f[:], ident_d, wr=[ident_f])
        k.dma(umask[:], umask_d, wr=[umask])
        k.dma(masks[:], mask_d, wr=[masks])
        k.dma(rope[:], rope_d, wr=[rope])
        k.dma(cond[:], cond_d, wr=[cond])
        k.dma(normw[:], normw_d, wr=[normw])
        k.dma(bada[:], bada_d, wr=[bada])
        k.dma(convw[:], convw_d, wr=[convw])
        k.dma(nA[:], alog_d, wr=[nA])
        k.dma(dtb[:], dtb_d, wr=[dtb])
        k.dma(fnorm[:], fnorm_d, wr=[fnorm])
        pool(lambda: G.tensor_copy(out=ident_b[:], in_=ident_f[:]), [ident_f], [ident_b])
        for hf in range(2):
            s_ = stg[hf]
            sv_ = s_[:, 0:896].rearrange("p (a f) -> p a f", f=128)
            k.dma(sv_, lmask_d[:, hf * 7:(hf + 1) * 7, :], wr=[s_])
            pool(lambda: G.tensor_copy(out=lmask[:, hf * 7:(hf + 1) * 7, :], in_=sv_), [s_], [lmask])
        pool(lambda: G.memset(ones_f[:], 1.0), [], [ones_f])
        pool(lambda: G.memset(ones_b[:], 1.0), [], [ones_b])
        pool(lambda: G.memset(mhalf[:], -0.5), [], [mhalf])
        act(lambda: A.activation(out=nA[:], in_=nA[:], func=AF.Exp), [nA], [nA])
        dve(lambda: V.tensor_scalar(out=nA[:], in0=nA[:], scalar1=-1.0, scalar2=None, op0=ALU.mult), [nA], [nA])
        act(lambda: A.activation(out=scT[:], in_=cond[:], func=AF.Tanh, scale=0.5), [cond], [scT])
        dve(lambda: V.scalar_tensor_tensor(out=scT[:], in0=scT[:], scalar=1.0, in1=cond[:], op0=ALU.add, op1=ALU.mult), [scT, cond], [scT])
        dve(lambda: V.tensor_scalar(out=scT[:], in0=scT[:], scalar1=0.5, scalar2=None, op0=ALU.mult), [scT], [scT])

        stg_i = [0]

        lc_fixed = [False]

        def load_cast_tasks(dst_ap, dst_buf, src_ap, n):
            nch = (n + 1023) // 1024
            w = n // nch
            assert w * nch == n
            tasks = []
            for c in range(nch):
                def t_(c=c):
                    if lc_fixed[0]:
                        s = stg[0]
                    else:
                        s = stg[stg_i[0] % 2]
                        stg_i[0] += 1
                    k.dma(s[:, 0:w], src_ap[:, c * w:(c + 1) * w], wr=[s])
                    pool(lambda: G.tensor_copy(out=dst_ap[:, c * w:(c + 1) * w], in_=s[:, 0:w]), [s], [dst_buf])
                tasks.append(t_)
            return tasks

        def weight_tasks(l):
            ts = []
            for kk in range(8):
                ts += load_cast_tasks(wtok[:, kk, :], wtok, wtok_d[l, kk * 128:(kk + 1) * 128, :], 1456)
            for kk in range(8):
                ts += load_cast_tasks(wfeat[:, kk, :], wfeat, wfeat_d[l, kk * 128:(kk + 1) * 128, :], 1536)
            for kk in range(2):
                ts += load_cast_tasks(wqb[:, kk, :], wqb, wqb_d[l, kk * 128:(kk + 1) * 128, :], 768)
            ts += load_cast_tasks(wkn[:], wkn, wkn_d[l], 512)
            ts += load_cast_tasks(wv[:], wv, wv_d[l], 512)
            ts.append(lambda: k.dma(qan[:], qan_d[:, l, :], wr=[qan]))
            ts.append(lambda: k.dma(kvn[:], kvn_d[:, l, :], wr=[kvn]))
            return ts

        def mod_tasks(l, direct=False):
            modT, amod = modTs[l % 2], amods[l % 2]
            st_ = {}
            ts = []

            def abuf(ch):
                return stg[ch % 2] if direct else stg[1]

            def ada_dma(ch):
                s = abuf(ch)
                sv = s[:, 0:1024].rearrange("p (k c) -> p k c", c=128)
                k.dma(sv, wada_d[l, ch], wr=[s])

            def chunk(ch):
                s = abuf(ch)
                sv = s[:, 0:1024].rearrange("p (k c) -> p k c", c=128)
                for kk in range(8):
                    pe(lambda: T.matmul(pm_ap[:, ch * 2:ch * 2 + 2], sv[:, kk, :], scT[:, kk, :],
                                        start=(kk == 0), stop=(kk == 7), skip_group_check=True), [s, scT], [ptr[0]], sig=(kk == 7))
                nx = ch + (2 if direct else 1)
                if nx < 24:
                    ada_dma(nx)

            def fin():
                dve(lambda: V.tensor_tensor(out=modT[:], in0=pm_ap[:, 0:48].rearrange("p (c t) -> p c t", t=2),
                                            in1=bada[:, l, :].unsqueeze(2).to_broadcast([128, 24, 2]), op=ALU.add), [ptr[0], bada], [modT])
                dve(lambda: V.scalar_tensor_tensor(out=amod[:], in0=modT[:, 8:16, :], scalar=1.0,
                                                   in1=normw[:, l, :].unsqueeze(2).to_broadcast([128, 8, 2]),
                                                   op0=ALU.add, op1=ALU.mult), [modT, normw], [amod])
            ts.append(lambda: ada_dma(0))
            if direct:
                ts.append(lambda: ada_dma(1))
            for ch in range(24):
                ts.append(lambda ch=ch: chunk(ch))
            ts.append(fin)
            return ts

        bg = []

        def bg_step(n):
            for _ in range(n):
                if bg:
                    bg.pop(0)()

        xsrcs = [Buf("xsrc0"), Buf("xsrc1")]

        def fm_norm_stats(src, src_ap, nfeat, eps_):
            src_of = src if callable(src) else (lambda kk_: src)
            ps = nxt("pp")
            for kk in range(8):
                sq = th.get()
                act(lambda: A.activation(out=sq[:], in_=src_ap(kk), func=AF.Square), [src_of(kk)], [sq])
                pe(lambda: T.matmul(ps[:], ones_f[:], sq[:], start=(kk == 0), stop=(kk == 7)), [ones_f, sq], [ps], sig=True)
            t1 = rsb
            act(lambda: A.activation(out=t1[:], in_=ps[:], func=AF.Ln, scale=1.0 / nfeat, bias=eps_), [ps], [t1])
            act(lambda: A.activation(out=t1[:], in_=t1[:], func=AF.Exp, scale=-0.5), [t1], [t1])
            return t1

        def layer(l, last):
            pending_w = []
            if l == 0:
                for t_ in mod_tasks(0, direct=True):
                    t_()
                pending_w = weight_tasks(0)
            bg_step(len(bg))
            lc_fixed[0] = False
            modT, amod = modTs[l % 2], amods[l % 2]
            k.dma(onb[:], onb_d[:, l, :], wr=[onb])
            xin = xT_d if l == 0 else xs_d
            for grp in range(2):
                ci = 1 if grp == 0 else 0
                xsrc = xsrcs[grp]
                t0g = grp * 1024
                nseq = 1 if grp == 0 else 4
                tseq = 1024 // nseq
                tps = tseq // 128
                lp = tseq + 2

                def zpos(t):
                    return (t // tseq) * lp + 1 + (t % tseq)

                alias([xblk, xblk2], xm + [oacc, ktok, vtok, zT, qT, kT, vext])
                alias([hT, qlnT, ckvT, kpe], [cT] + Y_gdn)
                alias(Z_att, Z_gdn)
                alias(W_small, W_gdn)
                xbs = (xblk, xblk2)
                for blk in range(2):
                    c0 = t0g + blk * 512
                    k.dma(xbs[blk][:], xin[:, :, c0:c0 + 512], rd=[xsrc], wr=[xbs[blk]])
                for blk in range(2):
                    xb = xbs[blk]
                    rs = fm_norm_stats(xb, lambda kk_: xb[:, kk_, :], 1024, EPS)
                    for kk in range(8):
                        tmp = th.get()
                        dve(lambda: V.tensor_tensor(out=tmp[:], in0=xb[:, kk, :], in1=rs[:], op=ALU.mult), [xb, rs], [tmp])
                        act(lambda: A.activation(out=hT[:, kk, blk * 512:(blk + 1) * 512], in_=tmp[:], func=AF.Identity,
                                                 scale=amod[:, kk, ci:ci + 1], bias=modT[:, kk, ci:ci + 1]), [tmp, amod, modT], [hT])

                while pending_w:
                    pending_w.pop(0)()
                def p2_mm(tl_):
                    tc_ = tl_ * 128
                    bks = (nxt("pp6"), nxt("pp6"), nxt("pp6"))
                    for (pz, c0, n) in ((bks[0], 0, 432), (bks[1], 432, 512), (bks[2], 944, 512)):
                        for kk in range(8):
                            pe(lambda: T.matmul(pz[:, 0:n], hT[:, kk, tc_:tc_ + 128], wtok[:, kk, c0:c0 + n],
                                                start=(kk == 0), stop=(kk == 7)), [hT, wtok], [pz], sig=(kk == 7))
                    return bks

                bks_next = p2_mm(0)
                for tl in range(8):
                    tc0 = tl * 128
                    pzA, pga, pgb = bks_next
                    if tl + 1 < 8:
                        bks_next = p2_mm(tl + 1)
                    z = zA.get()
                    act(lambda: A.copy(out=z[:], in_=pzA[:, 0:432]), [pzA], [z])
                    if dbg and tl == 0 and grp == 0:
                        k.dma(dbg_d[:, 0:512], rsb[:], rd=[rsb])
                        k.dma(dbg_d[:, 512:560], modT[:].rearrange("p c t -> p (c t)"), rd=[modT])
                        k.dma(dbg_d[:, 560:576], amod[:].rearrange("p c t -> p (c t)"), rd=[amod])
                        k.dma(dbg_d[:, 576:592], scT[:].rearrange("p c t -> p (c t)"), rd=[scT])
                        tq = th.get()
                        pool(lambda: G.tensor_copy(out=tq[:], in_=hT[:, 0, 0:512]), [hT], [tq])
                        k.dma(dbg_d[:, 1024:1536], tq[:], rd=[tq])
                        k.dma(dbg_d[:, 1536:1968], z[:], rd=[z])
                        tq2 = th.get()
                        pool(lambda: G.tensor_copy(out=tq2[:], in_=wtok[:, 0, 0:512]), [wtok], [tq2])
                        k.dma(dbg_d[:, 592:1024], tq2[:, 0:432], rd=[tq2])
                        raise StopBuild()
                    for (pg, gdst) in ((pga, ga2), (pgb, gb2)):
                        t1 = th.get()
                        act(lambda: A.activation(out=t1[:], in_=pg[:], func=AF.Tanh, scale=0.5), [pg], [t1])
                        dve(lambda: V.scalar_tensor_tensor(out=gdst[:, tl, :], in0=t1[:], scalar=1.0, in1=pg[:],
                                                           op0=ALU.add, op1=ALU.mult), [t1, pg], [gdst])
                    s1 = stt.get()
                    junk = th.get()
                    pool(lambda: G.memset(s1[:], 0.0), [], [s1])
                    act(lambda: A.activation(out=junk[:, 0:256], in_=z[:, 0:256], func=AF.Square, accum_out=s1[:, 0:1]), [z], [junk, s1])
                    act(lambda: A.activation(out=junk[:, 256:384], in_=z[:, 256:384], func=AF.Square, accum_out=s1[:, 1:2]), [z], [junk, s1])
                    dve(lambda: V.tensor_scalar(out=s1[:, 2:3], in0=s1[:, 0:1], scalar1=1.0 / 256, scalar2=EPS, op0=ALU.mult, op1=ALU.add), [s1], [s1])
                    dve(lambda: V.tensor_scalar(out=s1[:, 3:4], in0=s1[:, 1:2], scalar1=1.0 / 128, scalar2=EPS, op0=ALU.mult, op1=ALU.add), [s1], [s1])
                    pool(lambda: G.tensor_tensor(out=s1[:, 4:6], in0=s1[:, 2:4], in1=mhalf[:, 0:2], op=ALU.pow), [s1, mhalf], [s1])
                    cb = ckvb.get()
                    dve(lambda: V.scalar_tensor_tensor(out=cb[:, 0:256], in0=z[:, 0:256], scalar=s1[:, 4:5], in1=qan[:],
                                                       op0=ALU.mult, op1=ALU.mult), [z, s1, qan], [cb])
                    cs = ckvs.get()
                    dve(lambda: V.scalar_tensor_tensor(out=cs[:], in0=z[:, 256:384], scalar=s1[:, 5:6], in1=kvn[:],
                                                       op0=ALU.mult, op1=ALU.mult), [z, s1, kvn], [cs])
                    pool(lambda: G.tensor_copy(out=cb[:, 256:384], in_=cs[:]), [cs], [cb])
                    if grp == 1:
                        k.dma(nckv_d[l, tc0:tc0 + 128, :], cs[:], rd=[cs])
                    pt = nxt("ptr")
                    for j in range(3):
                        pe(lambda: T.transpose(pt[:, j * 128:(j + 1) * 128], cb[:, j * 128:(j + 1) * 128], ident_b[:]), [cb, ident_b], [pt], sig=(j == 2))
                    dve(lambda: V.tensor_copy(out=qlnT[:, :, tc0:tc0 + 128], in_=pt[:, 0:256].rearrange("p (a b) -> p a b", b=128)), [pt], [qlnT])
                    act(lambda: A.copy(out=ckvT[:, tc0:tc0 + 128], in_=pt[:, 256:384]), [pt], [ckvT])
                    if grp == 0:
                        xv = z[:, 384:416].rearrange("p (a h f) -> p a h f", a=2, h=2)
                        cosv = rope[:, tl, 0:16].rearrange("p (a f) -> p a f", a=2)
                        sinv = rope[:, tl, 16:32].rearrange("p (a f) -> p a f", a=2)
                        ov = kpe[:, tl, :].rearrange("p (a h f) -> p a h f", a=2, h=2)
                        t1 = stt.get()
                        t1v = t1[:, 0:16].rearrange("p (a f) -> p a f", a=2)
                        pool(lambda: G.tensor_tensor(out=ov[:, :, 0, :], in0=xv[:, :, 0, :], in1=cosv, op=ALU.mult), [z, rope], [kpe])
                        pool(lambda: G.tensor_tensor(out=t1v, in0=xv[:, :, 1, :], in1=sinv, op=ALU.mult), [z, rope], [t1])
                        pool(lambda: G.tensor_tensor(out=ov[:, :, 0, :], in0=ov[:, :, 0, :], in1=t1v, op=ALU.subtract), [kpe, t1], [kpe])
                        pool(lambda: G.tensor_tensor(out=ov[:, :, 1, :], in0=xv[:, :, 1, :], in1=cosv, op=ALU.mult), [z, rope], [kpe])
                        pool(lambda: G.tensor_tensor(out=t1v, in0=xv[:, :, 0, :], in1=sinv, op=ALU.mult), [z, rope], [t1])
                        pool(lambda: G.tensor_tensor(out=ov[:, :, 1, :], in0=ov[:, :, 1, :], in1=t1v, op=ALU.add), [kpe, t1], [kpe])
                    else:
                        pool(lambda: G.tensor_copy(out=kpe[:, tl, :], in_=z[:, 384:416]), [z], [kpe])
                        k.dma(nkpe_d[l, tc0:tc0 + 128, :], z[:, 384:416], rd=[z])
                    pool(lambda: G.tensor_copy(out=xab[:, tl, :], in_=z[:, 416:432]), [z], [xab])

                t1 = stt.get()
                t2 = stt.get()
                t1v = t1[:, 0:64].rearrange("p (t c) -> p t c", c=8) if False else None
                gtmp = th.get()
                gv = gtmp[:, 0:64].rearrange("p (t c) -> p t c", c=8)
                dve(lambda: V.tensor_tensor(out=gv, in0=xab[:, :, 0:8], in1=dtb[:, l, :].unsqueeze(1).to_broadcast([128, 8, 8]), op=ALU.add), [xab, dtb], [gtmp])
                act(lambda: A.activation(out=gv, in_=gv, func=AF.Exp), [gtmp], [gtmp])
                act(lambda: A.activation(out=gv, in_=gv, func=AF.Ln, bias=1.0), [gtmp], [gtmp])
                dve(lambda: V.tensor_tensor(out=gall[:], in0=gv, in1=nA[:, l, :].unsqueeze(1).to_broadcast([128, 8, 8]), op=ALU.mult), [gtmp, nA], [gall])
                act(lambda: A.activation(out=ball[:], in_=xab[:, :, 8:16], func=AF.Tanh, scale=0.5), [xab], [ball])
                dve(lambda: V.tensor_scalar(out=ball[:], in0=ball[:], scalar1=0.5, scalar2=0.5, op0=ALU.mult, op1=ALU.add), [ball], [ball])
                pgc = nxt("pp")
                for tl_ in range(8):
                    for d_ in range(2):
                        c_ = tl_ * 8 + d_ * 4
                        pe(lambda: T.matmul(pgc[:, c_:c_ + 4], umask[:, d_, :], gall[:, tl_, d_ * 4:(d_ + 1) * 4], start=True, stop=True, skip_group_check=True),
                           [umask, gall], [pgc], sig=(tl_ == 7 and d_ == 1))
                pgv = pgc[:, 0:64].rearrange("p (t c) -> p t c", c=8)
                dve(lambda: V.tensor_copy(out=svall[:, 0], in_=pgv), [pgc], [svall])
                dve(lambda: V.tensor_scalar(out=svall[:, 1], in0=pgv, scalar1=-1.0, scalar2=None, op0=ALU.mult), [pgc], [svall])
                act(lambda: A.activation(out=svall[:, 2], in_=svall[:, 0], func=AF.Exp), [svall], [svall])
                dve(lambda: V.tensor_scalar(out=svall[:, 3], in0=ball[:], scalar1=-1.0, scalar2=None, op0=ALU.mult), [ball], [svall])
                dve(lambda: V.tensor_scalar(out=svall[:, 4], in0=ball[:], scalar1=0.5, scalar2=None, op0=ALU.mult), [ball], [svall])
                dve(lambda: V.tensor_tensor(out=svall[:, 5], in0=ball[:], in1=svall[:, 2], op=ALU.mult), [ball, svall], [svall])

                alias([qT, kT, vext], [xblk, xblk2])
                pool(lambda: G.memset(vext[:, :, :, 64:65], 2.0), [], [vext])
                nkt = 10 if grp == 0 else 8
                if grp == 0:
                    k.dma(ctx_f[:], cckv_d[l].rearrange("(a p) r -> p a r", p=128), wr=[ctx_f])
                    k.dma(kpe[:, 8:10, :], ckpe_d[l].rearrange("(a p) r -> p a r", p=128), wr=[kpe])
                    for a_ in range(2):
                        pc = nxt("pp")
                        pe(lambda: T.transpose(pc[:, 0:128], ctx_f[:, a_, :], ident_f[:]), [ctx_f, ident_f], [pc])
                        act(lambda: A.copy(out=ckvT[:, 1024 + a_ * 128:1024 + (a_ + 1) * 128], in_=pc[:, 0:128]), [pc], [ckvT])
                for kt in range(nkt):
                    pv = nxt("pp")
                    pe(lambda: T.matmul(pv[:], ckvT[:, kt * 128:(kt + 1) * 128], wv[:], start=True, stop=True), [ckvT, wv], [pv])
                    act(lambda: A.copy(out=vext[:, kt, :, 0:64], in_=pv[:].rearrange("p (h d) -> p h d", d=64)), [pv], [vext])
                for hg in range(2):
                    def k_mm(kt_):
                        pk_ = nxt("pp")
                        pe(lambda: T.matmul(pk_[:, 0:256], ckvT[:, kt_ * 128:(kt_ + 1) * 128], wkn[:, hg * 256:(hg + 1) * 256], start=True, stop=True), [ckvT, wkn], [pk_])
                        return pk_

                    pk_next = k_mm(0)
                    for kt in range(nkt):
                        pk = pk_next
                        if kt + 1 < nkt:
                            pk_next = k_mm(kt + 1)
                        ks = ksb.get()
                        dve(lambda: V.tensor_copy(out=ks[:, 0:4, 0:64], in_=pk[:, 0:256].rearrange("p (h d) -> p h d", d=64)), [pk], [ks])
                        pool(lambda: G.tensor_copy(out=ks[:, 0:4, 64:96], in_=kpe[:, kt, :].unsqueeze(1).to_broadcast([128, 4, 32])), [kpe], [ks])
                        pt = nxt("ptr")
                        for h in range(4):
                            pe(lambda: T.transpose(pt[0:96, h * 128:(h + 1) * 128], ks[:, h, :], ident_b[:]), [ks, ident_b], [pt], sig=(h == 3))
                        act(lambda: A.copy(out=kT[:, :, kt * 128:(kt + 1) * 128], in_=pt[0:96, 0:512].rearrange("p (h t) -> p h t", t=128)), [pt], [kT])
                    def q_mm(tl_):
                        tc_ = tl_ * 128
                        pq_ = nxt("pp")
                        for (c0, n, o0) in ((hg * 256, 256, 0), (512 + hg * 128, 128, 256)):
                            for kk in range(2):
                                pe(lambda: T.matmul(pq_[:, o0:o0 + n], qlnT[:, kk, tc_:tc_ + 128], wqb[:, kk, c0:c0 + n],
                                                    start=(kk == 0), stop=(kk == 1), skip_group_check=True), [qlnT, wqb], [pq_], sig=(kk == 1 and o0 == 256))
                        return pq_

                    pq_next = q_mm(0)
                    for tl in range(8):
                        tc0 = tl * 128
                        pq = pq_next
                        if tl + 1 < 8:
                            pq_next = q_mm(tl + 1)
                        qs = qsb.get()
                        dve(lambda: V.tensor_copy(out=qs[:, 0:4, 0:64], in_=pq[:, 0:256].rearrange("p (h d) -> p h d", d=64)), [pq], [qs])
                        if grp == 0:
                            xv = pq[:, 256:384].rearrange("p (h a g f) -> p h a g f", h=4, a=2, g=2)
                            ov = qs[:, 0:4, 64:96].rearrange("p h (a g f) -> p h a g f", a=2, g=2)
                            cosv = rope[:, tl, 0:16].rearrange("p (a f) -> p a f", a=2).unsqueeze(1).to_broadcast([128, 4, 2, 8])
                            sinv = rope[:, tl, 16:32].rearrange("p (a f) -> p a f", a=2).unsqueeze(1).to_broadcast([128, 4, 2, 8])
                            ta = th.get()
                            tav = ta[:, 0:64].rearrange("p (h a f) -> p h a f", h=4, a=2)
                            tbv = ta[:, 64:128].rearrange("p (h a f) -> p h a f", h=4, a=2)
                            dve(lambda: V.tensor_tensor(out=tav, in0=xv[:, :, :, 0, :], in1=cosv, op=ALU.mult), [pq, rope], [ta])
                            dve(lambda: V.tensor_tensor(out=tbv, in0=xv[:, :, :, 1, :], in1=sinv, op=ALU.mult), [pq, rope], [ta])
                            dve(lambda: V.tensor_tensor(out=ov[:, :, :, 0, :], in0=tav, in1=tbv, op=ALU.subtract), [ta], [qs])
                            dve(lambda: V.tensor_tensor(out=tav, in0=xv[:, :, :, 1, :], in1=cosv, op=ALU.mult), [pq, rope], [ta])
                            dve(lambda: V.tensor_tensor(out=tbv, in0=xv[:, :, :, 0, :], in1=sinv, op=ALU.mult), [pq, rope], [ta])
                            dve(lambda: V.tensor_tensor(out=ov[:, :, :, 1, :], in0=tav, in1=tbv, op=ALU.add), [ta], [qs])
                        else:
                            dve(lambda: V.tensor_copy(out=qs[:, 0:4, 64:96], in_=pq[:, 256:384].rearrange("p (h d) -> p h d", d=32)), [pq], [qs])
                        pt = nxt("ptr")
                        for h in range(4):
                            pe(lambda: T.transpose(pt[0:96, h * 128:(h + 1) * 128], qs[:, h, :], ident_b[:]), [qs, ident_b], [pt], sig=(h == 3))
                        act(lambda: A.copy(out=qT[:, :, tc0:tc0 + 128], in_=pt[0:96, 0:512].rearrange("p (h t) -> p h t", t=128)), [pt], [qT])
                    nq = 512 if grp == 0 else 256
                    for qb in range(1024 // nq):
                        q0 = qb * nq
                        if grp == 0:
                            kcs = list(range(10))
                        else:
                            kcs = [qb * 2, qb * 2 + 1]
                        nqt = nq // 128
                        for h in range(4):
                            hh = hg * 4 + h
                            pa = nxt("pacc")
                            pscs = {}

                            def qk(ki_):
                                kc_ = kcs[ki_]
                                p_ = nxt("pp")
                                pe(lambda: T.matmul(p_[:, 0:nq], kT[:, h, kc_ * 128:(kc_ + 1) * 128], qT[:, h, q0:q0 + nq], start=True, stop=True), [kT, qT], [p_])
                                pscs[ki_] = p_

                            qk(0)
                            if len(kcs) > 1:
                                qk(1)
                            for ki, kc in enumerate(kcs):
                                if ki + 2 < len(kcs):
                                    qk(ki + 2)
                                psc = pscs.pop(ki)
                                pb = ptb.get()
                                act(lambda: A.activation(out=pb[:, 0:nq], in_=psc[:, 0:nq], func=AF.Exp, scale=96.0 ** -0.5), [psc], [pb])
                                for qt in range(nqt):
                                    pe(lambda: T.matmul(pa[:, qt * 65:(qt + 1) * 65], pb[:, qt * 128:(qt + 1) * 128], vext[:, kc, hh, :],
                                                        start=(ki == 0 and qt == 0), stop=(ki == len(kcs) - 1), skip_group_check=True),
                                       [pb, vext], [pa], sig=(ki == len(kcs) - 1 and qt == nqt - 1))
                            s1 = stt.get()
                            dve(lambda: V.reciprocal(out=s1[:, 0:nqt], in_=pa[:, 0:nqt * 65].rearrange("p (q c) -> p q c", c=65)[:, :, 64]), [pa], [s1])
                            for qt in range(nqt):
                                tl = (q0 // 128) + qt
                                dve(lambda: V.scalar_tensor_tensor(out=oa[:, tl, hh * 64:(hh + 1) * 64], in0=pa[:, qt * 65:qt * 65 + 64], scalar=s1[:, qt:qt + 1],
                                                                   in1=ga2[:, tl, hh * 64:(hh + 1) * 64], op0=ALU.mult, op1=ALU.mult), [pa, s1, ga2], [oa])
                for tl in range(8):
                    pt = nxt("ptr")
                    for j in range(4):
                        pe(lambda: T.transpose(pt[:, j * 128:(j + 1) * 128], oa[:, tl, j * 128:(j + 1) * 128], ident_b[:]), [oa, ident_b], [pt], sig=(j == 3))
                    act(lambda: A.copy(out=mixT[:, 0:4, tl * 128:(tl + 1) * 128], in_=pt[:, 0:512].rearrange("p (j t) -> p j t", t=128)), [pt], [mixT])

                alias([zT], [qT, kT, vext])
                for s_ in range(nseq):
                    pool(lambda: G.memset(zT[:, :, s_ * lp:s_ * lp + 1], 0.0), [], [zT])
                    pool(lambda: G.memset(zT[:, :, s_ * lp + lp - 1:s_ * lp + lp], 0.0), [], [zT])
                seg = min(512, tseq)
                for j in range(12):
                    for blk in range(2):
                        pz = nxt("pp")
                        for kk in range(8):
                            pe(lambda: T.matmul(pz[:], wfeat[:, kk, j * 128:(j + 1) * 128], hT[:, kk, blk * 512:(blk + 1) * 512],
                                                start=(kk == 0), stop=(kk == 7)), [wfeat, hT], [pz], sig=(kk == 7))
                        for s0 in range(0, 512, seg):
                            t0 = blk * 512 + s0
                            if (j + blk) % 2 == 0:
                                act(lambda: A.copy(out=zT[:, j, zpos(t0):zpos(t0) + seg], in_=pz[:, s0:s0 + seg]), [pz], [zT])
                            else:
                                dve(lambda: V.tensor_copy(out=zT[:, j, zpos(t0):zpos(t0) + seg], in_=pz[:, s0:s0 + seg]), [pz], [zT])
                alias([cT], [hT, qlnT, ckvT, kpe])
                def mk_diag(j_):
                    d_ = dg.get()
                    for kk in range(3):
                        dve(lambda: V.tensor_scalar(out=d_[:, kk, :], in0=ident_b[:], scalar1=convw[:, l, kk * 12 + j_:kk * 12 + j_ + 1], scalar2=None, op0=ALU.mult), [ident_b, convw], [d_])
                    return d_

                d3_next = mk_diag(0)
                for j in range(12):
                    d3 = d3_next
                    if j + 1 < 12:
                        d3_next = mk_diag(j + 1)
                    for blk in range(2):
                        pz = nxt("pp")
                        for s0 in range(0, 512, seg):
                            t0 = blk * 512 + s0
                            p0 = zpos(t0) - 1
                            for kk in range(3):
                                pe(lambda: T.matmul(pz[:, s0:s0 + seg], d3[:, kk, :], zT[:, j, p0 + kk:p0 + kk + seg], start=(kk == 0), stop=(kk == 2)),
                                   [d3, zT], [pz], sig=(kk == 2 and s0 + seg == 512))
                        t1 = th.get()
                        act(lambda: A.activation(out=t1[:], in_=pz[:], func=AF.Tanh, scale=0.5), [pz], [t1])
                        dve(lambda: V.scalar_tensor_tensor(out=cT[:, j, blk * 512:(blk + 1) * 512], in0=t1[:], scalar=1.0, in1=pz[:], op0=ALU.add, op1=ALU.mult), [t1, pz], [cT])
                def l2_front(j_, blk_):
                    sl_ = slice(blk_ * 512, (blk_ + 1) * 512)
                    sq = tb.get()
                    act(lambda: A.activation(out=sq[:], in_=cT[:, j_, sl_], func=AF.Square), [cT], [sq])
                    ps_ = nxt("pp")
                    pe(lambda: T.matmul(ps_[:], ones_b[:], sq[:], start=True, stop=True), [ones_b, sq], [ps_])
                    return ps_

                items = [(j_, b_) for j_ in range(8) for b_ in range(2)]
                ps_next = l2_front(*items[0])
                for ii, (j, blk) in enumerate(items):
                    if True:
                        sl = slice(blk * 512, (blk + 1) * 512)
                        ps = ps_next
                        if ii + 1 < len(items):
                            ps_next = l2_front(*items[ii + 1])
                        t1 = th.get()
                        mul = 128.0 if j < 4 else 1.0
                        act(lambda: A.activation(out=t1[:], in_=ps[:], func=AF.Ln, scale=mul, bias=4.0 * EPS * mul), [ps], [t1])
                        act(lambda: A.activation(out=t1[:], in_=t1[:], func=AF.Exp, scale=-0.5), [t1], [t1])
                        dve(lambda: V.tensor_tensor(out=cT[:, j, sl], in0=cT[:, j, sl], in1=t1[:], op=ALU.mult), [cT, t1], [cT])
                alias([oacc, ktok, vtok], [zT])
                alias(Z_gdn, Z_att)
                alias(W_gdn, W_small)
                for tl in range(8):
                    pt = nxt("ptr")
                    for j in range(8):
                        pe(lambda: T.transpose(pt[:, j * 128:(j + 1) * 128], cT[:, 4 + j, tl * 128:(tl + 1) * 128], ident_b[:]), [cT, ident_b], [pt], sig=(j == 7))
                    act(lambda: A.copy(out=ktok[:, tl, :], in_=pt[:, 0:512]), [pt], [ktok])
                    dve(lambda: V.tensor_copy(out=vtok[:, tl, :], in_=pt[:, 512:1024]), [pt], [vtok])

                alias(Y_gdn, [cT])
                pool(lambda: G.memset(oacc[:], 0.0), [], [oacc])
                for s_ in range(nseq):
                    for d in range(2):
                        if grp == 0:
                            k.dma(Sf[d][:], sgdn_d[l, d * 4:(d + 1) * 4].rearrange("a p v -> p a v"), wr=[Sf[d]])
                            pool(lambda: G.tensor_copy(out=Sb[d][:], in_=Sf[d][:]), [Sf[d]], [Sb[d]])
                        else:
                            pool(lambda: G.memset(Sf[d][:], 0.0), [], [Sf[d]])
                            pool(lambda: G.memset(Sb[d][:], 0.0), [], [Sb[d]])
                    for step in range(tps):
                        prefetch = (grp == 1 and not last)
                        if prefetch and s_ == 0 and step == 0:
                            lc_fixed[0] = True
                            wt, mt = weight_tasks(l + 1), mod_tasks(l + 1)
                            bg.append(mt.pop(0))
                            while wt or mt:
                                if wt:
                                    bg.append(wt.pop(0))
                                if mt:
                                    bg.append(mt.pop(0))
                        gens = [gdn_unit(l, s_ * tps + step, 0), gdn_unit(l, s_ * tps + tps - 1 - step, 1)]
                        rounds = 0
                        while gens:
                            for g_ in list(gens):
                                try:
                                    next(g_)
                                except StopIteration:
                                    gens.remove(g_)
                            rounds += 1
                            if prefetch and rounds % 5 == 0:
                                bg_step(2)
                    if grp == 1:
                        for d in range(2):
                            k.dma(nst_d[s_, l, d * 4:(d + 1) * 4].rearrange("a p v -> p a v"), Sf[d][:], rd=[Sf[d]])
                alias(W_small, W_gdn)
                for tl in range(8):
                    junk = th.get()
                    act(lambda: A.activation(out=junk[:], in_=oacc[:, tl, :], func=AF.Square), [oacc], [junk])
                    dve(lambda: V.tensor_reduce(out=rsb[:, tl * 4:(tl + 1) * 4], in_=junk[:].rearrange("p (h f) -> p h f", f=128),
                                                axis=mybir.AxisListType.X, op=ALU.add), [junk], [rsb])
                act(lambda: A.activation(out=rsb[:, 32:64], in_=rsb[:, 0:32], func=AF.Ln, scale=1.0 / 128, bias=EPS), [rsb], [rsb])
                act(lambda: A.activation(out=rsb[:, 32:64], in_=rsb[:, 32:64], func=AF.Exp, scale=-0.5), [rsb], [rsb])
                dve(lambda: V.tensor_scalar(out=rsb[:, 32:64], in0=rsb[:, 32:64], scalar1=0.5, scalar2=None, op0=ALU.mult), [rsb], [rsb])
                for tl in range(8):
                    ob = tb.get()
                    for h in range(4):
                        t1 = th.get()
                        dve(lambda: V.scalar_tensor_tensor(out=t1[:, 0:128], in0=oacc[:, tl, h * 128:(h + 1) * 128], scalar=rsb[:, 32 + tl * 4 + h:33 + tl * 4 + h], in1=onb[:],
                                                           op0=ALU.mult, op1=ALU.mult), [oacc, rsb, onb], [t1])
                        dve(lambda: V.tensor_tensor(out=ob[:, h * 128:(h + 1) * 128], in0=t1[:, 0:128], in1=gb2[:, tl, h * 128:(h + 1) * 128], op=ALU.mult), [t1, gb2], [ob])
                    pt = nxt("ptr")
                    for j in range(4):
                        pe(lambda: T.transpose(pt[:, j * 128:(j + 1) * 128], ob[:, j * 128:(j + 1) * 128], ident_b[:]), [ob, ident_b], [pt], sig=(j == 3))
                    act(lambda: A.copy(out=mixT[:, 4:8, tl * 128:(tl + 1) * 128], in_=pt[:, 0:512].rearrange("p (j t) -> p j t", t=128)), [pt], [mixT])

                alias(xm, [oacc, ktok, vtok])

                def wload(m_):
                    s_ = stg[stg_i[0] % 2]
                    stg_i[0] += 1
                    sv_ = s_[:, 0:1024].rearrange("p (k c) -> p k c", c=128)
                    k.dma(sv_, wout_d[l, m_], wr=[s_])
                    pool(lambda: G.tensor_copy(out=woutm[m_ % 2][:], in_=sv_), [s_], [woutm[m_ % 2]])

                def xload(m_):
                    k.dma(xm[m_][:], xin[:, m_, t0g:t0g + 1024], rd=[xsrc], wr=[xm[m_]])

                wload(0)
                wload(1)
                xload(0)
                xload(1)
                for m in range(8):
                    wm = woutm[m % 2]
                    for blk in range(2):
                        bs_ = slice(blk * 512, (blk + 1) * 512)
                        po = nxt("pp")
                        for kk in range(8):
                            pe(lambda: T.matmul(po[:], wm[:, kk, :], mixT[:, kk, bs_], start=(kk == 0), stop=(kk == 7)), [wm, mixT], [po], sig=(kk == 7))
                        dve(lambda: V.scalar_tensor_tensor(out=xm[m][:, bs_], in0=po[:], scalar=modT[:, 16 + m, ci:ci + 1], in1=xm[m][:, bs_],
                                                           op0=ALU.mult, op1=ALU.add), [po, modT, xm[m]], [xm[m]])
                    if m + 2 < 8:
                        wload(m + 2)
                        xload(m + 2)
                    if not last:
                        k.dma(xs_d[:, m, t0g:t0g + 1024], xm[m][:], rd=[xm[m]], wr=[xsrc])
                if last:
                    for blk in range(2):
                        bs_ = slice(blk * 512, (blk + 1) * 512)
                        rs = fm_norm_stats(lambda kk_: xm[kk_], lambda kk_: xm[kk_][:, bs_], 1024, EPS)
                        for kk in range(8):
                            dve(lambda: V.scalar_tensor_tensor(out=xm[kk][:, bs_], in0=xm[kk][:, bs_], scalar=fnorm[:, kk:kk + 1], in1=rs[:],
                                                               op0=ALU.mult, op1=ALU.mult), [xm[kk], fnorm, rs], [xm[kk]])
                    for kk in range(8):
                        k.dma(yT_d[:, kk, t0g:t0g + 1024], xm[kk][:], rd=[xm[kk]])

        def gdn_unit(l, tl, d):
            B = gsets[d]
            pbanks = ([pp[0], pp[1], pacc[0]], [pp[2], pp[3], pacc[1]])[d]
            pctr = [0]

            def nxt(kind):
                if kind == "ptr":
                    return ptr[d]
                b_ = pbanks[pctr[0] % 3]
                pctr[0] += 1
                return b_

            tcs = slice(tl * 128, (tl + 1) * 128)
            last = 127 if d == 0 else 0
            sv = gst.get()
            g4 = gall[:, tl, d * 4:(d + 1) * 4]
            RM = {0: 0, 1: 1, 2: 2, 5: 3, 6: 4, 7: 5}

            def SA(r_):
                return svall[:, RM[r_], tl, d * 4:(d + 1) * 4]

            ug = B["gUg"].get()
            dve(lambda: V.tensor_tensor(out=ug[:], in0=umask[:, d, :].unsqueeze(1).to_broadcast([128, 4, 128]),
                                        in1=g4.unsqueeze(2).to_broadcast([128, 4, 128]), op=ALU.mult), [umask, gall], [ug])
            ugf = ug[:].rearrange("p h f -> p (h f)")
            yield
            pb1, pb2 = nxt("pp"), nxt("pp")
            pe(lambda: T.matmul(pb1[:], ones_f[:], ugf, start=True, stop=False, skip_group_check=True), [ones_f, ug], [pb1], sig=False)
            for h in range(4):
                pe(lambda: T.matmul(pb1[:, h * 128:(h + 1) * 128], ident_f[:], masks[:, 2 * d, :], start=False, stop=(h == 3), skip_group_check=True), [ident_f, masks], [pb1], sig=(h == 3))
            pe(lambda: T.matmul(pb2[:], ones_f[:], ugf, start=True, stop=False, skip_group_check=True), [ones_f, ug], [pb2], sig=False)
            for h in range(4):
                pe(lambda: T.matmul(pb2[:, h * 128:(h + 1) * 128], ident_f[:], masks[:, 2 * d + 1, :], start=False, stop=(h == 3), skip_group_check=True), [ident_f, masks], [pb2], sig=(h == 3))
            E, Et = B["gE"].get(), B["gEt"].get()
            for h in range(4):
                act(lambda: A.activation(out=E[:, h, :], in_=pb1[:, h * 128:(h + 1) * 128], func=AF.Exp, scale=-1.0, bias=SA(0)[:, h:h + 1]), [pb1, svall], [E])
                act(lambda: A.activation(out=Et[:, h, :], in_=pb2[:, h * 128:(h + 1) * 128], func=AF.Exp, scale=1.0, bias=SA(1)[:, h:h + 1]), [pb2, svall], [Et])
            act(lambda: A.activation(out=sv[:, 3, :], in_=pb2[:].rearrange("p (h f) -> p h f", f=128)[:, :, last], func=AF.Exp), [pb2], [sv])
            dve(lambda: V.tensor_copy(out=sv[:, 4, :], in_=Et[:, :, last]), [Et], [sv])
            yield
            pkk, pqk = nxt("pp"), nxt("pp")
            for h in range(4):
                pe(lambda: T.matmul(pkk[:, h * 128:(h + 1) * 128], cT[:, 4 + h, tcs], cT[:, 4 + h, tcs], start=True, stop=True), [cT], [pkk], sig=(h == 3))
            for h in range(4):
                pe(lambda: T.matmul(pqk[:, h * 128:(h + 1) * 128], cT[:, 4 + h, tcs], cT[:, h, tcs], start=True, stop=True), [cT], [pqk], sig=(h == 3))
            P = B["gP"].get()
            for h in range(4):
                dve(lambda: V.scalar_tensor_tensor(out=P[:, h, :], in0=pkk[:, h * 128:(h + 1) * 128], scalar=SA(5)[:, h:h + 1], in1=E[:, h, :],
                                                   op0=ALU.mult, op1=ALU.mult), [pkk, svall, E], [P])
            intra = B["gin"].get()
            dve(lambda: V.tensor_tensor(out=intra[:].rearrange("p h f -> p (h f)"), in0=pqk[:], in1=Et[:].rearrange("p h f -> p (h f)"), op=ALU.mult), [pqk, Et], [intra])
            yield
            pt = nxt("ptr")
            for h in range(4):
                pe(lambda: T.transpose(pt[:, h * 128:(h + 1) * 128], P[:, h, :], ident_b[:]), [P, ident_b], [pt], sig=(h == 3))
            Pt = B["gPt"].get()
            act(lambda: A.copy(out=Pt[:].rearrange("p h f -> p (h f)"), in_=pt[:, 0:512]), [pt], [Pt])
            idb4 = ident_b[:].unsqueeze(1).to_broadcast([128, 4, 128])

            def mA(lev):
                return lmask[:, 2 * lev + (0 if d == 0 else 1), :].unsqueeze(1).to_broadcast([128, 4, 128])

            def mB(lev):
                return lmask[:, 2 * lev + (1 if d == 0 else 0), :].unsqueeze(1).to_broadcast([128, 4, 128])

            yield
            Tc, Ttc = B["gT"].get(), B["gTt"].get()
            xs, xst = B["gXs"].get(), B["gXst"].get()
            dve(lambda: V.tensor_tensor(out=xs[:], in0=P[:], in1=mA(0), op=ALU.mult), [P, lmask], [xs])
            dve(lambda: V.tensor_tensor(out=Tc[:], in0=xs[:], in1=idb4, op=ALU.add), [xs, ident_b], [Tc])
            dve(lambda: V.tensor_tensor(out=xst[:], in0=Pt[:], in1=mB(0), op=ALU.mult), [Pt, lmask], [xst])
            dve(lambda: V.tensor_tensor(out=Ttc[:], in0=xst[:], in1=idb4, op=ALU.add), [xst, ident_b], [Ttc])
            for lev in range(1, 7):
                yield
                py = nxt("pp")
                for h in range(4):
                    pe(lambda: T.matmul(py[:, h * 128:(h + 1) * 128], Pt[:, h, :], Tc[:, h, :], start=True, stop=True), [Pt, Tc], [py], sig=(h == 3))
                Y = B["gY"].get()
                dve(lambda: V.tensor_tensor(out=Y[:], in0=py[:].rearrange("p (h f) -> p h f", f=128), in1=mA(lev), op=ALU.mult), [py, lmask], [Y])
                yield
                if lev < 6:
                    pm_ = nxt("pp")
                    for h in range(4):
                        pe(lambda: T.matmul(pm_[:, h * 128:(h + 1) * 128], Ttc[:, h, :], Y[:, h, :], start=(h == 0), stop=False, skip_group_check=True), [Ttc, Y], [pm_], sig=False)
                        pe(lambda: T.matmul(pm_[:, h * 128:(h + 1) * 128], ident_b[:], Tc[:, h, :], start=False, stop=(h == 3), skip_group_check=True), [ident_b, Tc], [pm_], sig=(h == 3))
                pmt = nxt("pp")
                for h in range(4):
                    pe(lambda: T.matmul(pmt[:, h * 128:(h + 1) * 128], Y[:, h, :], Ttc[:, h, :], start=(h == 0), stop=False, skip_group_check=True), [Ttc, Y], [pmt], sig=False)
                    pe(lambda: T.matmul(pmt[:, h * 128:(h + 1) * 128], ident_b[:], Ttc[:, h, :], start=False, stop=(h == 3), skip_group_check=True), [ident_b, Ttc], [pmt], sig=(h == 3))
                if lev < 6:
                    Tn = B["gT"].get()
                    act(lambda: A.copy(out=Tn[:].rearrange("p h f -> p (h f)"), in_=pm_[:]), [pm_], [Tn])
                    Tc = Tn
                Ttn = B["gTt"].get()
                if lev % 2 == 0:
                    dve(lambda: V.tensor_copy(out=Ttn[:].rearrange("p h f -> p (h f)"), in_=pmt[:]), [pmt], [Ttn])
                else:
                    act(lambda: A.copy(out=Ttn[:].rearrange("p h f -> p (h f)"), in_=pmt[:]), [pmt], [Ttn])
                Ttc = Ttn
            Tt = Ttc
            yield
            vb, kb, kd = B["gvb"].get(), B["gkb"].get(), B["gkd"].get()
            for h in range(4):
                hs = slice(h * 128, (h + 1) * 128)
                act(lambda: A.activation(out=vb[:, h, :], in_=vtok[:, tl, hs], func=AF.Copy, scale=SA(6)[:, h:h + 1]), [vtok, svall], [vb])
                dve(lambda: V.tensor_scalar(out=kb[:, h, :], in0=ktok[:, tl, hs], scalar1=SA(7)[:, h:h + 1], scalar2=None, op0=ALU.mult), [ktok, svall], [kb])
                act(lambda: A.activation(out=kd[:, h, :], in_=ktok[:, tl, hs], func=AF.Copy, scale=sv[:, 4, h:h + 1]), [ktok, sv], [kd])
            pu, pw = nxt("pp"), nxt("pp")
            for h in range(4):
                pe(lambda: T.matmul(pu[:, h * 128:(h + 1) * 128], Tt[:, h, :], vb[:, h, :], start=True, stop=True), [Tt, vb], [pu], sig=(h == 3))
            for h in range(4):
                pe(lambda: T.matmul(pw[:, h * 128:(h + 1) * 128], kb[:, h, :], Tt[:, h, :], start=True, stop=True), [Tt, kb], [pw], sig=(h == 3))
            u, wT = B["gu"].get(), B["gwT"].get()
            act(lambda: A.copy(out=u[:].rearrange("p h f -> p (h f)"), in_=pu[:]), [pu], [u])
            dve(lambda: V.tensor_copy(out=wT[:].rearrange("p h f -> p (h f)"), in_=pw[:]), [pw], [wT])
            yield
            pws = nxt("pp")
            for h in range(4):
                pe(lambda: T.matmul(pws[:, h * 128:(h + 1) * 128], wT[:, h, :], Sb[d][:, h, :], start=True, stop=True), [wT, Sb[d]], [pws], sig=(h == 3))
            vn = B["gvn"].get()
            dve(lambda: V.tensor_tensor(out=vn[:].rearrange("p h f -> p (h f)"), in0=u[:].rearrange("p h f -> p (h f)"), in1=pws[:], op=ALU.subtract), [u, pws], [vn])
            yield
            pqs, piv, pds = nxt("pp"), nxt("pp"), nxt("pp")
            for h in range(4):
                pe(lambda: T.matmul(pqs[:, h * 128:(h + 1) * 128], cT[:, h, tcs], Sb[d][:, h, :], start=True, stop=True), [cT, Sb[d]], [pqs], sig=(h == 3))
            for h in range(4):
                pe(lambda: T.matmul(piv[:, h * 128:(h + 1) * 128], intra[:, h, :], vn[:, h, :], start=True, stop=True), [intra, vn], [piv], sig=(h == 3))
            for h in range(4):
                pe(lambda: T.matmul(pds[:, h * 128:(h + 1) * 128], kd[:, h, :], vn[:, h, :], start=True, stop=True), [kd, vn], [pds], sig=(h == 3))
            for h in range(4):
                hs = slice(h * 128, (h + 1) * 128)
                dve(lambda: V.scalar_tensor_tensor(out=oacc[:, tl, hs], in0=pqs[:, hs], scalar=SA(2)[:, h:h + 1], in1=oacc[:, tl, hs],
                                                   op0=ALU.mult, op1=ALU.add), [pqs, svall, oacc], [oacc])
            dve(lambda: V.tensor_tensor(out=oacc[:, tl, :], in0=piv[:], in1=oacc[:, tl, :], op=ALU.add), [piv, oacc], [oacc])
            for h in range(4):
                dve(lambda: V.scalar_tensor_tensor(out=Sf[d][:, h, :], in0=Sf[d][:, h, :], scalar=sv[:, 3, h:h + 1], in1=pds[:, h * 128:(h + 1) * 128],
                                                   op0=ALU.mult, op1=ALU.add), [Sf[d], sv, pds], [Sf[d]])
            act(lambda: A.copy(out=Sb[d][:], in_=Sf[d][:]), [Sf[d]], [Sb[d]])
            yield

        try:
            for l in range(n_layers):
                layer(l, l == n_layers - 1)
        except StopBuild:
            pass
        k.finish()
        print("instructions:", k.nops)
    return nc


def _host_inputs(inp, core):
    f = np.float32
    bs = core % 4
    xs = inp["x_sample"][bs]
    xp = inp["x_prompt"][4 * core:4 * core + 4].reshape(1024, 1024)
    xall = np.concatenate([xs, xp], 0)
    xT = np.ascontiguousarray(xall.T.reshape(8, 128, 2048).transpose(1, 0, 2))
    cond2 = np.stack([inp["c_ctx"], inp["c"][bs]], 1)
    cond = np.ascontiguousarray(cond2.reshape(8, 128, 2).transpose(1, 0, 2))

    def fm(v, nch):
        return np.ascontiguousarray(v.reshape(v.shape[0], nch, 128).transpose(2, 0, 1))

    def bc(v):
        return np.ascontiguousarray(np.broadcast_to(v[None], (128,) + v.shape))

    w_in = inp["w_in"]
    o = np.cumsum([0, 256, 128, 32, 512, 512, 512, 512, 512, 8, 8])
    sl = lambda i: slice(o[i], o[i + 1])
    wtok = np.concatenate([w_in[:, :, sl(0)], w_in[:, :, sl(1)], w_in[:, :, sl(2)], w_in[:, :, sl(8)], w_in[:, :, sl(9)],
                           w_in[:, :, sl(3)], w_in[:, :, sl(7)]], 2)
    wfeat = np.concatenate([w_in[:, :, sl(4)], w_in[:, :, sl(5)], w_in[:, :, sl(6)]], 2)
    wqb = inp["w_qb"].reshape(NL, 256, 8, 96)
    wqb2 = np.concatenate([wqb[..., :64].reshape(NL, 256, 512), wqb[..., 64:].reshape(NL, 256, 256)], 2)
    wkvb = inp["w_kvb"].reshape(NL, 128, 8, 128)
    wkn = wkvb[..., :64].reshape(NL, 128, 512)
    wv = wkvb[..., 64:].reshape(NL, 128, 512)
    convw = np.ascontiguousarray(inp["conv_w"].reshape(NL, 3, 12, 128).transpose(3, 0, 1, 2).reshape(128, NL, 36))
    ar = np.arange(128)
    ufwd = (ar[:, None] <= ar[None, :]).astype(f)
    ubwd = (ar[:, None] >= ar[None, :]).astype(f)
    umask = np.stack([ufwd, ubwd], 1)
    p_, f_ = ar[:, None], ar[None, :]
    m1f = BIG * (f_ >= p_)
    nm2f = -BIG * (f_ < p_)
    m1b = BIG * (f_ <= p_)
    nm2b = -BIG * (f_ > p_)
    masks = np.stack([m1f, nm2f, m1b, nm2b], 1).astype(f)
    lm = []
    for lev in range(7):
        s_ = 1 << lev
        ml = ((p_ // (2 * s_)) == (f_ // (2 * s_))) & ((p_ % (2 * s_)) >= s_) & ((f_ % (2 * s_)) < s_)
        lm.append(ml.astype(f))
        lm.append(ml.T.astype(f))
    lmask = np.stack(lm, 1)
    t = np.arange(1024)
    inv = (10000.0 ** (-np.arange(8, dtype=f) / 8)).astype(f)
    ang = np.stack([(t // 64).astype(f)[:, None] * inv, (t % 64).astype(f)[:, None] * inv], 1)
    cs = np.concatenate([np.cos(ang).reshape(1024, 16), np.sin(ang).reshape(1024, 16)], 1).astype(f)
    rope = np.ascontiguousarray(cs.reshape(8, 128, 32).transpose(1, 0, 2))
    d = {
        "xT": xT, "cond": cond, "normw": fm(inp["norm_w"], 8),
        "wada": inp["w_ada"].reshape(NL, 8, 128, 24, 128).transpose(0, 3, 2, 1, 4), "bada": fm(inp["b_ada"], 24),
        "wtok": wtok, "wfeat": wfeat, "wqb": wqb2, "wkn": wkn, "wv": wv, "wout": inp["w_out"].reshape(NL, 8, 128, 8, 128).transpose(0, 3, 2, 1, 4),
        "qan": bc(inp["q_a_norm"]), "kvn": bc(inp["kv_a_norm"]), "onb": bc(inp["o_norm"]), "convw": convw,
        "alog": bc(inp["a_log"].reshape(NL, 8)), "dtb": bc(inp["dt_bias"].reshape(NL, 8)),
        "fnorm": np.ascontiguousarray(inp["final_norm"].reshape(8, 128).T),
        "cckv": inp["cache_ckv"][bs], "ckpe": inp["cache_kpe"][bs], "sgdn": inp["state_gdn"][bs].reshape(NL, 8, 128, 128),
        "ident": np.eye(128, dtype=f), "umask": umask, "masks": masks, "rope": rope, "lmask": lmask,
    }
    return {k_: np.ascontiguousarray(v, dtype=f) for k_, v in d.items()}


_NC_CACHE = {}


def kernel(**inputs):
    inp = {k_: np.asarray(v, dtype=np.float32) for k_, v in inputs.items()}
    if "nc" not in _NC_CACHE:
        _NC_CACHE["nc"] = build()
    nc = _NC_CACHE["nc"]
    in_maps = [_host_inputs(inp, c) for c in range(8)]
    res = run_bass_kernel_spmd(nc, in_maps, core_ids=list(range(8)))
    R = res.results
    y_prompt = np.zeros((32, 256, 1024), np.float32)
    y_sample = np.zeros((4, 1024, 1024), np.float32)
    new_ckv = np.zeros((32, NL, 256, 128), np.float32)
    new_kpe = np.zeros((32, NL, 256, 32), np.float32)
    new_state = np.zeros((32, NL, 2, 4, 128, 128), np.float32)
    for c in range(8):
        yT = np.asarray(R[c]["yT"])
        y = yT.transpose(2, 1, 0).reshape(2048, 1024)
        if c < 4:
            y_sample[c] = y[:1024]
        y_prompt[4 * c:4 * c + 4] = y[1024:].reshape(4, 256, 1024)
        new_ckv[4 * c:4 * c + 4] = np.asarray(R[c]["nckv"]).reshape(NL, 4, 256, 128).transpose(1, 0, 2, 3)
        new_kpe[4 * c:4 * c + 4] = np.asarray(R[c]["nkpe"]).reshape(NL, 4, 256, 32).transpose(1, 0, 2, 3)
        new_state[4 * c:4 * c + 4] = np.asarray(R[c]["nst"]).reshape(4, NL, 2, 4, 128, 128)
    return (y_prompt, y_sample, new_ckv, new_kpe, new_state)
```

```python
import numpy as np
from contextlib import ExitStack
import concourse.bass as bass
import concourse.mybir as mybir
from concourse.bass_utils import run_bass_kernel_spmd

F32 = mybir.dt.float32
BF16 = mybir.dt.bfloat16
ALU = mybir.AluOpType
AF = mybir.ActivationFunctionType

NL = 4
EPS = 1e-6
BIG = 30000.0


class Buf:
    __slots__ = ("name", "t", "w", "r", "ps")

    def __init__(self, name, t=None):
        self.name = name
        self.t = t
        self.w = None
        self.r = []
        self.ps = False

    def __getitem__(self, idx):
        return self.t[idx]


def alias(new_bufs, old_bufs):
    evs = []
    for o in old_bufs:
        if o.w is not None:
            evs.append(o.w)
        evs.extend(o.r)
    for n in new_bufs:
        n.w = None
        n.r = list(evs)

    def __getitem__(self, idx):
        return self.t[idx]


class KB:
    def __init__(self, nc, stack, n_dma_sems=8):
        self.nc = nc
        self.stack = stack
        self.eng = {"pe": nc.tensor, "act": nc.scalar, "dve": nc.vector, "pool": nc.gpsimd, "sp": nc.sync}
        self.sem = {}
        self.cnt = {}
        for e in ["pe", "act", "dve", "pool"]:
            self.sem[e] = stack.enter_context(nc.semaphore("s_" + e))
            self.cnt[e] = 0
        self.dsem = []
        for i in range(n_dma_sems):
            k = "d%d" % i
            self.sem[k] = stack.enter_context(nc.semaphore("s_" + k))
            self.cnt[k] = 0
            self.dsem.append(k)
        self.dnext = 0
        self.waited = {}
        self.nops = 0

    def sb(self, name, shape, dt):
        t = self.stack.enter_context(self.nc.sbuf_tensor("sb_" + name, list(shape), dt))
        return Buf(name, t)

    def ps(self, name, shape, dt=F32):
        t = self.stack.enter_context(self.nc.psum_tensor("ps_" + name, list(shape), dt))
        b = Buf(name, t)
        b.ps = True
        return b

    def _need(self, e, ev):
        if ev is None:
            return
        k, v = ev
        if k == e == "pe":
            return
        if v > self.cnt[k]:
            raise RuntimeError("wait on unsignaled op: %s needs %s=%d (have %d)" % (e, k, v, self.cnt[k]))
        if self.waited.get((e, k), 0) >= v:
            return
        self.waited[(e, k)] = v
        self.eng[e].wait_ge(self.sem[k], v)

    def _deps(self, e, rd, wr):
        for b in rd:
            self._need(e, b.w)
            if b.ps:
                for ev in b.r:
                    if ev[0] != e:
                        self._need(e, ev)
        for b in wr:
            self._need(e, b.w)
            for ev in b.r:
                self._need(e, ev)

    def op(self, e, fn, rd=(), wr=(), sig=True):
        self._deps(e, rd, wr)
        ins = fn()
        self.nops += 1
        if sig:
            self.cnt[e] += 1
            ins.then_inc(self.sem[e], 1)
            ev = (e, self.cnt[e])
        else:
            ev = (e, self.cnt[e] + 1)
        for b in wr:
            b.w = ev
            b.r = []
        for b in rd:
            b.r = [x for x in b.r if x[0] != e] + [ev]
        return ins

    def dma(self, out_ap, in_ap, rd=(), wr=(), q="sp"):
        k = self.dsem[self.dnext]
        self.dnext = (self.dnext + 1) % len(self.dsem)
        if self.cnt[k] > 0:
            self._need(q, (k, self.cnt[k]))
        self._deps(q, rd, wr)
        ins = self.eng[q].dma_start(out=out_ap, in_=in_ap)
        self.cnt[k] += 16
        ins.then_inc(self.sem[k], 16)
        ev = (k, self.cnt[k])
        for b in wr:
            b.w = ev
            b.r = []
        for b in rd:
            b.r = b.r + [ev]
        self.nops += 1
        return ins

    def finish(self):
        for k in self.dsem:
            if self.cnt[k] > 0:
                self._need("sp", (k, self.cnt[k]))


def build(n_layers=NL, dbg=False):
    nc = bass.Bass("TRN2", target_bir_lowering=False)

    def din(name, shape, dt=F32):
        return nc.dram_tensor(name, list(shape), dt, kind="ExternalInput").ap()

    def dout(name, shape, dt=F32):
        return nc.dram_tensor(name, list(shape), dt, kind="ExternalOutput").ap()

    xT_d = din("xT", [128, 8, 2048])
    cond_d = din("cond", [128, 8, 2])
    normw_d = din("normw", [128, NL, 8])
    wada_d = din("wada", [NL, 24, 128, 8, 128])
    bada_d = din("bada", [128, NL, 24])
    wtok_d = din("wtok", [NL, 1024, 1456])
    wfeat_d = din("wfeat", [NL, 1024, 1536])
    wqb_d = din("wqb", [NL, 256, 768])
    wkn_d = din("wkn", [NL, 128, 512])
    wv_d = din("wv", [NL, 128, 512])
    wout_d = din("wout", [NL, 8, 128, 8, 128])
    qan_d = din("qan", [128, NL, 256])
    kvn_d = din("kvn", [128, NL, 128])
    onb_d = din("onb", [128, NL, 128])
    convw_d = din("convw", [128, NL, 36])
    alog_d = din("alog", [128, NL, 8])
    dtb_d = din("dtb", [128, NL, 8])
    fnorm_d = din("fnorm", [128, 8])
    cckv_d = din("cckv", [NL, 256, 128])
    ckpe_d = din("ckpe", [NL, 256, 32])
    sgdn_d = din("sgdn", [NL, 8, 128, 128])
    ident_d = din("ident", [128, 128])
    umask_d = din("umask", [128, 2, 128])
    mask_d = din("masks", [128, 4, 128])
    rope_d = din("rope", [128, 8, 32])
    lmask_d = din("lmask", [128, 14, 128])

    yT_d = dout("yT", [128, 8, 2048])
    nckv_d = dout("nckv", [NL, 1024, 128])
    nkpe_d = dout("nkpe", [NL, 1024, 32])
    nst_d = dout("nst", [4, NL, 8, 128, 128])
    xs_d = nc.dram_tensor("xscr", [128, 8, 2048], F32).ap()
    dbg_d = dout("dbg", [128, 2048]) if dbg else None

    class StopBuild(Exception):
        pass

    with ExitStack() as st:
        k = KB(nc, st)
        V, A, G, T = nc.vector, nc.scalar, nc.gpsimd, nc.tensor

        def dve(fn, rd, wr):
            return k.op("dve", fn, rd, wr)

        def act(fn, rd, wr):
            return k.op("act", fn, rd, wr)

        def pool(fn, rd, wr):
            return k.op("pool", fn, rd, wr)

        def pe(fn, rd, wr, sig=True):
            return k.op("pe", fn, rd, wr, sig)

        ident_f = k.sb("ident_f", [128, 128], F32)
        ident_b = k.sb("ident_b", [128, 128], BF16)
        ones_f = k.sb("ones_f", [128, 128], F32)
        ones_b = k.sb("ones_b", [128, 128], BF16)
        mhalf = k.sb("mhalf", [128, 16], F32)
        umask = k.sb("umask", [128, 2, 128], F32)
        masks = k.sb("masks", [128, 4, 128], F32)
        rope = k.sb("rope", [128, 8, 32], F32)
        lmask = k.sb("lmask", [128, 14, 128], BF16)
        cond = k.sb("cond", [128, 8, 2], F32)
        scT = k.sb("scT", [128, 8, 2], F32)
        normw = k.sb("normw", [128, NL, 8], F32)
        bada = k.sb("bada", [128, NL, 24], F32)
        qan = k.sb("qan", [128, 256], F32)
        kvn = k.sb("kvn", [128, 128], F32)
        onb = k.sb("onb", [128, 128], F32)
        convw = k.sb("convw", [128, NL, 36], F32)
        nA = k.sb("nA", [128, NL, 8], F32)
        dtb = k.sb("dtb", [128, NL, 8], F32)
        fnorm = k.sb("fnorm", [128, 8], F32)
        modTs = [k.sb("modT%d" % i, [128, 24, 2], F32) for i in range(2)]
        amods = [k.sb("amod%d" % i, [128, 8, 2], F32) for i in range(2)]

        stg = [k.sb("stg%d" % i, [128, 1024], F32) for i in range(2)]
        wtok = k.sb("wtok", [128, 8, 1456], BF16)
        wfeat = k.sb("wfeat", [128, 8, 1536], BF16)
        wqb = k.sb("wqb", [128, 2, 768], BF16)
        wkn = k.sb("wkn", [128, 512], BF16)
        wv = k.sb("wv", [128, 512], BF16)
        woutm = [k.sb("woutm%d" % i, [128, 8, 128], BF16) for i in range(2)]

        mixT = k.sb("mixT", [128, 8, 1024], BF16)
        gb2 = k.sb("gb2", [128, 8, 512], BF16)
        xab = k.sb("xab", [128, 8, 16], F32)
        gall = k.sb("gall", [128, 8, 8], F32)
        svall = k.sb("svall", [128, 6, 8, 8], F32)
        ball = k.sb("ball", [128, 8, 8], F32)

        arX = k.sb("arX", [128, 8192], F32)
        arY = k.sb("arY", [128, 6144], F32)
        arZ = k.sb("arZ", [128, 7424], F32)

        def view(ar, name, off, nwords, dt, shape, p1=128):
            ap = ar.t[0:p1, off:off + nwords]
            if dt == BF16:
                ap = ap.bitcast(BF16)
            if len(shape) == 3:
                ap = ap.rearrange("p (a b) -> p a b", b=shape[2])
            elif len(shape) == 4:
                ap = ap.rearrange("p (a b c) -> p a b c", b=shape[2], c=shape[3])
            assert tuple(ap.shape) == tuple(shape), (name, ap.shape, shape)
            return Buf(name, ap)

        xblk = view(arX, "xblk", 0, 4096, F32, [128, 8, 512])
        xblk2 = view(arX, "xblk2", 4096, 4096, F32, [128, 8, 512])
        xall = view(arX, "xall", 0, 8192, F32, [128, 8, 1024])
        xm = [Buf("xall%d" % m_, xall.t[:, m_, :]) for m_ in range(8)]
        qT = view(arX, "qT", 0, 2048, BF16, [96, 4, 1024], p1=96)
        kT = view(arX, "kT", 2048, 2560, BF16, [96, 4, 1280], p1=96)
        vext = view(arX, "vext", 4608, 2600, BF16, [128, 10, 8, 65])
        zT = view(arX, "zT", 0, 6192, BF16, [128, 12, 1032])
        oacc = view(arX, "oacc", 0, 4096, F32, [128, 8, 512])
        ktok = view(arX, "ktok", 4096, 2048, BF16, [128, 8, 512])
        vtok = view(arX, "vtok", 6144, 2048, BF16, [128, 8, 512])
        hT = view(arY, "hT", 0, 4096, BF16, [128, 8, 1024])
        qlnT = view(arY, "qlnT", 4096, 1024, BF16, [128, 2, 1024])
        ckvT = view(arY, "ckvT", 5120, 640, BF16, [128, 1280])
        kpe = view(arY, "kpe", 5760, 320, F32, [128, 10, 32])
        cT = view(arY, "cT", 0, 6144, BF16, [128, 12, 1024])

        class Rot:
            def __init__(s, bufs):
                s.b = bufs
                s.i = 0

            def get(s):
                b = s.b[s.i]
                s.i = (s.i + 1) % len(s.b)
                return b

        def rot(name, shape, dt, n):
            return Rot([k.sb("%s%d" % (name, i), shape, dt) for i in range(n)])

        zo = [0]

        def zview(name, nwords, dt, shape):
            b = view(arZ, name, zo[0], nwords, dt, shape)
            zo[0] += nwords
            return b

        ga2 = zview("ga2", 2048, BF16, [128, 8, 512])
        oa = zview("oa", 2048, BF16, [128, 8, 512])
        ptb = Rot([zview("ptb%d" % i, 256, BF16, [128, 512]) for i in range(3)])
        qsb = Rot([zview("qsb%d" % i, 384, BF16, [128, 8, 96]) for i in range(2)])
        ksb = Rot([zview("ksb%d" % i, 384, BF16, [128, 8, 96]) for i in range(2)])
        Z_att = [ga2, oa] + ptb.b + qsb.b + ksb.b
        assert zo[0] <= 7424
        arW = k.sb("arW", [128, 4736], F32)
        wo = [0]

        def wview(name, nwords, dt, shape):
            b = view(arW, name, wo[0], nwords, dt, shape)
            wo[0] += nwords
            return b

        zA = Rot([wview("zA%d" % i, 432, F32, [128, 432]) for i in range(2)])
        th = Rot([wview("th%d" % i, 512, F32, [128, 512]) for i in range(3)])
        tb = Rot([wview("tb%d" % i, 256, BF16, [128, 512]) for i in range(2)])
        ckvs = Rot([wview("ckvs%d" % i, 128, F32, [128, 128]) for i in range(2)])
        ckvb = Rot([wview("ckvb%d" % i, 192, BF16, [128, 384]) for i in range(2)])
        dg = Rot([wview("dg%d" % i, 192, BF16, [128, 3, 128]) for i in range(2)])
        ctx_f = wview("ctx_f", 256, F32, [128, 2, 128])
        rsb = wview("rsb", 512, F32, [128, 512])
        W_small = zA.b + th.b + tb.b + ckvs.b + ckvb.b + dg.b + [ctx_f, rsb]
        assert wo[0] <= 4736, wo[0]

        def gdn_set(mk_f32, mk_bf16, tag):
            S = {}
            for nm in ("gE", "gEt", "gUg", "gu"):
                S[nm] = Rot([mk_f32(nm + tag)])
            for nm in ("gP", "gPt", "gXs", "gXst", "gY", "gin", "gvb", "gkb", "gkd", "gwT", "gvn"):
                S[nm] = Rot([mk_bf16(nm + tag)])
            S["gT"] = Rot([mk_bf16("gT%d%s" % (i, tag)) for i in range(2)])
            S["gTt"] = Rot([mk_bf16("gTt%d%s" % (i, tag)) for i in range(2)])
            S["all"] = [b for r in S.values() for b in r.b]
            return S

        zo[0] = 0
        set0 = gdn_set(lambda n: zview(n, 512, F32, [128, 4, 128]), lambda n: zview(n, 256, BF16, [128, 4, 128]), "a")
        Sf = [zview("Sf%d" % d, 512, F32, [128, 4, 128]) for d in range(2)]
        Sb = [zview("Sb%d" % d, 256, BF16, [128, 4, 128]) for d in range(2)]
        assert zo[0] <= 7424, zo[0]
        yo = [4096]

        def yview(name):
            b = view(arY, name, yo[0], 512, F32, [128, 4, 128])
            yo[0] += 512
            return b

        wo[0] = 0
        set1 = gdn_set(yview, lambda n: wview(n, 256, BF16, [128, 4, 128]), "b")
        assert yo[0] <= 6144 and wo[0] <= 4736
        Z_gdn = set0["all"] + Sf + Sb
        W_gdn = [b for b in set1["all"] if not b.name.startswith(("gE", "gUg", "gu"))]
        Y_gdn = [b for b in set1["all"] if b.name.startswith(("gE", "gUg", "gu"))]
        gsets = [set0, set1]

        stt = rot("stt", [128, 16], F32, 4)
        gst = rot("gst", [128, 8, 4], F32, 4)

        pp = [k.ps("pp%d" % i, [128, 512], F32) for i in range(4)]
        pacc = [k.ps("pacc%d" % i, [128, 512], F32) for i in range(2)]
        ptr = [k.ps("ptr%d" % i, [128, 1024], BF16) for i in range(2)]
        rr = {"pp": 0, "pacc": 0, "ptr": 0, "pp6": 0}
        pm_ap = ptr[0].t[:, 512:1024].bitcast(F32)

        pp6 = pp + pacc

        def nxt(kind):
            lst = {"pp": pp, "pacc": pacc, "ptr": ptr, "pp6": pp6}[kind]
            b = lst[rr[kind] % len(lst)]
            rr[kind] += 1
            return b

        k.dma(ident_f[:], ident_d, wr=[ident_f])
        k.dma(umask[:], umask_d, wr=[umask])
        k.dma(masks[:], mask_d, wr=[masks])
        k.dma(rope[:], rope_d, wr=[rope])
        k.dma(cond[:], cond_d, wr=[cond])
        k.dma(normw[:], normw_d, wr=[normw])
        k.dma(bada[:], bada_d, wr=[bada])
        k.dma(convw[:], convw_d, wr=[convw])
        k.dma(nA[:], alog_d, wr=[nA])
        k.dma(dtb[:], dtb_d, wr=[dtb])
        k.dma(fnorm[:], fnorm_d, wr=[fnorm])
        pool(lambda: G.tensor_copy(out=ident_b[:], in_=ident_f[:]), [ident_f], [ident_b])
        for hf in range(2):
            s_ = stg[hf]
            sv_ = s_[:, 0:896].rearrange("p (a f) -> p a f", f=128)
            k.dma(sv_, lmask_d[:, hf * 7:(hf + 1) * 7, :], wr=[s_])
            pool(lambda: G.tensor_copy(out=lmask[:, hf * 7:(hf + 1) * 7, :], in_=sv_), [s_], [lmask])
        pool(lambda: G.memset(ones_f[:], 1.0), [], [ones_f])
        pool(lambda: G.memset(ones_b[:], 1.0), [], [ones_b])
        pool(lambda: G.memset(mhalf[:], -0.5), [], [mhalf])
        act(lambda: A.activation(out=nA[:], in_=nA[:], func=AF.Exp), [nA], [nA])
        dve(lambda: V.tensor_scalar(out=nA[:], in0=nA[:], scalar1=-1.0, scalar2=None, op0=ALU.mult), [nA], [nA])
        act(lambda: A.activation(out=scT[:], in_=cond[:], func=AF.Tanh, scale=0.5), [cond], [scT])
        dve(lambda: V.scalar_tensor_tensor(out=scT[:], in0=scT[:], scalar=1.0, in1=cond[:], op0=ALU.add, op1=ALU.mult), [scT, cond], [scT])
        dve(lambda: V.tensor_scalar(out=scT[:], in0=scT[:], scalar1=0.5, scalar2=None, op0=ALU.mult), [scT], [scT])

        stg_i = [0]

        lc_fixed = [False]

        def load_cast_tasks(dst_ap, dst_buf, src_ap, n):
            nch = (n + 1023) // 1024
            w = n // nch
            assert w * nch == n
            tasks = []
            for c in range(nch):
                def t_(c=c):
                    if lc_fixed[0]:
                        s = stg[0]
                    else:
                        s = stg[stg_i[0] % 2]
                        stg_i[0] += 1
                    k.dma(s[:, 0:w], src_ap[:, c * w:(c + 1) * w], wr=[s])
                    pool(lambda: G.tensor_copy(out=dst_ap[:, c * w:(c + 1) * w], in_=s[:, 0:w]), [s], [dst_buf])
                tasks.append(t_)
            return tasks

        def weight_tasks(l):
            ts = []
            for kk in range(8):
                ts += load_cast_tasks(wtok[:, kk, :], wtok, wtok_d[l, kk * 128:(kk + 1) * 128, :], 1456)
            for kk in range(8):
                ts += load_cast_tasks(wfeat[:, kk, :], wfeat, wfeat_d[l, kk * 128:(kk + 1) * 128, :], 1536)
            for kk in range(2):
                ts += load_cast_tasks(wqb[:, kk, :], wqb, wqb_d[l, kk * 128:(kk + 1) * 128, :], 768)
            ts += load_cast_tasks(wkn[:], wkn, wkn_d[l], 512)
            ts += load_cast_tasks(wv[:], wv, wv_d[l], 512)
            ts.append(lambda: k.dma(qan[:], qan_d[:, l, :], wr=[qan]))
            ts.append(lambda: k.dma(kvn[:], kvn_d[:, l, :], wr=[kvn]))
            return ts

        def mod_tasks(l, direct=False):
            modT, amod = modTs[l % 2], amods[l % 2]
            st_ = {}
            ts = []

            def abuf(ch):
                return stg[ch % 2] if direct else stg[1]

            def ada_dma(ch):
                s = abuf(ch)
                sv = s[:, 0:1024].rearrange("p (k c) -> p k c", c=128)
                k.dma(sv, wada_d[l, ch], wr=[s])

            def chunk(ch):
                s = abuf(ch)
                sv = s[:, 0:1024].rearrange("p (k c) -> p k c", c=128)
                for kk in range(8):
                    pe(lambda: T.matmul(pm_ap[:, ch * 2:ch * 2 + 2], sv[:, kk, :], scT[:, kk, :],
                                        start=(kk == 0), stop=(kk == 7), skip_group_check=True), [s, scT], [ptr[0]], sig=(kk == 7))
                nx = ch + (2 if direct else 1)
                if nx < 24:
                    ada_dma(nx)

            def fin():
                dve(lambda: V.tensor_tensor(out=modT[:], in0=pm_ap[:, 0:48].rearrange("p (c t) -> p c t", t=2),
                                            in1=bada[:, l, :].unsqueeze(2).to_broadcast([128, 24, 2]), op=ALU.add), [ptr[0], bada], [modT])
                dve(lambda: V.scalar_tensor_tensor(out=amod[:], in0=modT[:, 8:16, :], scalar=1.0,
                                                   in1=normw[:, l, :].unsqueeze(2).to_broadcast([128, 8, 2]),
                                                   op0=ALU.add, op1=ALU.mult), [modT, normw], [amod])
            ts.append(lambda: ada_dma(0))
            if direct:
                ts.append(lambda: ada_dma(1))
            for ch in range(24):
                ts.append(lambda ch=ch: chunk(ch))
            ts.append(fin)
            return ts

        bg = []

        def bg_step(n):
            for _ in range(n):
                if bg:
                    bg.pop(0)()

        xsrcs = [Buf("xsrc0"), Buf("xsrc1")]

        def fm_norm_stats(src, src_ap, nfeat, eps_):
            src_of = src if callable(src) else (lambda kk_: src)
            ps = nxt("pp")
            for kk in range(8):
                sq = th.get()
                act(lambda: A.activation(out=sq[:], in_=src_ap(kk), func=AF.Square), [src_of(kk)], [sq])
                pe(lambda: T.matmul(ps[:], ones_f[:], sq[:], start=(kk == 0), stop=(kk == 7)), [ones_f, sq], [ps], sig=True)
            t1 = rsb
            act(lambda: A.activation(out=t1[:], in_=ps[:], func=AF.Ln, scale=1.0 / nfeat, bias=eps_), [ps], [t1])
            act(lambda: A.activation(out=t1[:], in_=t1[:], func=AF.Exp, scale=-0.5), [t1], [t1])
            return t1

        def layer(l, last):
            pending_w = []
            if l == 0:
                for t_ in mod_tasks(0, direct=True):
                    t_()
                pending_w = weight_tasks(0)
            bg_step(len(bg))
            lc_fixed[0] = False
            modT, amod = modTs[l % 2], amods[l % 2]
            k.dma(onb[:], onb_d[:, l, :], wr=[onb])
            xin = xT_d if l == 0 else xs_d
            for grp in range(2):
                ci = 1 if grp == 0 else 0
                xsrc = xsrcs[grp]
                t0g = grp * 1024
                nseq = 1 if grp == 0 else 4
                tseq = 1024 // nseq
                tps = tseq // 128
                lp = tseq + 2

                def zpos(t):
                    return (t // tseq) * lp + 1 + (t % tseq)

                alias([xblk, xblk2], xm + [oacc, ktok, vtok, zT, qT, kT, vext])
                alias([hT, qlnT, ckvT, kpe], [cT] + Y_gdn)
                alias(Z_att, Z_gdn)
                alias(W_small, W_gdn)
                xbs = (xblk, xblk2)
                for blk in range(2):
                    c0 = t0g + blk * 512
                    k.dma(xbs[blk][:], xin[:, :, c0:c0 + 512], rd=[xsrc], wr=[xbs[blk]])
                for blk in range(2):
                    xb = xbs[blk]
                    rs = fm_norm_stats(xb, lambda kk_: xb[:, kk_, :], 1024, EPS)
                    for kk in range(8):
                        tmp = th.get()
                        dve(lambda: V.tensor_tensor(out=tmp[:], in0=xb[:, kk, :], in1=rs[:], op=ALU.mult), [xb, rs], [tmp])
                        act(lambda: A.activation(out=hT[:, kk, blk * 512:(blk + 1) * 512], in_=tmp[:], func=AF.Identity,
                                                 scale=amod[:, kk, ci:ci + 1], bias=modT[:, kk, ci:ci + 1]), [tmp, amod, modT], [hT])

                while pending_w:
                    pending_w.pop(0)()
                def p2_mm(tl_):
                    tc_ = tl_ * 128
                    bks = (nxt("pp6"), nxt("pp6"), nxt("pp6"))
                    for (pz, c0, n) in ((bks[0], 0, 432), (bks[1], 432, 512), (bks[2], 944, 512)):
                        for kk in range(8):
                            pe(lambda: T.matmul(pz[:, 0:n], hT[:, kk, tc_:tc_ + 128], wtok[:, kk, c0:c0 + n],
                                                start=(kk == 0), stop=(kk == 7)), [hT, wtok], [pz], sig=(kk == 7))
                    return bks

                bks_next = p2_mm(0)
                for tl in range(8):
                    tc0 = tl * 128
                    pzA, pga, pgb = bks_next
                    if tl + 1 < 8:
                        bks_next = p2_mm(tl + 1)
                    z = zA.get()
                    act(lambda: A.copy(out=z[:], in_=pzA[:, 0:432]), [pzA], [z])
                    if dbg and tl == 0 and grp == 0:
                        k.dma(dbg_d[:, 0:512], rsb[:], rd=[rsb])
                        k.dma(dbg_d[:, 512:560], modT[:].rearrange("p c t -> p (c t)"), rd=[modT])
                        k.dma(dbg_d[:, 560:576], amod[:].rearrange("p c t -> p (c t)"), rd=[amod])
                        k.dma(dbg_d[:, 576:592], scT[:].rearrange("p c t -> p (c t)"), rd=[scT])
                        tq = th.get()
                        pool(lambda: G.tensor_copy(out=tq[:], in_=hT[:, 0, 0:512]), [hT], [tq])
                        k.dma(dbg_d[:, 1024:1536], tq[:], rd=[tq])
                        k.dma(dbg_d[:, 1536:1968], z[:], rd=[z])
                        tq2 = th.get()
                        pool(lambda: G.tensor_copy(out=tq2[:], in_=wtok[:, 0, 0:512]), [wtok], [tq2])
                        k.dma(dbg_d[:, 592:1024], tq2[:, 0:432], rd=[tq2])
                        raise StopBuild()
                    for (pg, gdst) in ((pga, ga2), (pgb, gb2)):
                        t1 = th.get()
                        act(lambda: A.activation(out=t1[:], in_=pg[:], func=AF.Tanh, scale=0.5), [pg], [t1])
                        dve(lambda: V.scalar_tensor_tensor(out=gdst[:, tl, :], in0=t1[:], scalar=1.0, in1=pg[:],
                                                           op0=ALU.add, op1=ALU.mult), [t1, pg], [gdst])
                    s1 = stt.get()
                    junk = th.get()
                    pool(lambda: G.memset(s1[:], 0.0), [], [s1])
                    act(lambda: A.activation(out=junk[:, 0:256], in_=z[:, 0:256], func=AF.Square, accum_out=s1[:, 0:1]), [z], [junk, s1])
                    act(lambda: A.activation(out=junk[:, 256:384], in_=z[:, 256:384], func=AF.Square, accum_out=s1[:, 1:2]), [z], [junk, s1])
                    dve(lambda: V.tensor_scalar(out=s1[:, 2:3], in0=s1[:, 0:1], scalar1=1.0 / 256, scalar2=EPS, op0=ALU.mult, op1=ALU.add), [s1], [s1])
                    dve(lambda: V.tensor_scalar(out=s1[:, 3:4], in0=s1[:, 1:2], scalar1=1.0 / 128, scalar2=EPS, op0=ALU.mult, op1=ALU.add), [s1], [s1])
                    pool(lambda: G.tensor_tensor(out=s1[:, 4:6], in0=s1[:, 2:4], in1=mhalf[:, 0:2], op=ALU.pow), [s1, mhalf], [s1])
                    cb = ckvb.get()
                    dve(lambda: V.scalar_tensor_tensor(out=cb[:, 0:256], in0=z[:, 0:256], scalar=s1[:, 4:5], in1=qan[:],
                                                       op0=ALU.mult, op1=ALU.mult), [z, s1, qan], [cb])
                    cs = ckvs.get()
                    dve(lambda: V.scalar_tensor_tensor(out=cs[:], in0=z[:, 256:384], scalar=s1[:, 5:6], in1=kvn[:],
                                                       op0=ALU.mult, op1=ALU.mult), [z, s1, kvn], [cs])
                    pool(lambda: G.tensor_copy(out=cb[:, 256:384], in_=cs[:]), [cs], [cb])
                    if grp == 1:
                        k.dma(nckv_d[l, tc0:tc0 + 128, :], cs[:], rd=[cs])
                    pt = nxt("ptr")
                    for j in range(3):
                        pe(lambda: T.transpose(pt[:, j * 128:(j + 1) * 128], cb[:, j * 128:(j + 1) * 128], ident_b[:]), [cb, ident_b], [pt], sig=(j == 2))
                    dve(lambda: V.tensor_copy(out=qlnT[:, :, tc0:tc0 + 128], in_=pt[:, 0:256].rearrange("p (a b) -> p a b", b=128)), [pt], [qlnT])
                    act(lambda: A.copy(out=ckvT[:, tc0:tc0 + 128], in_=pt[:, 256:384]), [pt], [ckvT])
                    if grp == 0:
                        xv = z[:, 384:416].rearrange("p (a h f) -> p a h f", a=2, h=2)
                        cosv = rope[:, tl, 0:16].rearrange("p (a f) -> p a f", a=2)
                        sinv = rope[:, tl, 16:32].rearrange("p (a f) -> p a f", a=2)
                        ov = kpe[:, tl, :].rearrange("p (a h f) -> p a h f", a=2, h=2)
                        t1 = stt.get()
                        t1v = t1[:, 0:16].rearrange("p (a f) -> p a f", a=2)
                        pool(lambda: G.tensor_tensor(out=ov[:, :, 0, :], in0=xv[:, :, 0, :], in1=cosv, op=ALU.mult), [z, rope], [kpe])
                        pool(lambda: G.tensor_tensor(out=t1v, in0=xv[:, :, 1, :], in1=sinv, op=ALU.mult), [z, rope], [t1])
                        pool(lambda: G.tensor_tensor(out=ov[:, :, 0, :], in0=ov[:, :, 0, :], in1=t1v, op=ALU.subtract), [kpe, t1], [kpe])
                        pool(lambda: G.tensor_tensor(out=ov[:, :, 1, :], in0=xv[:, :, 1, :], in1=cosv, op=ALU.mult), [z, rope], [kpe])
                        pool(lambda: G.tensor_tensor(out=t1v, in0=xv[:, :, 0, :], in1=sinv, op=ALU.mult), [z, rope], [t1])
                        pool(lambda: G.tensor_tensor(out=ov[:, :, 1, :], in0=ov[:, :, 1, :], in1=t1v, op=ALU.add), [kpe, t1], [kpe])
                    else:
                        pool(lambda: G.tensor_copy(out=kpe[:, tl, :], in_=z[:, 384:416]), [z], [kpe])
                        k.dma(nkpe_d[l, tc0:tc0 + 128, :], z[:, 384:416], rd=[z])
                    pool(lambda: G.tensor_copy(out=xab[:, tl, :], in_=z[:, 416:432]), [z], [xab])

                t1 = stt.get()
                t2 = stt.get()
                t1v = t1[:, 0:64].rearrange("p (t c) -> p t c", c=8) if False else None
                gtmp = th.get()
                gv = gtmp[:, 0:64].rearrange("p (t c) -> p t c", c=8)
                dve(lambda: V.tensor_tensor(out=gv, in0=xab[:, :, 0:8], in1=dtb[:, l, :].unsqueeze(1).to_broadcast([128, 8, 8]), op=ALU.add), [xab, dtb], [gtmp])
                act(lambda: A.activation(out=gv, in_=gv, func=AF.Exp), [gtmp], [gtmp])
                act(lambda: A.activation(out=gv, in_=gv, func=AF.Ln, bias=1.0), [gtmp], [gtmp])
                dve(lambda: V.tensor_tensor(out=gall[:], in0=gv, in1=nA[:, l, :].unsqueeze(1).to_broadcast([128, 8, 8]), op=ALU.mult), [gtmp, nA], [gall])
                act(lambda: A.activation(out=ball[:], in_=xab[:, :, 8:16], func=AF.Tanh, scale=0.5), [xab], [ball])
                dve(lambda: V.tensor_scalar(out=ball[:], in0=ball[:], scalar1=0.5, scalar2=0.5, op0=ALU.mult, op1=ALU.add), [ball], [ball])
                pgc = nxt("pp")
                for tl_ in range(8):
                    for d_ in range(2):
                        c_ = tl_ * 8 + d_ * 4
                        pe(lambda: T.matmul(pgc[:, c_:c_ + 4], umask[:, d_, :], gall[:, tl_, d_ * 4:(d_ + 1) * 4], start=True, stop=True, skip_group_check=True),
                           [umask, gall], [pgc], sig=(tl_ == 7 and d_ == 1))
                pgv = pgc[:, 0:64].rearrange("p (t c) -> p t c", c=8)
                dve(lambda: V.tensor_copy(out=svall[:, 0], in_=pgv), [pgc], [svall])
                dve(lambda: V.tensor_scalar(out=svall[:, 1], in0=pgv, scalar1=-1.0, scalar2=None, op0=ALU.mult), [pgc], [svall])
                act(lambda: A.activation(out=svall[:, 2], in_=svall[:, 0], func=AF.Exp), [svall], [svall])
                dve(lambda: V.tensor_scalar(out=svall[:, 3], in0=ball[:], scalar1=-1.0, scalar2=None, op0=ALU.mult), [ball], [svall])
                dve(lambda: V.tensor_scalar(out=svall[:, 4], in0=ball[:], scalar1=0.5, scalar2=None, op0=ALU.mult), [ball], [svall])
                dve(lambda: V.tensor_tensor(out=svall[:, 5], in0=ball[:], in1=svall[:, 2], op=ALU.mult), [ball, svall], [svall])

                alias([qT, kT, vext], [xblk, xblk2])
                pool(lambda: G.memset(vext[:, :, :, 64:65], 2.0), [], [vext])
                nkt = 10 if grp == 0 else 8
                if grp == 0:
                    k.dma(ctx_f[:], cckv_d[l].rearrange("(a p) r -> p a r", p=128), wr=[ctx_f])
                    k.dma(kpe[:, 8:10, :], ckpe_d[l].rearrange("(a p) r -> p a r", p=128), wr=[kpe])
                    for a_ in range(2):
                        pc = nxt("pp")
                        pe(lambda: T.transpose(pc[:, 0:128], ctx_f[:, a_, :], ident_f[:]), [ctx_f, ident_f], [pc])
                        act(lambda: A.copy(out=ckvT[:, 1024 + a_ * 128:1024 + (a_ + 1) * 128], in_=pc[:, 0:128]), [pc], [ckvT])
                for kt in range(nkt):
                    pv = nxt("pp")
                    pe(lambda: T.matmul(pv[:], ckvT[:, kt * 128:(kt + 1) * 128], wv[:], start=True, stop=True), [ckvT, wv], [pv])
                    act(lambda: A.copy(out=vext[:, kt, :, 0:64], in_=pv[:].rearrange("p (h d) -> p h d", d=64)), [pv], [vext])
                for hg in range(2):
                    def k_mm(kt_):
                        pk_ = nxt("pp")
                        pe(lambda: T.matmul(pk_[:, 0:256], ckvT[:, kt_ * 128:(kt_ + 1) * 128], wkn[:, hg * 256:(hg + 1) * 256], start=True, stop=True), [ckvT, wkn], [pk_])
                        return pk_

                    pk_next = k_mm(0)
                    for kt in range(nkt):
                        pk = pk_next
                        if kt + 1 < nkt:
                            pk_next = k_mm(kt + 1)
                        ks = ksb.get()
                        dve(lambda: V.tensor_copy(out=ks[:, 0:4, 0:64], in_=pk[:, 0:256].rearrange("p (h d) -> p h d", d=64)), [pk], [ks])
                        pool(lambda: G.tensor_copy(out=ks[:, 0:4, 64:96], in_=kpe[:, kt, :].unsqueeze(1).to_broadcast([128, 4, 32])), [kpe], [ks])
                        pt = nxt("ptr")
                        for h in range(4):
                            pe(lambda: T.transpose(pt[0:96, h * 128:(h + 1) * 128], ks[:, h, :], ident_b[:]), [ks, ident_b], [pt], sig=(h == 3))
                        act(lambda: A.copy(out=kT[:, :, kt * 128:(kt + 1) * 128], in_=pt[0:96, 0:512].rearrange("p (h t) -> p h t", t=128)), [pt], [kT])
                    def q_mm(tl_):
                        tc_ = tl_ * 128
                        pq_ = nxt("pp")
                        for (c0, n, o0) in ((hg * 256, 256, 0), (512 + hg * 128, 128, 256)):
                            for kk in range(2):
                                pe(lambda: T.matmul(pq_[:, o0:o0 + n], qlnT[:, kk, tc_:tc_ + 128], wqb[:, kk, c0:c0 + n],
                                                    start=(kk == 0), stop=(kk == 1), skip_group_check=True), [qlnT, wqb], [pq_], sig=(kk == 1 and o0 == 256))
                        return pq_

                    pq_next = q_mm(0)
                    for tl in range(8):
                        tc0 = tl * 128
                        pq = pq_next
                        if tl + 1 < 8:
                            pq_next = q_mm(tl + 1)
                        qs = qsb.get()
                        dve(lambda: V.tensor_copy(out=qs[:, 0:4, 0:64], in_=pq[:, 0:256].rearrange("p (h d) -> p h d", d=64)), [pq], [qs])
                        if grp == 0:
                            xv = pq[:, 256:384].rearrange("p (h a g f) -> p h a g f", h=4, a=2, g=2)
                            ov = qs[:, 0:4, 64:96].rearrange("p h (a g f) -> p h a g f", a=2, g=2)
                            cosv = rope[:, tl, 0:16].rearrange("p (a f) -> p a f", a=2).unsqueeze(1).to_broadcast([128, 4, 2, 8])
                            sinv = rope[:, tl, 16:32].rearrange("p (a f) -> p a f", a=2).unsqueeze(1).to_broadcast([128, 4, 2, 8])
                            ta = th.get()
                            tav = ta[:, 0:64].rearrange("p (h a f) -> p h a f", h=4, a=2)
                            tbv = ta[:, 64:128].rearrange("p (h a f) -> p h a f", h=4, a=2)
                            dve(lambda: V.tensor_tensor(out=tav, in0=xv[:, :, :, 0, :], in1=cosv, op=ALU.mult), [pq, rope], [ta])
                            dve(lambda: V.tensor_tensor(out=tbv, in0=xv[:, :, :, 1, :], in1=sinv, op=ALU.mult), [pq, rope], [ta])
                            dve(lambda: V.tensor_tensor(out=ov[:, :, :, 0, :], in0=tav, in1=tbv, op=ALU.subtract), [ta], [qs])
                            dve(lambda: V.tensor_tensor(out=tav, in0=xv[:, :, :, 1, :], in1=cosv, op=ALU.mult), [pq, rope], [ta])
                            dve(lambda: V.tensor_tensor(out=tbv, in0=xv[:, :, :, 0, :], in1=sinv, op=ALU.mult), [pq, rope], [ta])
                            dve(lambda: V.tensor_tensor(out=ov[:, :, :, 1, :], in0=tav, in1=tbv, op=ALU.add), [ta], [qs])
                        else:
                            dve(lambda: V.tensor_copy(out=qs[:, 0:4, 64:96], in_=pq[:, 256:384].rearrange("p (h d) -> p h d", d=32)), [pq], [qs])
                        pt = nxt("ptr")
                        for h in range(4):
                            pe(lambda: T.transpose(pt[0:96, h * 128:(h + 1) * 128], qs[:, h, :], ident_b[:]), [qs, ident_b], [pt], sig=(h == 3))
                        act(lambda: A.copy(out=qT[:, :, tc0:tc0 + 128], in_=pt[0:96, 0:512].rearrange("p (h t) -> p h t", t=128)), [pt], [qT])
                    nq = 512 if grp == 0 else 256
                    for qb in range(1024 // nq):
                        q0 = qb * nq
                        if grp == 0:
                            kcs = list(range(10))
                        else:
                            kcs = [qb * 2, qb * 2 + 1]
                        nqt = nq // 128
                        for h in range(4):
                            hh = hg * 4 + h
                            pa = nxt("pacc")
                            pscs = {}

                            def qk(ki_):
                                kc_ = kcs[ki_]
                                p_ = nxt("pp")
                                pe(lambda: T.matmul(p_[:, 0:nq], kT[:, h, kc_ * 128:(kc_ + 1) * 128], qT[:, h, q0:q0 + nq], start=True, stop=True), [kT, qT], [p_])
                                pscs[ki_] = p_

                            qk(0)
                            if len(kcs) > 1:
                                qk(1)
                            for ki, kc in enumerate(kcs):
                                if ki + 2 < len(kcs):
                                    qk(ki + 2)
                                psc = pscs.pop(ki)
                                pb = ptb.get()
                                act(lambda: A.activation(out=pb[:, 0:nq], in_=psc[:, 0:nq], func=AF.Exp, scale=96.0 ** -0.5), [psc], [pb])
                                for qt in range(nqt):
                                    pe(lambda: T.matmul(pa[:, qt * 65:(qt + 1) * 65], pb[:, qt * 128:(qt + 1) * 128], vext[:, kc, hh, :],
                                                        start=(ki == 0 and qt == 0), stop=(ki == len(kcs) - 1), skip_group_check=True),
                                       [pb, vext], [pa], sig=(ki == len(kcs) - 1 and qt == nqt - 1))
                            s1 = stt.get()
                            dve(lambda: V.reciprocal(out=s1[:, 0:nqt], in_=pa[:, 0:nqt * 65].rearrange("p (q c) -> p q c", c=65)[:, :, 64]), [pa], [s1])
                            for qt in range(nqt):
                                tl = (q0 // 128) + qt
                                dve(lambda: V.scalar_tensor_tensor(out=oa[:, tl, hh * 64:(hh + 1) * 64], in0=pa[:, qt * 65:qt * 65 + 64], scalar=s1[:, qt:qt + 1],
                                                                   in1=ga2[:, tl, hh * 64:(hh + 1) * 64], op0=ALU.mult, op1=ALU.mult), [pa, s1, ga2], [oa])
                for tl in range(8):
                    pt = nxt("ptr")
                    for j in range(4):
                        pe(lambda: T.transpose(pt[:, j * 128:(j + 1) * 128], oa[:, tl, j * 128:(j + 1) * 128], ident_b[:]), [oa, ident_b], [pt], sig=(j == 3))
                    act(lambda: A.copy(out=mixT[:, 0:4, tl * 128:(tl + 1) * 128], in_=pt[:, 0:512].rearrange("p (j t) -> p j t", t=128)), [pt], [mixT])

                alias([zT], [qT, kT, vext])
                for s_ in range(nseq):
                    pool(lambda: G.memset(zT[:, :, s_ * lp:s_ * lp + 1], 0.0), [], [zT])
                    pool(lambda: G.memset(zT[:, :, s_ * lp + lp - 1:s_ * lp + lp], 0.0), [], [zT])
                seg = min(512, tseq)
                for j in range(12):
                    for blk in range(2):
                        pz = nxt("pp")
                        for kk in range(8):
                            pe(lambda: T.matmul(pz[:], wfeat[:, kk, j * 128:(j + 1) * 128], hT[:, kk, blk * 512:(blk + 1) * 512],
                                                start=(kk == 0), stop=(kk == 7)), [wfeat, hT], [pz], sig=(kk == 7))
                        for s0 in range(0, 512, seg):
                            t0 = blk * 512 + s0
                            if (j + blk) % 2 == 0:
                                act(lambda: A.copy(out=zT[:, j, zpos(t0):zpos(t0) + seg], in_=pz[:, s0:s0 + seg]), [pz], [zT])
                            else:
                                dve(lambda: V.tensor_copy(out=zT[:, j, zpos(t0):zpos(t0) + seg], in_=pz[:, s0:s0 + seg]), [pz], [zT])
                alias([cT], [hT, qlnT, ckvT, kpe])
                def mk_diag(j_):
                    d_ = dg.get()
                    for kk in range(3):
                        dve(lambda: V.tensor_scalar(out=d_[:, kk, :], in0=ident_b[:], scalar1=convw[:, l, kk * 12 + j_:kk * 12 + j_ + 1], scalar2=None, op0=ALU.mult), [ident_b, convw], [d_])
                    return d_

                d3_next = mk_diag(0)
                for j in range(12):
                    d3 = d3_next
                    if j + 1 < 12:
                        d3_next = mk_diag(j + 1)
                    for blk in range(2):
                        pz = nxt("pp")
                        for s0 in range(0, 512, seg):
                            t0 = blk * 512 + s0
                            p0 = zpos(t0) - 1
                            for kk in range(3):
                                pe(lambda: T.matmul(pz[:, s0:s0 + seg], d3[:, kk, :], zT[:, j, p0 + kk:p0 + kk + seg], start=(kk == 0), stop=(kk == 2)),
                                   [d3, zT], [pz], sig=(kk == 2 and s0 + seg == 512))
                        t1 = th.get()
                        act(lambda: A.activation(out=t1[:], in_=pz[:], func=AF.Tanh, scale=0.5), [pz], [t1])
                        dve(lambda: V.scalar_tensor_tensor(out=cT[:, j, blk * 512:(blk + 1) * 512], in0=t1[:], scalar=1.0, in1=pz[:], op0=ALU.add, op1=ALU.mult), [t1, pz], [cT])
                def l2_front(j_, blk_):
                    sl_ = slice(blk_ * 512, (blk_ + 1) * 512)
                    sq = tb.get()
                    act(lambda: A.activation(out=sq[:], in_=cT[:, j_, sl_], func=AF.Square), [cT], [sq])
                    ps_ = nxt("pp")
                    pe(lambda: T.matmul(ps_[:], ones_b[:], sq[:], start=True, stop=True), [ones_b, sq], [ps_])
                    return ps_

                items = [(j_, b_) for j_ in range(8) for b_ in range(2)]
                ps_next = l2_front(*items[0])
                for ii, (j, blk) in enumerate(items):
                    if True:
                        sl = slice(blk * 512, (blk + 1) * 512)
                        ps = ps_next
                        if ii + 1 < len(items):
                            ps_next = l2_front(*items[ii + 1])
                        t1 = th.get()
                        mul = 128.0 if j < 4 else 1.0
                        act(lambda: A.activation(out=t1[:], in_=ps[:], func=AF.Ln, scale=mul, bias=4.0 * EPS * mul), [ps], [t1])
                        act(lambda: A.activation(out=t1[:], in_=t1[:], func=AF.Exp, scale=-0.5), [t1], [t1])
                        dve(lambda: V.tensor_tensor(out=cT[:, j, sl], in0=cT[:, j, sl], in1=t1[:], op=ALU.mult), [cT, t1], [cT])
                alias([oacc, ktok, vtok], [zT])
                alias(Z_gdn, Z_att)
                alias(W_gdn, W_small)
                for tl in range(8):
                    pt = nxt("ptr")
                    for j in range(8):
                        pe(lambda: T.transpose(pt[:, j * 128:(j + 1) * 128], cT[:, 4 + j, tl * 128:(tl + 1) * 128], ident_b[:]), [cT, ident_b], [pt], sig=(j == 7))
                    act(lambda: A.copy(out=ktok[:, tl, :], in_=pt[:, 0:512]), [pt], [ktok])
                    dve(lambda: V.tensor_copy(out=vtok[:, tl, :], in_=pt[:, 512:1024]), [pt], [vtok])

                alias(Y_gdn, [cT])
                pool(lambda: G.memset(oacc[:], 0.0), [], [oacc])
                for s_ in range(nseq):
                    for d in range(2):
                        if grp == 0:
                            k.dma(Sf[d][:], sgdn_d[l, d * 4:(d + 1) * 4].rearrange("a p v -> p a v"), wr=[Sf[d]])
                            pool(lambda: G.tensor_copy(out=Sb[d][:], in_=Sf[d][:]), [Sf[d]], [Sb[d]])
                        else:
                            pool(lambda: G.memset(Sf[d][:], 0.0), [], [Sf[d]])
                            pool(lambda: G.memset(Sb[d][:], 0.0), [], [Sb[d]])
                    for step in range(tps):
                        prefetch = (grp == 1 and not last)
                        if prefetch and s_ == 0 and step == 0:
                            lc_fixed[0] = True
                            wt, mt = weight_tasks(l + 1), mod_tasks(l + 1)
                            bg.append(mt.pop(0))
                            while wt or mt:
                                if wt:
                                    bg.append(wt.pop(0))
                                if mt:
                                    bg.append(mt.pop(0))
                        more = step + 1 < tps
                        gens = [gdn_unit(l, s_ * tps + step, 0, (s_ * tps + step + 1) if more else None),
                                gdn_unit(l, s_ * tps + tps - 1 - step, 1, (s_ * tps + tps - 2 - step) if more else None)]
                        rounds = 0
                        while gens:
                            for g_ in list(gens):
                                try:
                                    next(g_)
                                except StopIteration:
                                    gens.remove(g_)
                            rounds += 1
                            if prefetch and rounds % 5 == 0:
                                bg_step(2)
                    if grp == 1:
                        for d in range(2):
                            k.dma(nst_d[s_, l, d * 4:(d + 1) * 4].rearrange("a p v -> p a v"), Sf[d][:], rd=[Sf[d]])
                alias(W_small, W_gdn)
                for tl in range(8):
                    junk = th.get()
                    act(lambda: A.activation(out=junk[:], in_=oacc[:, tl, :], func=AF.Square), [oacc], [junk])
                    dve(lambda: V.tensor_reduce(out=rsb[:, tl * 4:(tl + 1) * 4], in_=junk[:].rearrange("p (h f) -> p h f", f=128),
                                                axis=mybir.AxisListType.X, op=ALU.add), [junk], [rsb])
                act(lambda: A.activation(out=rsb[:, 32:64], in_=rsb[:, 0:32], func=AF.Ln, scale=1.0 / 128, bias=EPS), [rsb], [rsb])
                act(lambda: A.activation(out=rsb[:, 32:64], in_=rsb[:, 32:64], func=AF.Exp, scale=-0.5), [rsb], [rsb])
                dve(lambda: V.tensor_scalar(out=rsb[:, 32:64], in0=rsb[:, 32:64], scalar1=0.5, scalar2=None, op0=ALU.mult), [rsb], [rsb])
                for tl in range(8):
                    ob = tb.get()
                    for h in range(4):
                        t1 = th.get()
                        dve(lambda: V.scalar_tensor_tensor(out=t1[:, 0:128], in0=oacc[:, tl, h * 128:(h + 1) * 128], scalar=rsb[:, 32 + tl * 4 + h:33 + tl * 4 + h], in1=onb[:],
                                                           op0=ALU.mult, op1=ALU.mult), [oacc, rsb, onb], [t1])
                        dve(lambda: V.tensor_tensor(out=ob[:, h * 128:(h + 1) * 128], in0=t1[:, 0:128], in1=gb2[:, tl, h * 128:(h + 1) * 128], op=ALU.mult), [t1, gb2], [ob])
                    pt = nxt("ptr")
                    for j in range(4):
                        pe(lambda: T.transpose(pt[:, j * 128:(j + 1) * 128], ob[:, j * 128:(j + 1) * 128], ident_b[:]), [ob, ident_b], [pt], sig=(j == 3))
                    act(lambda: A.copy(out=mixT[:, 4:8, tl * 128:(tl + 1) * 128], in_=pt[:, 0:512].rearrange("p (j t) -> p j t", t=128)), [pt], [mixT])

                alias(xm, [oacc, ktok, vtok])

                def wload(m_):
                    s_ = stg[stg_i[0] % 2]
                    stg_i[0] += 1
                    sv_ = s_[:, 0:1024].rearrange("p (k c) -> p k c", c=128)
                    k.dma(sv_, wout_d[l, m_], wr=[s_])
                    pool(lambda: G.tensor_copy(out=woutm[m_ % 2][:], in_=sv_), [s_], [woutm[m_ % 2]])

                def xload(m_):
                    k.dma(xm[m_][:], xin[:, m_, t0g:t0g + 1024], rd=[xsrc], wr=[xm[m_]])

                wload(0)
                wload(1)
                xload(0)
                xload(1)
                for m in range(8):
                    wm = woutm[m % 2]
                    for blk in range(2):
                        bs_ = slice(blk * 512, (blk + 1) * 512)
                        po = nxt("pp")
                        for kk in range(8):
                            pe(lambda: T.matmul(po[:], wm[:, kk, :], mixT[:, kk, bs_], start=(kk == 0), stop=(kk == 7)), [wm, mixT], [po], sig=(kk == 7))
                        dve(lambda: V.scalar_tensor_tensor(out=xm[m][:, bs_], in0=po[:], scalar=modT[:, 16 + m, ci:ci + 1], in1=xm[m][:, bs_],
                                                           op0=ALU.mult, op1=ALU.add), [po, modT, xm[m]], [xm[m]])
                    if m + 2 < 8:
                        wload(m + 2)
                        xload(m + 2)
                    if not last:
                        k.dma(xs_d[:, m, t0g:t0g + 1024], xm[m][:], rd=[xm[m]], wr=[xsrc])
                if last:
                    for blk in range(2):
                        bs_ = slice(blk * 512, (blk + 1) * 512)
                        rs = fm_norm_stats(lambda kk_: xm[kk_], lambda kk_: xm[kk_][:, bs_], 1024, EPS)
                        for kk in range(8):
                            dve(lambda: V.scalar_tensor_tensor(out=xm[kk][:, bs_], in0=xm[kk][:, bs_], scalar=fnorm[:, kk:kk + 1], in1=rs[:],
                                                               op0=ALU.mult, op1=ALU.mult), [xm[kk], fnorm, rs], [xm[kk]])
                    for kk in range(8):
                        k.dma(yT_d[:, kk, t0g:t0g + 1024], xm[kk][:], rd=[xm[kk]])

        ug_ready = {}

        def gdn_unit(l, tl, d, nxt_tl=None):
            B = gsets[d]
            pbanks = ([pp[0], pp[1], pacc[0]], [pp[2], pp[3], pacc[1]])[d]
            pctr = [0]

            def nxt(kind):
                if kind == "ptr":
                    return ptr[d]
                b_ = pbanks[pctr[0] % 3]
                pctr[0] += 1
                return b_

            tcs = slice(tl * 128, (tl + 1) * 128)
            last = 127 if d == 0 else 0
            sv = gst.get()
            g4 = gall[:, tl, d * 4:(d + 1) * 4]
            RM = {0: 0, 1: 1, 2: 2, 5: 3, 6: 4, 7: 5}

            def SA(r_):
                return svall[:, RM[r_], tl, d * 4:(d + 1) * 4]

            def build_ug(tl_):
                ug_ = B["gUg"].get()
                dve(lambda: V.tensor_tensor(out=ug_[:], in0=umask[:, d, :].unsqueeze(1).to_broadcast([128, 4, 128]),
                                            in1=gall[:, tl_, d * 4:(d + 1) * 4].unsqueeze(2).to_broadcast([128, 4, 128]), op=ALU.mult), [umask, gall], [ug_])
                return ug_

            if ug_ready.get(d, (None, None))[0] == tl:
                ug = ug_ready.pop(d)[1]
            else:
                ug = build_ug(tl)
            ugf = ug[:].rearrange("p h f -> p (h f)")
            yield
            pb1, pb2 = nxt("pp"), nxt("pp")
            pe(lambda: T.matmul(pb1[:], ones_f[:], ugf, start=True, stop=False, skip_group_check=True), [ones_f, ug], [pb1], sig=False)
            for h in range(4):
                pe(lambda: T.matmul(pb1[:, h * 128:(h + 1) * 128], ident_f[:], masks[:, 2 * d, :], start=False, stop=(h == 3), skip_group_check=True), [ident_f, masks], [pb1], sig=(h == 3))
            pe(lambda: T.matmul(pb2[:], ones_f[:], ugf, start=True, stop=False, skip_group_check=True), [ones_f, ug], [pb2], sig=False)
            for h in range(4):
                pe(lambda: T.matmul(pb2[:, h * 128:(h + 1) * 128], ident_f[:], masks[:, 2 * d + 1, :], start=False, stop=(h == 3), skip_group_check=True), [ident_f, masks], [pb2], sig=(h == 3))
            E, Et = B["gE"].get(), B["gEt"].get()
            for h in range(4):
                act(lambda: A.activation(out=E[:, h, :], in_=pb1[:, h * 128:(h + 1) * 128], func=AF.Exp, scale=-1.0, bias=SA(0)[:, h:h + 1]), [pb1, svall], [E])
                act(lambda: A.activation(out=Et[:, h, :], in_=pb2[:, h * 128:(h + 1) * 128], func=AF.Exp, scale=1.0, bias=SA(1)[:, h:h + 1]), [pb2, svall], [Et])
            act(lambda: A.activation(out=sv[:, 3, :], in_=pb2[:].rearrange("p (h f) -> p h f", f=128)[:, :, last], func=AF.Exp), [pb2], [sv])
            dve(lambda: V.tensor_copy(out=sv[:, 4, :], in_=Et[:, :, last]), [Et], [sv])
            yield
            pkk, pqk = nxt("pp"), nxt("pp")
            for h in range(4):
                pe(lambda: T.matmul(pkk[:, h * 128:(h + 1) * 128], cT[:, 4 + h, tcs], cT[:, 4 + h, tcs], start=True, stop=True), [cT], [pkk], sig=(h == 3))
            for h in range(4):
                pe(lambda: T.matmul(pqk[:, h * 128:(h + 1) * 128], cT[:, 4 + h, tcs], cT[:, h, tcs], start=True, stop=True), [cT], [pqk], sig=(h == 3))
            P = B["gP"].get()
            for h in range(4):
                dve(lambda: V.scalar_tensor_tensor(out=P[:, h, :], in0=pkk[:, h * 128:(h + 1) * 128], scalar=SA(5)[:, h:h + 1], in1=E[:, h, :],
                                                   op0=ALU.mult, op1=ALU.mult), [pkk, svall, E], [P])
            intra = B["gin"].get()
            dve(lambda: V.tensor_tensor(out=intra[:].rearrange("p h f -> p (h f)"), in0=pqk[:], in1=Et[:].rearrange("p h f -> p (h f)"), op=ALU.mult), [pqk, Et], [intra])
            yield
            pt = nxt("ptr")
            for h in range(4):
                pe(lambda: T.transpose(pt[:, h * 128:(h + 1) * 128], P[:, h, :], ident_b[:]), [P, ident_b], [pt], sig=(h == 3))
            Pt = B["gPt"].get()
            act(lambda: A.copy(out=Pt[:].rearrange("p h f -> p (h f)"), in_=pt[:, 0:512]), [pt], [Pt])
            idb4 = ident_b[:].unsqueeze(1).to_broadcast([128, 4, 128])

            def mA(lev):
                return lmask[:, 2 * lev + (0 if d == 0 else 1), :].unsqueeze(1).to_broadcast([128, 4, 128])

            def mB(lev):
                return lmask[:, 2 * lev + (1 if d == 0 else 0), :].unsqueeze(1).to_broadcast([128, 4, 128])

            yield
            Tc, Ttc = B["gT"].get(), B["gTt"].get()
            xs, xst = B["gXs"].get(), B["gXst"].get()
            dve(lambda: V.tensor_tensor(out=xs[:], in0=P[:], in1=mA(0), op=ALU.mult), [P, lmask], [xs])
            dve(lambda: V.tensor_tensor(out=Tc[:], in0=xs[:], in1=idb4, op=ALU.add), [xs, ident_b], [Tc])
            dve(lambda: V.tensor_tensor(out=xst[:], in0=Pt[:], in1=mB(0), op=ALU.mult), [Pt, lmask], [xst])
            dve(lambda: V.tensor_tensor(out=Ttc[:], in0=xst[:], in1=idb4, op=ALU.add), [xst, ident_b], [Ttc])
            for lev in range(1, 7):
                yield
                py = nxt("pp")
                for h in range(4):
                    pe(lambda: T.matmul(py[:, h * 128:(h + 1) * 128], Pt[:, h, :], Tc[:, h, :], start=True, stop=True), [Pt, Tc], [py], sig=(h == 3))
                Y = B["gY"].get()
                dve(lambda: V.tensor_tensor(out=Y[:], in0=py[:].rearrange("p (h f) -> p h f", f=128), in1=mA(lev), op=ALU.mult), [py, lmask], [Y])
                yield
                if lev < 6:
                    pm_ = nxt("pp")
                    for h in range(4):
                        pe(lambda: T.matmul(pm_[:, h * 128:(h + 1) * 128], Ttc[:, h, :], Y[:, h, :], start=(h == 0), stop=False, skip_group_check=True), [Ttc, Y], [pm_], sig=False)
                        pe(lambda: T.matmul(pm_[:, h * 128:(h + 1) * 128], ident_b[:], Tc[:, h, :], start=False, stop=(h == 3), skip_group_check=True), [ident_b, Tc], [pm_], sig=(h == 3))
                pmt = nxt("pp")
                for h in range(4):
                    pe(lambda: T.matmul(pmt[:, h * 128:(h + 1) * 128], Y[:, h, :], Ttc[:, h, :], start=(h == 0), stop=False, skip_group_check=True), [Ttc, Y], [pmt], sig=False)
                    pe(lambda: T.matmul(pmt[:, h * 128:(h + 1) * 128], ident_b[:], Ttc[:, h, :], start=False, stop=(h == 3), skip_group_check=True), [ident_b, Ttc], [pmt], sig=(h == 3))
                if lev < 6:
                    Tn = B["gT"].get()
                    act(lambda: A.copy(out=Tn[:].rearrange("p h f -> p (h f)"), in_=pm_[:]), [pm_], [Tn])
                    Tc = Tn
                Ttn = B["gTt"].get()
                if lev % 2 == 0:
                    dve(lambda: V.tensor_copy(out=Ttn[:].rearrange("p h f -> p (h f)"), in_=pmt[:]), [pmt], [Ttn])
                else:
                    act(lambda: A.copy(out=Ttn[:].rearrange("p h f -> p (h f)"), in_=pmt[:]), [pmt], [Ttn])
                Ttc = Ttn
            Tt = Ttc
            yield
            vb, kb, kd = B["gvb"].get(), B["gkb"].get(), B["gkd"].get()
            for h in range(4):
                hs = slice(h * 128, (h + 1) * 128)
                act(lambda: A.activation(out=vb[:, h, :], in_=vtok[:, tl, hs], func=AF.Copy, scale=SA(6)[:, h:h + 1]), [vtok, svall], [vb])
                dve(lambda: V.tensor_scalar(out=kb[:, h, :], in0=ktok[:, tl, hs], scalar1=SA(7)[:, h:h + 1], scalar2=None, op0=ALU.mult), [ktok, svall], [kb])
                act(lambda: A.activation(out=kd[:, h, :], in_=ktok[:, tl, hs], func=AF.Copy, scale=sv[:, 4, h:h + 1]), [ktok, sv], [kd])
            pu, pw = nxt("pp"), nxt("pp")
            for h in range(4):
                pe(lambda: T.matmul(pu[:, h * 128:(h + 1) * 128], Tt[:, h, :], vb[:, h, :], start=True, stop=True), [Tt, vb], [pu], sig=(h == 3))
            for h in range(4):
                pe(lambda: T.matmul(pw[:, h * 128:(h + 1) * 128], kb[:, h, :], Tt[:, h, :], start=True, stop=True), [Tt, kb], [pw], sig=(h == 3))
            u, wT = B["gu"].get(), B["gwT"].get()
            act(lambda: A.copy(out=u[:].rearrange("p h f -> p (h f)"), in_=pu[:]), [pu], [u])
            dve(lambda: V.tensor_copy(out=wT[:].rearrange("p h f -> p (h f)"), in_=pw[:]), [pw], [wT])
            yield
            pws = nxt("pp")
            for h in range(4):
                pe(lambda: T.matmul(pws[:, h * 128:(h + 1) * 128], wT[:, h, :], Sb[d][:, h, :], start=True, stop=True), [wT, Sb[d]], [pws], sig=(h == 3))
            vn = B["gvn"].get()
            dve(lambda: V.tensor_tensor(out=vn[:].rearrange("p h f -> p (h f)"), in0=u[:].rearrange("p h f -> p (h f)"), in1=pws[:], op=ALU.subtract), [u, pws], [vn])
            if nxt_tl is not None:
                ug_ready[d] = (nxt_tl, build_ug(nxt_tl))
            yield
            pqs, piv, pds = nxt("pp"), nxt("pp"), nxt("pp")
            for h in range(4):
                pe(lambda: T.matmul(pqs[:, h * 128:(h + 1) * 128], cT[:, h, tcs], Sb[d][:, h, :], start=True, stop=True), [cT, Sb[d]], [pqs], sig=(h == 3))
            for h in range(4):
                pe(lambda: T.matmul(piv[:, h * 128:(h + 1) * 128], intra[:, h, :], vn[:, h, :], start=True, stop=True), [intra, vn], [piv], sig=(h == 3))
            for h in range(4):
                pe(lambda: T.matmul(pds[:, h * 128:(h + 1) * 128], kd[:, h, :], vn[:, h, :], start=True, stop=True), [kd, vn], [pds], sig=(h == 3))
            for h in range(4):
                hs = slice(h * 128, (h + 1) * 128)
                dve(lambda: V.scalar_tensor_tensor(out=oacc[:, tl, hs], in0=pqs[:, hs], scalar=SA(2)[:, h:h + 1], in1=oacc[:, tl, hs],
                                                   op0=ALU.mult, op1=ALU.add), [pqs, svall, oacc], [oacc])
            dve(lambda: V.tensor_tensor(out=oacc[:, tl, :], in0=piv[:], in1=oacc[:, tl, :], op=ALU.add), [piv, oacc], [oacc])
            for h in range(4):
                dve(lambda: V.scalar_tensor_tensor(out=Sf[d][:, h, :], in0=Sf[d][:, h, :], scalar=sv[:, 3, h:h + 1], in1=pds[:, h * 128:(h + 1) * 128],
                                                   op0=ALU.mult, op1=ALU.add), [Sf[d], sv, pds], [Sf[d]])
            act(lambda: A.copy(out=Sb[d][:], in_=Sf[d][:]), [Sf[d]], [Sb[d]])
            yield

        try:
            for l in range(n_layers):
                layer(l, l == n_layers - 1)
        except StopBuild:
            pass
        k.finish()
        print("instructions:", k.nops)
    return nc


def _host_inputs(inp, core):
    f = np.float32
    bs = core % 4
    xs = inp["x_sample"][bs]
    xp = inp["x_prompt"][4 * core:4 * core + 4].reshape(1024, 1024)
    xall = np.concatenate([xs, xp], 0)
    xT = np.ascontiguousarray(xall.T.reshape(8, 128, 2048).transpose(1, 0, 2))
    cond2 = np.stack([inp["c_ctx"], inp["c"][bs]], 1)
    cond = np.ascontiguousarray(cond2.reshape(8, 128, 2).transpose(1, 0, 2))

    def fm(v, nch):
        return np.ascontiguousarray(v.reshape(v.shape[0], nch, 128).transpose(2, 0, 1))

    def bc(v):
        return np.ascontiguousarray(np.broadcast_to(v[None], (128,) + v.shape))

    w_in = inp["w_in"]
    o = np.cumsum([0, 256, 128, 32, 512, 512, 512, 512, 512, 8, 8])
    sl = lambda i: slice(o[i], o[i + 1])
    wtok = np.concatenate([w_in[:, :, sl(0)], w_in[:, :, sl(1)], w_in[:, :, sl(2)], w_in[:, :, sl(8)], w_in[:, :, sl(9)],
                           w_in[:, :, sl(3)], w_in[:, :, sl(7)]], 2)
    wfeat = np.concatenate([w_in[:, :, sl(4)], w_in[:, :, sl(5)], w_in[:, :, sl(6)]], 2)
    wqb = inp["w_qb"].reshape(NL, 256, 8, 96)
    wqb2 = np.concatenate([wqb[..., :64].reshape(NL, 256, 512), wqb[..., 64:].reshape(NL, 256, 256)], 2)
    wkvb = inp["w_kvb"].reshape(NL, 128, 8, 128)
    wkn = wkvb[..., :64].reshape(NL, 128, 512)
    wv = wkvb[..., 64:].reshape(NL, 128, 512)
    convw = np.ascontiguousarray(inp["conv_w"].reshape(NL, 3, 12, 128).transpose(3, 0, 1, 2).reshape(128, NL, 36))
    ar = np.arange(128)
    ufwd = (ar[:, None] <= ar[None, :]).astype(f)
    ubwd = (ar[:, None] >= ar[None, :]).astype(f)
    umask = np.stack([ufwd, ubwd], 1)
    p_, f_ = ar[:, None], ar[None, :]
    m1f = BIG * (f_ >= p_)
    nm2f = -BIG * (f_ < p_)
    m1b = BIG * (f_ <= p_)
    nm2b = -BIG * (f_ > p_)
    masks = np.stack([m1f, nm2f, m1b, nm2b], 1).astype(f)
    lm = []
    for lev in range(7):
        s_ = 1 << lev
        ml = ((p_ // (2 * s_)) == (f_ // (2 * s_))) & ((p_ % (2 * s_)) >= s_) & ((f_ % (2 * s_)) < s_)
        lm.append(ml.astype(f))
        lm.append(ml.T.astype(f))
    lmask = np.stack(lm, 1)
    t = np.arange(1024)
    inv = (10000.0 ** (-np.arange(8, dtype=f) / 8)).astype(f)
    ang = np.stack([(t // 64).astype(f)[:, None] * inv, (t % 64).astype(f)[:, None] * inv], 1)
    cs = np.concatenate([np.cos(ang).reshape(1024, 16), np.sin(ang).reshape(1024, 16)], 1).astype(f)
    rope = np.ascontiguousarray(cs.reshape(8, 128, 32).transpose(1, 0, 2))
    d = {
        "xT": xT, "cond": cond, "normw": fm(inp["norm_w"], 8),
        "wada": inp["w_ada"].reshape(NL, 8, 128, 24, 128).transpose(0, 3, 2, 1, 4), "bada": fm(inp["b_ada"], 24),
        "wtok": wtok, "wfeat": wfeat, "wqb": wqb2, "wkn": wkn, "wv": wv, "wout": inp["w_out"].reshape(NL, 8, 128, 8, 128).transpose(0, 3, 2, 1, 4),
        "qan": bc(inp["q_a_norm"]), "kvn": bc(inp["kv_a_norm"]), "onb": bc(inp["o_norm"]), "convw": convw,
        "alog": bc(inp["a_log"].reshape(NL, 8)), "dtb": bc(inp["dt_bias"].reshape(NL, 8)),
        "fnorm": np.ascontiguousarray(inp["final_norm"].reshape(8, 128).T),
        "cckv": inp["cache_ckv"][bs], "ckpe": inp["cache_kpe"][bs], "sgdn": inp["state_gdn"][bs].reshape(NL, 8, 128, 128),
        "ident": np.eye(128, dtype=f), "umask": umask, "masks": masks, "rope": rope, "lmask": lmask,
    }
    return {k_: np.ascontiguousarray(v, dtype=f) for k_, v in d.items()}


_NC_CACHE = {}


def kernel(**inputs):
    inp = {k_: np.asarray(v, dtype=np.float32) for k_, v in inputs.items()}
    if "nc" not in _NC_CACHE:
        _NC_CACHE["nc"] = build()
    nc = _NC_CACHE["nc"]
    in_maps = [_host_inputs(inp, c) for c in range(8)]
    res = run_bass_kernel_spmd(nc, in_maps, core_ids=list(range(8)))
    R = res.results
    y_prompt = np.zeros((32, 256, 1024), np.float32)
    y_sample = np.zeros((4, 1024, 1024), np.float32)
    new_ckv = np.zeros((32, NL, 256, 128), np.float32)
    new_kpe = np.zeros((32, NL, 256, 32), np.float32)
    new_state = np.zeros((32, NL, 2, 4, 128, 128), np.float32)
    for c in range(8):
        yT = np.asarray(R[c]["yT"])
        y = yT.transpose(2, 1, 0).reshape(2048, 1024)
        if c < 4:
            y_sample[c] = y[:1024]
        y_prompt[4 * c:4 * c + 4] = y[1024:].reshape(4, 256, 1024)
        new_ckv[4 * c:4 * c + 4] = np.asarray(R[c]["nckv"]).reshape(NL, 4, 256, 128).transpose(1, 0, 2, 3)
        new_kpe[4 * c:4 * c + 4] = np.asarray(R[c]["nkpe"]).reshape(NL, 4, 256, 32).transpose(1, 0, 2, 3)
        new_state[4 * c:4 * c + 4] = np.asarray(R[c]["nst"]).reshape(4, NL, 2, 4, 128, 128)
    return (y_prompt, y_sample, new_ckv, new_kpe, new_state)
```

```python
import numpy as np
from contextlib import ExitStack
import concourse.bass as bass
import concourse.mybir as mybir
from concourse.bass_utils import run_bass_kernel_spmd

F32 = mybir.dt.float32
BF16 = mybir.dt.bfloat16
ALU = mybir.AluOpType
AF = mybir.ActivationFunctionType

NL = 4
EPS = 1e-6
BIG = 30000.0


class Buf:
    __slots__ = ("name", "t", "w", "r", "ps")

    def __init__(self, name, t=None):
        self.name = name
        self.t = t
        self.w = None
        self.r = []
        self.ps = False

    def __getitem__(self, idx):
        return self.t[idx]


def alias(new_bufs, old_bufs):
    evs = []
    for o in old_bufs:
        if o.w is not None:
            evs.append(o.w)
        evs.extend(o.r)
    for n in new_bufs:
        n.w = None
        n.r = list(evs)

    def __getitem__(self, idx):
        return self.t[idx]


class KB:
    def __init__(self, nc, stack, n_dma_sems=8):
        self.nc = nc
        self.stack = stack
        self.eng = {"pe": nc.tensor, "act": nc.scalar, "dve": nc.vector, "pool": nc.gpsimd, "sp": nc.sync}
        self.sem = {}
        self.cnt = {}
        for e in ["pe", "act", "dve", "pool"]:
            self.sem[e] = stack.enter_context(nc.semaphore("s_" + e))
            self.cnt[e] = 0
        self.dsem = []
        for i in range(n_dma_sems):
            k = "d%d" % i
            self.sem[k] = stack.enter_context(nc.semaphore("s_" + k))
            self.cnt[k] = 0
            self.dsem.append(k)
        self.dnext = 0
        self.waited = {}
        self.nops = 0

    def sb(self, name, shape, dt):
        t = self.stack.enter_context(self.nc.sbuf_tensor("sb_" + name, list(shape), dt))
        return Buf(name, t)

    def ps(self, name, shape, dt=F32):
        t = self.stack.enter_context(self.nc.psum_tensor("ps_" + name, list(shape), dt))
        b = Buf(name, t)
        b.ps = True
        return b

    def _need(self, e, ev):
        if ev is None:
            return
        k, v = ev
        if k == e == "pe":
            return
        if v > self.cnt[k]:
            raise RuntimeError("wait on unsignaled op: %s needs %s=%d (have %d)" % (e, k, v, self.cnt[k]))
        if self.waited.get((e, k), 0) >= v:
            return
        self.waited[(e, k)] = v
        self.eng[e].wait_ge(self.sem[k], v)

    def _deps(self, e, rd, wr):
        for b in rd:
            self._need(e, b.w)
            if b.ps:
                for ev in b.r:
                    if ev[0] != e:
                        self._need(e, ev)
        for b in wr:
            self._need(e, b.w)
            for ev in b.r:
                self._need(e, ev)

    def op(self, e, fn, rd=(), wr=(), sig=True):
        self._deps(e, rd, wr)
        ins = fn()
        self.nops += 1
        if sig:
            self.cnt[e] += 1
            ins.then_inc(self.sem[e], 1)
            ev = (e, self.cnt[e])
        else:
            ev = (e, self.cnt[e] + 1)
        for b in wr:
            b.w = ev
            b.r = []
        for b in rd:
            b.r = [x for x in b.r if x[0] != e] + [ev]
        return ins

    def dma(self, out_ap, in_ap, rd=(), wr=(), q="sp"):
        k = self.dsem[self.dnext]
        self.dnext = (self.dnext + 1) % len(self.dsem)
        if self.cnt[k] > 0:
            self._need(q, (k, self.cnt[k]))
        self._deps(q, rd, wr)
        ins = self.eng[q].dma_start(out=out_ap, in_=in_ap)
        self.cnt[k] += 16
        ins.then_inc(self.sem[k], 16)
        ev = (k, self.cnt[k])
        for b in wr:
            b.w = ev
            b.r = []
        for b in rd:
            b.r = b.r + [ev]
        self.nops += 1
        return ins

    def finish(self):
        for k in self.dsem:
            if self.cnt[k] > 0:
                self._need("sp", (k, self.cnt[k]))


def build(n_layers=NL, dbg=False):
    nc = bass.Bass("TRN2", target_bir_lowering=False)

    def din(name, shape, dt=F32):
        return nc.dram_tensor(name, list(shape), dt, kind="ExternalInput").ap()

    def dout(name, shape, dt=F32):
        return nc.dram_tensor(name, list(shape), dt, kind="ExternalOutput").ap()

    xT_d = din("xT", [128, 8, 2048])
    cond_d = din("cond", [128, 8, 2])
    normw_d = din("normw", [128, NL, 8])
    wada_d = din("wada", [NL, 24, 128, 8, 128])
    bada_d = din("bada", [128, NL, 24])
    wtok_d = din("wtok", [NL, 1024, 1456])
    wfeat_d = din("wfeat", [NL, 1024, 1536])
    wqb_d = din("wqb", [NL, 256, 768])
    wkn_d = din("wkn", [NL, 128, 512])
    wv_d = din("wv", [NL, 128, 512])
    wout_d = din("wout", [NL, 8, 128, 8, 128])
    qan_d = din("qan", [128, NL, 256])
    kvn_d = din("kvn", [128, NL, 128])
    onb_d = din("onb", [128, NL, 128])
    convw_d = din("convw", [128, NL, 36])
    alog_d = din("alog", [128, NL, 8])
    dtb_d = din("dtb", [128, NL, 8])
    fnorm_d = din("fnorm", [128, 8])
    cckv_d = din("cckv", [NL, 256, 128])
    ckpe_d = din("ckpe", [NL, 256, 32])
    sgdn_d = din("sgdn", [NL, 8, 128, 128])
    ident_d = din("ident", [128, 128])
    umask_d = din("umask", [128, 2, 128])
    mask_d = din("masks", [128, 4, 128])
    rope_d = din("rope", [128, 8, 32])
    lmask_d = din("lmask", [128, 14, 128])

    yT_d = dout("yT", [128, 8, 2048])
    nckv_d = dout("nckv", [NL, 1024, 128])
    nkpe_d = dout("nkpe", [NL, 1024, 32])
    nst_d = dout("nst", [4, NL, 8, 128, 128])
    xs_d = nc.dram_tensor("xscr", [128, 8, 2048], F32).ap()
    dbg_d = dout("dbg", [128, 2048]) if dbg else None

    class StopBuild(Exception):
        pass

    with ExitStack() as st:
        k = KB(nc, st)
        V, A, G, T = nc.vector, nc.scalar, nc.gpsimd, nc.tensor

        def dve(fn, rd, wr):
            return k.op("dve", fn, rd, wr)

        def act(fn, rd, wr):
            return k.op("act", fn, rd, wr)

        def pool(fn, rd, wr):
            return k.op("pool", fn, rd, wr)

        def pe(fn, rd, wr, sig=True):
            return k.op("pe", fn, rd, wr, sig)

        ident_f = k.sb("ident_f", [128, 128], F32)
        ident_b = k.sb("ident_b", [128, 128], BF16)
        ones_f = k.sb("ones_f", [128, 128], F32)
        ones_b = k.sb("ones_b", [128, 128], BF16)
        mhalf = k.sb("mhalf", [128, 16], F32)
        umask = k.sb("umask", [128, 2, 128], F32)
        masks = k.sb("masks", [128, 4, 128], F32)
        rope = k.sb("rope", [128, 8, 32], F32)
        lmask = k.sb("lmask", [128, 14, 128], BF16)
        cond = k.sb("cond", [128, 8, 2], F32)
        scT = k.sb("scT", [128, 8, 2], F32)
        normw = k.sb("normw", [128, NL, 8], F32)
        bada = k.sb("bada", [128, NL, 24], F32)
        qan = k.sb("qan", [128, 256], F32)
        kvn = k.sb("kvn", [128, 128], F32)
        onb = k.sb("onb", [128, 128], F32)
        convw = k.sb("convw", [128, NL, 36], F32)
        nA = k.sb("nA", [128, NL, 8], F32)
        dtb = k.sb("dtb", [128, NL, 8], F32)
        fnorm = k.sb("fnorm", [128, 8], F32)
        modTs = [k.sb("modT%d" % i, [128, 24, 2], F32) for i in range(2)]
        amods = [k.sb("amod%d" % i, [128, 8, 2], F32) for i in range(2)]

        stg = [k.sb("stg%d" % i, [128, 1024], F32) for i in range(2)]
        wtok = k.sb("wtok", [128, 8, 1456], BF16)
        wfeat = k.sb("wfeat", [128, 8, 1536], BF16)
        wqb = k.sb("wqb", [128, 2, 768], BF16)
        wkn = k.sb("wkn", [128, 512], BF16)
        wv = k.sb("wv", [128, 512], BF16)
        woutm = [k.sb("woutm%d" % i, [128, 8, 128], BF16) for i in range(2)]

        mixT = k.sb("mixT", [128, 8, 1024], BF16)
        gb2 = k.sb("gb2", [128, 8, 512], BF16)
        xab = k.sb("xab", [128, 8, 16], F32)
        gall = k.sb("gall", [128, 8, 8], F32)
        svall = k.sb("svall", [128, 6, 8, 8], F32)
        ball = k.sb("ball", [128, 8, 8], F32)

        arX = k.sb("arX", [128, 8192], F32)
        arY = k.sb("arY", [128, 6144], F32)
        arZ = k.sb("arZ", [128, 7424], F32)

        def view(ar, name, off, nwords, dt, shape, p1=128):
            ap = ar.t[0:p1, off:off + nwords]
            if dt == BF16:
                ap = ap.bitcast(BF16)
            if len(shape) == 3:
                ap = ap.rearrange("p (a b) -> p a b", b=shape[2])
            elif len(shape) == 4:
                ap = ap.rearrange("p (a b c) -> p a b c", b=shape[2], c=shape[3])
            assert tuple(ap.shape) == tuple(shape), (name, ap.shape, shape)
            return Buf(name, ap)

        xblk = view(arX, "xblk", 0, 4096, F32, [128, 8, 512])
        xblk2 = view(arX, "xblk2", 4096, 4096, F32, [128, 8, 512])
        xall = view(arX, "xall", 0, 8192, F32, [128, 8, 1024])
        xm = [Buf("xall%d" % m_, xall.t[:, m_, :]) for m_ in range(8)]
        qT = view(arX, "qT", 0, 2048, BF16, [96, 4, 1024], p1=96)
        kT = view(arX, "kT", 2048, 2560, BF16, [96, 4, 1280], p1=96)
        vext = view(arX, "vext", 4608, 2600, BF16, [128, 10, 8, 65])
        zT = view(arX, "zT", 0, 6192, BF16, [128, 12, 1032])
        oacc = view(arX, "oacc", 0, 4096, F32, [128, 8, 512])
        ktok = view(arX, "ktok", 4096, 2048, BF16, [128, 8, 512])
        vtok = view(arX, "vtok", 6144, 2048, BF16, [128, 8, 512])
        hT = view(arY, "hT", 0, 4096, BF16, [128, 8, 1024])
        qlnT = view(arY, "qlnT", 4096, 1024, BF16, [128, 2, 1024])
        ckvT = view(arY, "ckvT", 5120, 640, BF16, [128, 1280])
        kpe = view(arY, "kpe", 5760, 320, F32, [128, 10, 32])
        cT = view(arY, "cT", 0, 6144, BF16, [128, 12, 1024])

        class Rot:
            def __init__(s, bufs):
                s.b = bufs
                s.i = 0

            def get(s):
                b = s.b[s.i]
                s.i = (s.i + 1) % len(s.b)
                return b

        def rot(name, shape, dt, n):
            return Rot([k.sb("%s%d" % (name, i), shape, dt) for i in range(n)])

        zo = [0]

        def zview(name, nwords, dt, shape):
            b = view(arZ, name, zo[0], nwords, dt, shape)
            zo[0] += nwords
            return b

        ga2 = zview("ga2", 2048, BF16, [128, 8, 512])
        oa = zview("oa", 2048, BF16, [128, 8, 512])
        ptb = Rot([zview("ptb%d" % i, 256, BF16, [128, 512]) for i in range(3)])
        qsb = Rot([zview("qsb%d" % i, 384, BF16, [128, 8, 96]) for i in range(2)])
        ksb = Rot([zview("ksb%d" % i, 384, BF16, [128, 8, 96]) for i in range(2)])
        Z_att = [ga2, oa] + ptb.b + qsb.b + ksb.b
        assert zo[0] <= 7424
        arW = k.sb("arW", [128, 4736], F32)
        wo = [0]

        def wview(name, nwords, dt, shape):
            b = view(arW, name, wo[0], nwords, dt, shape)
            wo[0] += nwords
            return b

        zA = Rot([wview("zA%d" % i, 432, F32, [128, 432]) for i in range(2)])
        th = Rot([wview("th%d" % i, 512, F32, [128, 512]) for i in range(3)])
        tb = Rot([wview("tb%d" % i, 256, BF16, [128, 512]) for i in range(2)])
        ckvs = Rot([wview("ckvs%d" % i, 128, F32, [128, 128]) for i in range(2)])
        ckvb = Rot([wview("ckvb%d" % i, 192, BF16, [128, 384]) for i in range(2)])
        dg = Rot([wview("dg%d" % i, 192, BF16, [128, 3, 128]) for i in range(2)])
        ctx_f = wview("ctx_f", 256, F32, [128, 2, 128])
        rsb = wview("rsb", 512, F32, [128, 512])
        W_small = zA.b + th.b + tb.b + ckvs.b + ckvb.b + dg.b + [ctx_f, rsb]
        assert wo[0] <= 4736, wo[0]

        def gdn_set(mk_f32, mk_bf16, tag):
            S = {}
            for nm in ("gE", "gEt", "gUg", "gu"):
                S[nm] = Rot([mk_f32(nm + tag)])
            for nm in ("gP", "gPt", "gXs", "gXst", "gY", "gin", "gvb", "gkb", "gkd", "gwT", "gvn"):
                S[nm] = Rot([mk_bf16(nm + tag)])
            S["gT"] = Rot([mk_bf16("gT%d%s" % (i, tag)) for i in range(2)])
            S["gTt"] = Rot([mk_bf16("gTt%d%s" % (i, tag)) for i in range(2)])
            S["all"] = [b for r in S.values() for b in r.b]
            return S

        zo[0] = 0
        set0 = gdn_set(lambda n: zview(n, 512, F32, [128, 4, 128]), lambda n: zview(n, 256, BF16, [128, 4, 128]), "a")
        Sf = [zview("Sf%d" % d, 512, F32, [128, 4, 128]) for d in range(2)]
        Sb = [zview("Sb%d" % d, 256, BF16, [128, 4, 128]) for d in range(2)]
        assert zo[0] <= 7424, zo[0]
        yo = [4096]

        def yview(name):
            b = view(arY, name, yo[0], 512, F32, [128, 4, 128])
            yo[0] += 512
            return b

        wo[0] = 0
        set1 = gdn_set(yview, lambda n: wview(n, 256, BF16, [128, 4, 128]), "b")
        assert yo[0] <= 6144 and wo[0] <= 4736
        Z_gdn = set0["all"] + Sf + Sb
        W_gdn = [b for b in set1["all"] if not b.name.startswith(("gE", "gUg", "gu"))]
        Y_gdn = [b for b in set1["all"] if b.name.startswith(("gE", "gUg", "gu"))]
        gsets = [set0, set1]

        stt = rot("stt", [128, 16], F32, 4)
        gst = rot("gst", [128, 8, 4], F32, 4)

        pp = [k.ps("pp%d" % i, [128, 512], F32) for i in range(4)]
        pacc = [k.ps("pacc%d" % i, [128, 512], F32) for i in range(2)]
        ptr = [k.ps("ptr%d" % i, [128, 1024], BF16) for i in range(2)]
        rr = {"pp": 0, "pacc": 0, "ptr": 0, "pp6": 0}
        pm_ap = ptr[0].t[:, 512:1024].bitcast(F32)

        pp6 = pp + pacc

        def nxt(kind):
            lst = {"pp": pp, "pacc": pacc, "ptr": ptr, "pp6": pp6}[kind]
            b = lst[rr[kind] % len(lst)]
            rr[kind] += 1
            return b

        k.dma(ident_f[:], ident_d, wr=[ident_f])
        k.dma(umask[:], umask_d, wr=[umask])
        k.dma(masks[:], mask_d, wr=[masks])
        k.dma(rope[:], rope_d, wr=[rope])
        k.dma(cond[:], cond_d, wr=[cond])
        k.dma(normw[:], normw_d, wr=[normw])
        k.dma(bada[:], bada_d, wr=[bada])
        k.dma(convw[:], convw_d, wr=[convw])
        k.dma(nA[:], alog_d, wr=[nA])
        k.dma(dtb[:], dtb_d, wr=[dtb])
        k.dma(fnorm[:], fnorm_d, wr=[fnorm])
        pool(lambda: G.tensor_copy(out=ident_b[:], in_=ident_f[:]), [ident_f], [ident_b])
        for hf in range(2):
            s_ = stg[hf]
            sv_ = s_[:, 0:896].rearrange("p (a f) -> p a f", f=128)
            k.dma(sv_, lmask_d[:, hf * 7:(hf + 1) * 7, :], wr=[s_])
            pool(lambda: G.tensor_copy(out=lmask[:, hf * 7:(hf + 1) * 7, :], in_=sv_), [s_], [lmask])
        pool(lambda: G.memset(ones_f[:], 1.0), [], [ones_f])
        pool(lambda: G.memset(ones_b[:], 1.0), [], [ones_b])
        pool(lambda: G.memset(mhalf[:], -0.5), [], [mhalf])
        act(lambda: A.activation(out=nA[:], in_=nA[:], func=AF.Exp), [nA], [nA])
        dve(lambda: V.tensor_scalar(out=nA[:], in0=nA[:], scalar1=-1.0, scalar2=None, op0=ALU.mult), [nA], [nA])
        act(lambda: A.activation(out=scT[:], in_=cond[:], func=AF.Tanh, scale=0.5), [cond], [scT])
        dve(lambda: V.scalar_tensor_tensor(out=scT[:], in0=scT[:], scalar=1.0, in1=cond[:], op0=ALU.add, op1=ALU.mult), [scT, cond], [scT])
        dve(lambda: V.tensor_scalar(out=scT[:], in0=scT[:], scalar1=0.5, scalar2=None, op0=ALU.mult), [scT], [scT])

        stg_i = [0]

        lc_fixed = [False]

        def load_cast_tasks(dst_ap, dst_buf, src_ap, n):
            nch = (n + 1023) // 1024
            w = n // nch
            assert w * nch == n
            tasks = []
            for c in range(nch):
                def t_(c=c):
                    if lc_fixed[0]:
                        s = stg[0]
                    else:
                        s = stg[stg_i[0] % 2]
                        stg_i[0] += 1
                    k.dma(s[:, 0:w], src_ap[:, c * w:(c + 1) * w], wr=[s])
                    pool(lambda: G.tensor_copy(out=dst_ap[:, c * w:(c + 1) * w], in_=s[:, 0:w]), [s], [dst_buf])
                tasks.append(t_)
            return tasks

        def weight_tasks(l):
            ts = []
            for kk in range(8):
                ts += load_cast_tasks(wtok[:, kk, :], wtok, wtok_d[l, kk * 128:(kk + 1) * 128, :], 1456)
            for kk in range(8):
                ts += load_cast_tasks(wfeat[:, kk, :], wfeat, wfeat_d[l, kk * 128:(kk + 1) * 128, :], 1536)
            for kk in range(2):
                ts += load_cast_tasks(wqb[:, kk, :], wqb, wqb_d[l, kk * 128:(kk + 1) * 128, :], 768)
            ts += load_cast_tasks(wkn[:], wkn, wkn_d[l], 512)
            ts += load_cast_tasks(wv[:], wv, wv_d[l], 512)
            ts.append(lambda: k.dma(qan[:], qan_d[:, l, :], wr=[qan]))
            ts.append(lambda: k.dma(kvn[:], kvn_d[:, l, :], wr=[kvn]))
            return ts

        def mod_tasks(l, direct=False):
            modT, amod = modTs[l % 2], amods[l % 2]
            st_ = {}
            ts = []

            def abuf(ch):
                return stg[ch % 2] if direct else stg[1]

            def ada_dma(ch):
                s = abuf(ch)
                sv = s[:, 0:1024].rearrange("p (k c) -> p k c", c=128)
                k.dma(sv, wada_d[l, ch], wr=[s])

            def chunk(ch):
                s = abuf(ch)
                sv = s[:, 0:1024].rearrange("p (k c) -> p k c", c=128)
                for kk in range(8):
                    pe(lambda: T.matmul(pm_ap[:, ch * 2:ch * 2 + 2], sv[:, kk, :], scT[:, kk, :],
                                        start=(kk == 0), stop=(kk == 7), skip_group_check=True), [s, scT], [ptr[0]], sig=(kk == 7))
                nx = ch + (2 if direct else 1)
                if nx < 24:
                    ada_dma(nx)

            def fin():
                dve(lambda: V.tensor_tensor(out=modT[:], in0=pm_ap[:, 0:48].rearrange("p (c t) -> p c t", t=2),
                                            in1=bada[:, l, :].unsqueeze(2).to_broadcast([128, 24, 2]), op=ALU.add), [ptr[0], bada], [modT])
                dve(lambda: V.scalar_tensor_tensor(out=amod[:], in0=modT[:, 8:16, :], scalar=1.0,
                                                   in1=normw[:, l, :].unsqueeze(2).to_broadcast([128, 8, 2]),
                                                   op0=ALU.add, op1=ALU.mult), [modT, normw], [amod])
            ts.append(lambda: ada_dma(0))
            if direct:
                ts.append(lambda: ada_dma(1))
            for ch in range(24):
                ts.append(lambda ch=ch: chunk(ch))
            ts.append(fin)
            return ts

        bg = []

        def bg_step(n):
            for _ in range(n):
                if bg:
                    bg.pop(0)()

        xsrcs = [Buf("xsrc0"), Buf("xsrc1")]

        def fm_norm_stats(src, src_ap, nfeat, eps_):
            src_of = src if callable(src) else (lambda kk_: src)
            ps = nxt("pp")
            for kk in range(8):
                sq = th.get()
                act(lambda: A.activation(out=sq[:], in_=src_ap(kk), func=AF.Square), [src_of(kk)], [sq])
                pe(lambda: T.matmul(ps[:], ones_f[:], sq[:], start=(kk == 0), stop=(kk == 7)), [ones_f, sq], [ps], sig=True)
            t1 = rsb
            act(lambda: A.activation(out=t1[:], in_=ps[:], func=AF.Ln, scale=1.0 / nfeat, bias=eps_), [ps], [t1])
            act(lambda: A.activation(out=t1[:], in_=t1[:], func=AF.Exp, scale=-0.5), [t1], [t1])
            return t1

        def layer(l, last):
            pending_w = []
            if l == 0:
                for t_ in mod_tasks(0, direct=True):
                    t_()
                pending_w = weight_tasks(0)
            bg_step(len(bg))
            lc_fixed[0] = False
            modT, amod = modTs[l % 2], amods[l % 2]
            k.dma(onb[:], onb_d[:, l, :], wr=[onb])
            xin = xT_d if l == 0 else xs_d
            for grp in range(2):
                ci = 1 if grp == 0 else 0
                xsrc = xsrcs[grp]
                t0g = grp * 1024
                nseq = 1 if grp == 0 else 4
                tseq = 1024 // nseq
                tps = tseq // 128
                lp = tseq + 2

                def zpos(t):
                    return (t // tseq) * lp + 1 + (t % tseq)

                alias([xblk, xblk2], xm + [oacc, ktok, vtok, zT, qT, kT, vext])
                alias([hT, qlnT, ckvT, kpe], [cT] + Y_gdn)
                alias(Z_att, Z_gdn)
                alias(W_small, W_gdn)
                xbs = (xblk, xblk2)
                for blk in range(2):
                    c0 = t0g + blk * 512
                    k.dma(xbs[blk][:], xin[:, :, c0:c0 + 512], rd=[xsrc], wr=[xbs[blk]])
                for blk in range(2):
                    xb = xbs[blk]
                    rs = fm_norm_stats(xb, lambda kk_: xb[:, kk_, :], 1024, EPS)
                    for kk in range(8):
                        tmp = th.get()
                        dve(lambda: V.tensor_tensor(out=tmp[:], in0=xb[:, kk, :], in1=rs[:], op=ALU.mult), [xb, rs], [tmp])
                        act(lambda: A.activation(out=hT[:, kk, blk * 512:(blk + 1) * 512], in_=tmp[:], func=AF.Identity,
                                                 scale=amod[:, kk, ci:ci + 1], bias=modT[:, kk, ci:ci + 1]), [tmp, amod, modT], [hT])

                while pending_w:
                    pending_w.pop(0)()
                def p2_mm(tl_):
                    tc_ = tl_ * 128
                    bks = (nxt("pp6"), nxt("pp6"), nxt("pp6"))
                    for (pz, c0, n) in ((bks[0], 0, 432), (bks[1], 432, 512), (bks[2], 944, 512)):
                        for kk in range(8):
                            pe(lambda: T.matmul(pz[:, 0:n], hT[:, kk, tc_:tc_ + 128], wtok[:, kk, c0:c0 + n],
                                                start=(kk == 0), stop=(kk == 7)), [hT, wtok], [pz], sig=(kk == 7))
                    return bks

                bks_next = p2_mm(0)
                for tl in range(8):
                    tc0 = tl * 128
                    pzA, pga, pgb = bks_next
                    if tl + 1 < 8:
                        bks_next = p2_mm(tl + 1)
                    z = zA.get()
                    act(lambda: A.copy(out=z[:], in_=pzA[:, 0:432]), [pzA], [z])
                    if dbg and tl == 0 and grp == 0:
                        k.dma(dbg_d[:, 0:512], rsb[:], rd=[rsb])
                        k.dma(dbg_d[:, 512:560], modT[:].rearrange("p c t -> p (c t)"), rd=[modT])
                        k.dma(dbg_d[:, 560:576], amod[:].rearrange("p c t -> p (c t)"), rd=[amod])
                        k.dma(dbg_d[:, 576:592], scT[:].rearrange("p c t -> p (c t)"), rd=[scT])
                        tq = th.get()
                        pool(lambda: G.tensor_copy(out=tq[:], in_=hT[:, 0, 0:512]), [hT], [tq])
                        k.dma(dbg_d[:, 1024:1536], tq[:], rd=[tq])
                        k.dma(dbg_d[:, 1536:1968], z[:], rd=[z])
                        tq2 = th.get()
                        pool(lambda: G.tensor_copy(out=tq2[:], in_=wtok[:, 0, 0:512]), [wtok], [tq2])
                        k.dma(dbg_d[:, 592:1024], tq2[:, 0:432], rd=[tq2])
                        raise StopBuild()
                    for (pg, gdst) in ((pga, ga2), (pgb, gb2)):
                        t1 = th.get()
                        act(lambda: A.activation(out=t1[:], in_=pg[:], func=AF.Tanh, scale=0.5), [pg], [t1])
                        dve(lambda: V.scalar_tensor_tensor(out=gdst[:, tl, :], in0=t1[:], scalar=1.0, in1=pg[:],
                                                           op0=ALU.add, op1=ALU.mult), [t1, pg], [gdst])
                    s1 = stt.get()
                    junk = th.get()
                    pool(lambda: G.memset(s1[:], 0.0), [], [s1])
                    act(lambda: A.activation(out=junk[:, 0:256], in_=z[:, 0:256], func=AF.Square, accum_out=s1[:, 0:1]), [z], [junk, s1])
                    act(lambda: A.activation(out=junk[:, 256:384], in_=z[:, 256:384], func=AF.Square, accum_out=s1[:, 1:2]), [z], [junk, s1])
                    dve(lambda: V.tensor_scalar(out=s1[:, 2:3], in0=s1[:, 0:1], scalar1=1.0 / 256, scalar2=EPS, op0=ALU.mult, op1=ALU.add), [s1], [s1])
                    dve(lambda: V.tensor_scalar(out=s1[:, 3:4], in0=s1[:, 1:2], scalar1=1.0 / 128, scalar2=EPS, op0=ALU.mult, op1=ALU.add), [s1], [s1])
                    pool(lambda: G.tensor_tensor(out=s1[:, 4:6], in0=s1[:, 2:4], in1=mhalf[:, 0:2], op=ALU.pow), [s1, mhalf], [s1])
                    cb = ckvb.get()
                    dve(lambda: V.scalar_tensor_tensor(out=cb[:, 0:256], in0=z[:, 0:256], scalar=s1[:, 4:5], in1=qan[:],
                                                       op0=ALU.mult, op1=ALU.mult), [z, s1, qan], [cb])
                    cs = ckvs.get()
                    dve(lambda: V.scalar_tensor_tensor(out=cs[:], in0=z[:, 256:384], scalar=s1[:, 5:6], in1=kvn[:],
                                                       op0=ALU.mult, op1=ALU.mult), [z, s1, kvn], [cs])
                    pool(lambda: G.tensor_copy(out=cb[:, 256:384], in_=cs[:]), [cs], [cb])
                    if grp == 1:
                        k.dma(nckv_d[l, tc0:tc0 + 128, :], cs[:], rd=[cs])
                    pt = nxt("ptr")
                    for j in range(3):
                        pe(lambda: T.transpose(pt[:, j * 128:(j + 1) * 128], cb[:, j * 128:(j + 1) * 128], ident_b[:]), [cb, ident_b], [pt], sig=(j == 2))
                    dve(lambda: V.tensor_copy(out=qlnT[:, :, tc0:tc0 + 128], in_=pt[:, 0:256].rearrange("p (a b) -> p a b", b=128)), [pt], [qlnT])
                    act(lambda: A.copy(out=ckvT[:, tc0:tc0 + 128], in_=pt[:, 256:384]), [pt], [ckvT])
                    if grp == 0:
                        xv = z[:, 384:416].rearrange("p (a h f) -> p a h f", a=2, h=2)
                        cosv = rope[:, tl, 0:16].rearrange("p (a f) -> p a f", a=2)
                        sinv = rope[:, tl, 16:32].rearrange("p (a f) -> p a f", a=2)
                        ov = kpe[:, tl, :].rearrange("p (a h f) -> p a h f", a=2, h=2)
                        t1 = stt.get()
                        t1v = t1[:, 0:16].rearrange("p (a f) -> p a f", a=2)
                        pool(lambda: G.tensor_tensor(out=ov[:, :, 0, :], in0=xv[:, :, 0, :], in1=cosv, op=ALU.mult), [z, rope], [kpe])
                        pool(lambda: G.tensor_tensor(out=t1v, in0=xv[:, :, 1, :], in1=sinv, op=ALU.mult), [z, rope], [t1])
                        pool(lambda: G.tensor_tensor(out=ov[:, :, 0, :], in0=ov[:, :, 0, :], in1=t1v, op=ALU.subtract), [kpe, t1], [kpe])
                        pool(lambda: G.tensor_tensor(out=ov[:, :, 1, :], in0=xv[:, :, 1, :], in1=cosv, op=ALU.mult), [z, rope], [kpe])
                        pool(lambda: G.tensor_tensor(out=t1v, in0=xv[:, :, 0, :], in1=sinv, op=ALU.mult), [z, rope], [t1])
                        pool(lambda: G.tensor_tensor(out=ov[:, :, 1, :], in0=ov[:, :, 1, :], in1=t1v, op=ALU.add), [kpe, t1], [kpe])
                    else:
                        pool(lambda: G.tensor_copy(out=kpe[:, tl, :], in_=z[:, 384:416]), [z], [kpe])
                        k.dma(nkpe_d[l, tc0:tc0 + 128, :], z[:, 384:416], rd=[z])
                    pool(lambda: G.tensor_copy(out=xab[:, tl, :], in_=z[:, 416:432]), [z], [xab])

                t1 = stt.get()
                t2 = stt.get()
                t1v = t1[:, 0:64].rearrange("p (t c) -> p t c", c=8) if False else None
                gtmp = th.get()
                gv = gtmp[:, 0:64].rearrange("p (t c) -> p t c", c=8)
                dve(lambda: V.tensor_tensor(out=gv, in0=xab[:, :, 0:8], in1=dtb[:, l, :].unsqueeze(1).to_broadcast([128, 8, 8]), op=ALU.add), [xab, dtb], [gtmp])
                act(lambda: A.activation(out=gv, in_=gv, func=AF.Exp), [gtmp], [gtmp])
                act(lambda: A.activation(out=gv, in_=gv, func=AF.Ln, bias=1.0), [gtmp], [gtmp])
                dve(lambda: V.tensor_tensor(out=gall[:], in0=gv, in1=nA[:, l, :].unsqueeze(1).to_broadcast([128, 8, 8]), op=ALU.mult), [gtmp, nA], [gall])
                act(lambda: A.activation(out=ball[:], in_=xab[:, :, 8:16], func=AF.Tanh, scale=0.5), [xab], [ball])
                dve(lambda: V.tensor_scalar(out=ball[:], in0=ball[:], scalar1=0.5, scalar2=0.5, op0=ALU.mult, op1=ALU.add), [ball], [ball])
                pgc = nxt("pp")
                for tl_ in range(8):
                    for d_ in range(2):
                        c_ = tl_ * 8 + d_ * 4
                        pe(lambda: T.matmul(pgc[:, c_:c_ + 4], umask[:, d_, :], gall[:, tl_, d_ * 4:(d_ + 1) * 4], start=True, stop=True, skip_group_check=True),
                           [umask, gall], [pgc], sig=(tl_ == 7 and d_ == 1))
                pgv = pgc[:, 0:64].rearrange("p (t c) -> p t c", c=8)
                dve(lambda: V.tensor_copy(out=svall[:, 0], in_=pgv), [pgc], [svall])
                dve(lambda: V.tensor_scalar(out=svall[:, 1], in0=pgv, scalar1=-1.0, scalar2=None, op0=ALU.mult), [pgc], [svall])
                act(lambda: A.activation(out=svall[:, 2], in_=svall[:, 0], func=AF.Exp), [svall], [svall])
                dve(lambda: V.tensor_scalar(out=svall[:, 3], in0=ball[:], scalar1=-1.0, scalar2=None, op0=ALU.mult), [ball], [svall])
                dve(lambda: V.tensor_scalar(out=svall[:, 4], in0=ball[:], scalar1=0.5, scalar2=None, op0=ALU.mult), [ball], [svall])
                dve(lambda: V.tensor_tensor(out=svall[:, 5], in0=ball[:], in1=svall[:, 2], op=ALU.mult), [ball, svall], [svall])

                alias([qT, kT, vext], [xblk, xblk2])
                pool(lambda: G.memset(vext[:, :, :, 64:65], 2.0), [], [vext])
                nkt = 10 if grp == 0 else 8
                if grp == 0:
                    k.dma(ctx_f[:], cckv_d[l].rearrange("(a p) r -> p a r", p=128), wr=[ctx_f])
                    k.dma(kpe[:, 8:10, :], ckpe_d[l].rearrange("(a p) r -> p a r", p=128), wr=[kpe])
                    for a_ in range(2):
                        pc = nxt("pp")
                        pe(lambda: T.transpose(pc[:, 0:128], ctx_f[:, a_, :], ident_f[:]), [ctx_f, ident_f], [pc])
                        act(lambda: A.copy(out=ckvT[:, 1024 + a_ * 128:1024 + (a_ + 1) * 128], in_=pc[:, 0:128]), [pc], [ckvT])
                for kt in range(nkt):
                    pv = nxt("pp")
                    pe(lambda: T.matmul(pv[:], ckvT[:, kt * 128:(kt + 1) * 128], wv[:], start=True, stop=True), [ckvT, wv], [pv])
                    act(lambda: A.copy(out=vext[:, kt, :, 0:64], in_=pv[:].rearrange("p (h d) -> p h d", d=64)), [pv], [vext])
                for hg in range(2):
                    def k_mm(kt_):
                        pk_ = nxt("pp")
                        pe(lambda: T.matmul(pk_[:, 0:256], ckvT[:, kt_ * 128:(kt_ + 1) * 128], wkn[:, hg * 256:(hg + 1) * 256], start=True, stop=True), [ckvT, wkn], [pk_])
                        return pk_

                    pk_next = k_mm(0)
                    for kt in range(nkt):
                        pk = pk_next
                        if kt + 1 < nkt:
                            pk_next = k_mm(kt + 1)
                        ks = ksb.get()
                        dve(lambda: V.tensor_copy(out=ks[:, 0:4, 0:64], in_=pk[:, 0:256].rearrange("p (h d) -> p h d", d=64)), [pk], [ks])
                        pool(lambda: G.tensor_copy(out=ks[:, 0:4, 64:96], in_=kpe[:, kt, :].unsqueeze(1).to_broadcast([128, 4, 32])), [kpe], [ks])
                        pt = nxt("ptr")
                        for h in range(4):
                            pe(lambda: T.transpose(pt[0:96, h * 128:(h + 1) * 128], ks[:, h, :], ident_b[:]), [ks, ident_b], [pt], sig=(h == 3))
                        act(lambda: A.copy(out=kT[:, :, kt * 128:(kt + 1) * 128], in_=pt[0:96, 0:512].rearrange("p (h t) -> p h t", t=128)), [pt], [kT])
                    def q_mm(tl_):
                        tc_ = tl_ * 128
                        pq_ = nxt("pp")
                        for (c0, n, o0) in ((hg * 256, 256, 0), (512 + hg * 128, 128, 256)):
                            for kk in range(2):
                                pe(lambda: T.matmul(pq_[:, o0:o0 + n], qlnT[:, kk, tc_:tc_ + 128], wqb[:, kk, c0:c0 + n],
                                                    start=(kk == 0), stop=(kk == 1), skip_group_check=True), [qlnT, wqb], [pq_], sig=(kk == 1 and o0 == 256))
                        return pq_

                    pq_next = q_mm(0)
                    for tl in range(8):
                        tc0 = tl * 128
                        pq = pq_next
                        if tl + 1 < 8:
                            pq_next = q_mm(tl + 1)
                        qs = qsb.get()
                        dve(lambda: V.tensor_copy(out=qs[:, 0:4, 0:64], in_=pq[:, 0:256].rearrange("p (h d) -> p h d", d=64)), [pq], [qs])
                        if grp == 0:
                            xv = pq[:, 256:384].rearrange("p (h a g f) -> p h a g f", h=4, a=2, g=2)
                            ov = qs[:, 0:4, 64:96].rearrange("p h (a g f) -> p h a g f", a=2, g=2)
                            cosv = rope[:, tl, 0:16].rearrange("p (a f) -> p a f", a=2).unsqueeze(1).to_broadcast([128, 4, 2, 8])
                            sinv = rope[:, tl, 16:32].rearrange("p (a f) -> p a f", a=2).unsqueeze(1).to_broadcast([128, 4, 2, 8])
                            ta = th.get()
                            tav = ta[:, 0:64].rearrange("p (h a f) -> p h a f", h=4, a=2)
                            tbv = ta[:, 64:128].rearrange("p (h a f) -> p h a f", h=4, a=2)
                            dve(lambda: V.tensor_tensor(out=tav, in0=xv[:, :, :, 0, :], in1=cosv, op=ALU.mult), [pq, rope], [ta])
                            dve(lambda: V.tensor_tensor(out=tbv, in0=xv[:, :, :, 1, :], in1=sinv, op=ALU.mult), [pq, rope], [ta])
                            dve(lambda: V.tensor_tensor(out=ov[:, :, :, 0, :], in0=tav, in1=tbv, op=ALU.subtract), [ta], [qs])
                            dve(lambda: V.tensor_tensor(out=tav, in0=xv[:, :, :, 1, :], in1=cosv, op=ALU.mult), [pq, rope], [ta])
                            dve(lambda: V.tensor_tensor(out=tbv, in0=xv[:, :, :, 0, :], in1=sinv, op=ALU.mult), [pq, rope], [ta])
                            dve(lambda: V.tensor_tensor(out=ov[:, :, :, 1, :], in0=tav, in1=tbv, op=ALU.add), [ta], [qs])
                        else:
                            dve(lambda: V.tensor_copy(out=qs[:, 0:4, 64:96], in_=pq[:, 256:384].rearrange("p (h d) -> p h d", d=32)), [pq], [qs])
                        pt = nxt("ptr")
                        for h in range(4):
                            pe(lambda: T.transpose(pt[0:96, h * 128:(h + 1) * 128], qs[:, h, :], ident_b[:]), [qs, ident_b], [pt], sig=(h == 3))
                        act(lambda: A.copy(out=qT[:, :, tc0:tc0 + 128], in_=pt[0:96, 0:512].rearrange("p (h t) -> p h t", t=128)), [pt], [qT])
                    nq = 512 if grp == 0 else 256
                    for qb in range(1024 // nq):
                        q0 = qb * nq
                        if grp == 0:
                            kcs = list(range(10))
                        else:
                            kcs = [qb * 2, qb * 2 + 1]
                        nqt = nq // 128
                        for h in range(4):
                            hh = hg * 4 + h
                            pa = nxt("pacc")
                            pscs = {}

                            def qk(ki_):
                                kc_ = kcs[ki_]
                                p_ = nxt("pp")
                                pe(lambda: T.matmul(p_[:, 0:nq], kT[:, h, kc_ * 128:(kc_ + 1) * 128], qT[:, h, q0:q0 + nq], start=True, stop=True), [kT, qT], [p_])
                                pscs[ki_] = p_

                            qk(0)
                            if len(kcs) > 1:
                                qk(1)
                            for ki, kc in enumerate(kcs):
                                if ki + 2 < len(kcs):
                                    qk(ki + 2)
                                psc = pscs.pop(ki)
                                pb = ptb.get()
                                act(lambda: A.activation(out=pb[:, 0:nq], in_=psc[:, 0:nq], func=AF.Exp, scale=96.0 ** -0.5), [psc], [pb])
                                for qt in range(nqt):
                                    pe(lambda: T.matmul(pa[:, qt * 65:(qt + 1) * 65], pb[:, qt * 128:(qt + 1) * 128], vext[:, kc, hh, :],
                                                        start=(ki == 0 and qt == 0), stop=(ki == len(kcs) - 1), skip_group_check=True),
                                       [pb, vext], [pa], sig=(ki == len(kcs) - 1 and qt == nqt - 1))
                            s1 = stt.get()
                            dve(lambda: V.reciprocal(out=s1[:, 0:nqt], in_=pa[:, 0:nqt * 65].rearrange("p (q c) -> p q c", c=65)[:, :, 64]), [pa], [s1])
                            for qt in range(nqt):
                                tl = (q0 // 128) + qt
                                dve(lambda: V.scalar_tensor_tensor(out=oa[:, tl, hh * 64:(hh + 1) * 64], in0=pa[:, qt * 65:qt * 65 + 64], scalar=s1[:, qt:qt + 1],
                                                                   in1=ga2[:, tl, hh * 64:(hh + 1) * 64], op0=ALU.mult, op1=ALU.mult), [pa, s1, ga2], [oa])
                for tl in range(8):
                    pt = nxt("ptr")
                    for j in range(4):
                        pe(lambda: T.transpose(pt[:, j * 128:(j + 1) * 128], oa[:, tl, j * 128:(j + 1) * 128], ident_b[:]), [oa, ident_b], [pt], sig=(j == 3))
                    act(lambda: A.copy(out=mixT[:, 0:4, tl * 128:(tl + 1) * 128], in_=pt[:, 0:512].rearrange("p (j t) -> p j t", t=128)), [pt], [mixT])

                alias([zT], [qT, kT, vext])
                for s_ in range(nseq):
                    pool(lambda: G.memset(zT[:, :, s_ * lp:s_ * lp + 1], 0.0), [], [zT])
                    pool(lambda: G.memset(zT[:, :, s_ * lp + lp - 1:s_ * lp + lp], 0.0), [], [zT])
                seg = min(512, tseq)
                for j in range(12):
                    for blk in range(2):
                        pz = nxt("pp")
                        for kk in range(8):
                            pe(lambda: T.matmul(pz[:], wfeat[:, kk, j * 128:(j + 1) * 128], hT[:, kk, blk * 512:(blk + 1) * 512],
                                                start=(kk == 0), stop=(kk == 7)), [wfeat, hT], [pz], sig=(kk == 7))
                        for s0 in range(0, 512, seg):
                            t0 = blk * 512 + s0
                            if (j + blk) % 2 == 0:
                                act(lambda: A.copy(out=zT[:, j, zpos(t0):zpos(t0) + seg], in_=pz[:, s0:s0 + seg]), [pz], [zT])
                            else:
                                dve(lambda: V.tensor_copy(out=zT[:, j, zpos(t0):zpos(t0) + seg], in_=pz[:, s0:s0 + seg]), [pz], [zT])
                alias([cT], [hT, qlnT, ckvT, kpe])
                def mk_diag(j_):
                    d_ = dg.get()
                    for kk in range(3):
                        dve(lambda: V.tensor_scalar(out=d_[:, kk, :], in0=ident_b[:], scalar1=convw[:, l, kk * 12 + j_:kk * 12 + j_ + 1], scalar2=None, op0=ALU.mult), [ident_b, convw], [d_])
                    return d_

                d3_next = mk_diag(0)
                for j in range(12):
                    d3 = d3_next
                    if j + 1 < 12:
                        d3_next = mk_diag(j + 1)
                    for blk in range(2):
                        pz = nxt("pp")
                        for s0 in range(0, 512, seg):
                            t0 = blk * 512 + s0
                            p0 = zpos(t0) - 1
                            for kk in range(3):
                                pe(lambda: T.matmul(pz[:, s0:s0 + seg], d3[:, kk, :], zT[:, j, p0 + kk:p0 + kk + seg], start=(kk == 0), stop=(kk == 2)),
                                   [d3, zT], [pz], sig=(kk == 2 and s0 + seg == 512))
                        t1 = th.get()
                        act(lambda: A.activation(out=t1[:], in_=pz[:], func=AF.Tanh, scale=0.5), [pz], [t1])
                        dve(lambda: V.scalar_tensor_tensor(out=cT[:, j, blk * 512:(blk + 1) * 512], in0=t1[:], scalar=1.0, in1=pz[:], op0=ALU.add, op1=ALU.mult), [t1, pz], [cT])
                def l2_front(j_, blk_):
                    sl_ = slice(blk_ * 512, (blk_ + 1) * 512)
                    sq = tb.get()
                    act(lambda: A.activation(out=sq[:], in_=cT[:, j_, sl_], func=AF.Square), [cT], [sq])
                    ps_ = nxt("pp")
                    pe(lambda: T.matmul(ps_[:], ones_b[:], sq[:], start=True, stop=True), [ones_b, sq], [ps_])
                    return ps_

                items = [(j_, b_) for j_ in range(8) for b_ in range(2)]
                ps_next = l2_front(*items[0])
                for ii, (j, blk) in enumerate(items):
                    if True:
                        sl = slice(blk * 512, (blk + 1) * 512)
                        ps = ps_next
                        if ii + 1 < len(items):
                            ps_next = l2_front(*items[ii + 1])
                        t1 = th.get()
                        mul = 128.0 if j < 4 else 1.0
                        act(lambda: A.activation(out=t1[:], in_=ps[:], func=AF.Ln, scale=mul, bias=4.0 * EPS * mul), [ps], [t1])
                        act(lambda: A.activation(out=t1[:], in_=t1[:], func=AF.Exp, scale=-0.5), [t1], [t1])
                        dve(lambda: V.tensor_tensor(out=cT[:, j, sl], in0=cT[:, j, sl], in1=t1[:], op=ALU.mult), [cT, t1], [cT])
                alias([oacc, ktok, vtok], [zT])
                alias(Z_gdn, Z_att)
                alias(W_gdn, W_small)
                for tl in range(8):
                    pt = nxt("ptr")
                    for j in range(8):
                        pe(lambda: T.transpose(pt[:, j * 128:(j + 1) * 128], cT[:, 4 + j, tl * 128:(tl + 1) * 128], ident_b[:]), [cT, ident_b], [pt], sig=(j == 7))
                    act(lambda: A.copy(out=ktok[:, tl, :], in_=pt[:, 0:512]), [pt], [ktok])
                    dve(lambda: V.tensor_copy(out=vtok[:, tl, :], in_=pt[:, 512:1024]), [pt], [vtok])

                alias(Y_gdn, [cT])
                pool(lambda: G.memset(oacc[:], 0.0), [], [oacc])
                for s_ in range(nseq):
                    for d in range(2):
                        if grp == 0:
                            k.dma(Sf[d][:], sgdn_d[l, d * 4:(d + 1) * 4].rearrange("a p v -> p a v"), wr=[Sf[d]])
                            pool(lambda: G.tensor_copy(out=Sb[d][:], in_=Sf[d][:]), [Sf[d]], [Sb[d]])
                        else:
                            pool(lambda: G.memset(Sf[d][:], 0.0), [], [Sf[d]])
                            pool(lambda: G.memset(Sb[d][:], 0.0), [], [Sb[d]])
                    for step in range(tps):
                        prefetch = (grp == 1 and not last)
                        if prefetch and s_ == 0 and step == 0:
                            lc_fixed[0] = True
                            wt, mt = weight_tasks(l + 1), mod_tasks(l + 1)
                            bg.append(mt.pop(0))
                            while wt or mt:
                                if wt:
                                    bg.append(wt.pop(0))
                                if mt:
                                    bg.append(mt.pop(0))
                        more = step + 1 < tps
                        gens = [gdn_unit(l, s_ * tps + step, 0, (s_ * tps + step + 1) if more else None),
                                gdn_unit(l, s_ * tps + tps - 1 - step, 1, (s_ * tps + tps - 2 - step) if more else None)]
                        rounds = 0
                        while gens:
                            for g_ in list(gens):
                                try:
                                    next(g_)
                                except StopIteration:
                                    gens.remove(g_)
                            rounds += 1
                            if prefetch and rounds % 5 == 0:
                                bg_step(2)
                    if grp == 1:
                        for d in range(2):
                            k.dma(nst_d[s_, l, d * 4:(d + 1) * 4].rearrange("a p v -> p a v"), Sf[d][:], rd=[Sf[d]])
                alias(W_small, W_gdn)
                for tl in range(8):
                    junk = th.get()
                    act(lambda: A.activation(out=junk[:], in_=oacc[:, tl, :], func=AF.Square), [oacc], [junk])
                    dve(lambda: V.tensor_reduce(out=rsb[:, tl * 4:(tl + 1) * 4], in_=junk[:].rearrange("p (h f) -> p h f", f=128),
                                                axis=mybir.AxisListType.X, op=ALU.add), [junk], [rsb])
                act(lambda: A.activation(out=rsb[:, 32:64], in_=rsb[:, 0:32], func=AF.Ln, scale=1.0 / 128, bias=EPS), [rsb], [rsb])
                act(lambda: A.activation(out=rsb[:, 32:64], in_=rsb[:, 32:64], func=AF.Exp, scale=-0.5), [rsb], [rsb])
                dve(lambda: V.tensor_scalar(out=rsb[:, 32:64], in0=rsb[:, 32:64], scalar1=0.5, scalar2=None, op0=ALU.mult), [rsb], [rsb])
                for tl in range(8):
                    ob = tb.get()
                    for h in range(4):
                        t1 = th.get()
                        dve(lambda: V.scalar_tensor_tensor(out=t1[:, 0:128], in0=oacc[:, tl, h * 128:(h + 1) * 128], scalar=rsb[:, 32 + tl * 4 + h:33 + tl * 4 + h], in1=onb[:],
                                                           op0=ALU.mult, op1=ALU.mult), [oacc, rsb, onb], [t1])
                        dve(lambda: V.tensor_tensor(out=ob[:, h * 128:(h + 1) * 128], in0=t1[:, 0:128], in1=gb2[:, tl, h * 128:(h + 1) * 128], op=ALU.mult), [t1, gb2], [ob])
                    pt = nxt("ptr")
                    for j in range(4):
                        pe(lambda: T.transpose(pt[:, j * 128:(j + 1) * 128], ob[:, j * 128:(j + 1) * 128], ident_b[:]), [ob, ident_b], [pt], sig=(j == 3))
                    act(lambda: A.copy(out=mixT[:, 4:8, tl * 128:(tl + 1) * 128], in_=pt[:, 0:512].rearrange("p (j t) -> p j t", t=128)), [pt], [mixT])

                alias(xm, [oacc, ktok, vtok])

                def wload(m_):
                    s_ = stg[stg_i[0] % 2]
                    stg_i[0] += 1
                    sv_ = s_[:, 0:1024].rearrange("p (k c) -> p k c", c=128)
                    k.dma(sv_, wout_d[l, m_], wr=[s_])
                    pool(lambda: G.tensor_copy(out=woutm[m_ % 2][:], in_=sv_), [s_], [woutm[m_ % 2]])

                def xload(m_):
                    k.dma(xm[m_][:], xin[:, m_, t0g:t0g + 1024], rd=[xsrc], wr=[xm[m_]])

                wload(0)
                wload(1)
                xload(0)
                xload(1)
                for m in range(8):
                    wm = woutm[m % 2]
                    for blk in range(2):
                        bs_ = slice(blk * 512, (blk + 1) * 512)
                        po = nxt("pp")
                        for kk in range(8):
                            pe(lambda: T.matmul(po[:], wm[:, kk, :], mixT[:, kk, bs_], start=(kk == 0), stop=(kk == 7)), [wm, mixT], [po], sig=(kk == 7))
                        dve(lambda: V.scalar_tensor_tensor(out=xm[m][:, bs_], in0=po[:], scalar=modT[:, 16 + m, ci:ci + 1], in1=xm[m][:, bs_],
                                                           op0=ALU.mult, op1=ALU.add), [po, modT, xm[m]], [xm[m]])
                    if m + 2 < 8:
                        wload(m + 2)
                        xload(m + 2)
                    if not last:
                        k.dma(xs_d[:, m, t0g:t0g + 1024], xm[m][:], rd=[xm[m]], wr=[xsrc])
                if last:
                    for blk in range(2):
                        bs_ = slice(blk * 512, (blk + 1) * 512)
                        rs = fm_norm_stats(lambda kk_: xm[kk_], lambda kk_: xm[kk_][:, bs_], 1024, EPS)
                        for kk in range(8):
                            dve(lambda: V.scalar_tensor_tensor(out=xm[kk][:, bs_], in0=xm[kk][:, bs_], scalar=fnorm[:, kk:kk + 1], in1=rs[:],
                                                               op0=ALU.mult, op1=ALU.mult), [xm[kk], fnorm, rs], [xm[kk]])
                    for kk in range(8):
                        k.dma(yT_d[:, kk, t0g:t0g + 1024], xm[kk][:], rd=[xm[kk]])

        ug_ready = {}

        def gdn_unit(l, tl, d, nxt_tl=None):
            B = gsets[d]
            pbanks = ([pp[0], pp[1], pacc[0]], [pp[2], pp[3], pacc[1]])[d]
            pctr = [0]

            def nxt(kind):
                if kind == "ptr":
                    return ptr[d]
                b_ = pbanks[pctr[0] % 3]
                pctr[0] += 1
                return b_

            tcs = slice(tl * 128, (tl + 1) * 128)
            last = 127 if d == 0 else 0
            sv = gst.get()
            g4 = gall[:, tl, d * 4:(d + 1) * 4]
            RM = {0: 0, 1: 1, 2: 2, 5: 3, 6: 4, 7: 5}

            def SA(r_):
                return svall[:, RM[r_], tl, d * 4:(d + 1) * 4]

            def build_ug(tl_):
                ug_ = B["gUg"].get()
                dve(lambda: V.tensor_tensor(out=ug_[:], in0=umask[:, d, :].unsqueeze(1).to_broadcast([128, 4, 128]),
                                            in1=gall[:, tl_, d * 4:(d + 1) * 4].unsqueeze(2).to_broadcast([128, 4, 128]), op=ALU.mult), [umask, gall], [ug_])
                return ug_

            if ug_ready.get(d, (None, None))[0] == tl:
                ug = ug_ready.pop(d)[1]
            else:
                ug = build_ug(tl)
            ugf = ug[:].rearrange("p h f -> p (h f)")
            yield
            pb1, pb2 = nxt("pp"), nxt("pp")
            pe(lambda: T.matmul(pb1[:], ones_f[:], ugf, start=True, stop=False, skip_group_check=True), [ones_f, ug], [pb1], sig=False)
            for h in range(4):
                pe(lambda: T.matmul(pb1[:, h * 128:(h + 1) * 128], ident_f[:], masks[:, 2 * d, :], start=False, stop=(h == 3), skip_group_check=True), [ident_f, masks], [pb1], sig=(h == 3))
            pe(lambda: T.matmul(pb2[:], ones_f[:], ugf, start=True, stop=False, skip_group_check=True), [ones_f, ug], [pb2], sig=False)
            for h in range(4):
                pe(lambda: T.matmul(pb2[:, h * 128:(h + 1) * 128], ident_f[:], masks[:, 2 * d + 1, :], start=False, stop=(h == 3), skip_group_check=True), [ident_f, masks], [pb2], sig=(h == 3))
            E, Et = B["gE"].get(), B["gEt"].get()
            for h in range(4):
                act(lambda: A.activation(out=E[:, h, :], in_=pb1[:, h * 128:(h + 1) * 128], func=AF.Exp, scale=-1.0, bias=SA(0)[:, h:h + 1]), [pb1, svall], [E])
                act(lambda: A.activation(out=Et[:, h, :], in_=pb2[:, h * 128:(h + 1) * 128], func=AF.Exp, scale=1.0, bias=SA(1)[:, h:h + 1]), [pb2, svall], [Et])
            act(lambda: A.activation(out=sv[:, 3, :], in_=pb2[:].rearrange("p (h f) -> p h f", f=128)[:, :, last], func=AF.Exp), [pb2], [sv])
            dve(lambda: V.tensor_copy(out=sv[:, 4, :], in_=Et[:, :, last]), [Et], [sv])
            vb, kb, kd = B["gvb"].get(), B["gkb"].get(), B["gkd"].get()
            vt4 = vtok[:, tl, :].rearrange("p (h f) -> p h f", f=128)
            kt4 = ktok[:, tl, :].rearrange("p (h f) -> p h f", f=128)
            bc4 = [128, 4, 128]
            dve(lambda: V.tensor_tensor(out=vb[:], in0=vt4, in1=SA(6).unsqueeze(2).to_broadcast(bc4), op=ALU.mult), [vtok, svall], [vb])
            dve(lambda: V.tensor_tensor(out=kb[:], in0=kt4, in1=SA(7).unsqueeze(2).to_broadcast(bc4), op=ALU.mult), [ktok, svall], [kb])
            dve(lambda: V.tensor_tensor(out=kd[:], in0=kt4, in1=sv[:, 4, :].unsqueeze(2).to_broadcast(bc4), op=ALU.mult), [ktok, sv], [kd])
            yield
            pkk, pqk = nxt("pp"), nxt("pp")
            for h in range(4):
                pe(lambda: T.matmul(pkk[:, h * 128:(h + 1) * 128], cT[:, 4 + h, tcs], cT[:, 4 + h, tcs], start=True, stop=True), [cT], [pkk], sig=(h == 3))
            for h in range(4):
                pe(lambda: T.matmul(pqk[:, h * 128:(h + 1) * 128], cT[:, 4 + h, tcs], cT[:, h, tcs], start=True, stop=True), [cT], [pqk], sig=(h == 3))
            P = B["gP"].get()
            for h in range(4):
                dve(lambda: V.scalar_tensor_tensor(out=P[:, h, :], in0=pkk[:, h * 128:(h + 1) * 128], scalar=SA(5)[:, h:h + 1], in1=E[:, h, :],
                                                   op0=ALU.mult, op1=ALU.mult), [pkk, svall, E], [P])
            intra = B["gin"].get()
            dve(lambda: V.tensor_tensor(out=intra[:].rearrange("p h f -> p (h f)"), in0=pqk[:], in1=Et[:].rearrange("p h f -> p (h f)"), op=ALU.mult), [pqk, Et], [intra])
            yield
            pt = nxt("ptr")
            for h in range(4):
                pe(lambda: T.transpose(pt[:, h * 128:(h + 1) * 128], P[:, h, :], ident_b[:]), [P, ident_b], [pt], sig=(h == 3))
            Pt = B["gPt"].get()
            act(lambda: A.copy(out=Pt[:].rearrange("p h f -> p (h f)"), in_=pt[:, 0:512]), [pt], [Pt])
            idb4 = ident_b[:].unsqueeze(1).to_broadcast([128, 4, 128])

            def mA(lev):
                return lmask[:, 2 * lev + (0 if d == 0 else 1), :].unsqueeze(1).to_broadcast([128, 4, 128])

            def mB(lev):
                return lmask[:, 2 * lev + (1 if d == 0 else 0), :].unsqueeze(1).to_broadcast([128, 4, 128])

            yield
            Tc, Ttc = B["gT"].get(), B["gTt"].get()
            xs, xst = B["gXs"].get(), B["gXst"].get()
            dve(lambda: V.tensor_tensor(out=xs[:], in0=P[:], in1=mA(0), op=ALU.mult), [P, lmask], [xs])
            dve(lambda: V.tensor_tensor(out=Tc[:], in0=xs[:], in1=idb4, op=ALU.add), [xs, ident_b], [Tc])
            dve(lambda: V.tensor_tensor(out=xst[:], in0=Pt[:], in1=mB(0), op=ALU.mult), [Pt, lmask], [xst])
            dve(lambda: V.tensor_tensor(out=Ttc[:], in0=xst[:], in1=idb4, op=ALU.add), [xst, ident_b], [Ttc])
            for lev in range(1, 7):
                yield
                py = nxt("pp")
                for h in range(4):
                    pe(lambda: T.matmul(py[:, h * 128:(h + 1) * 128], Pt[:, h, :], Tc[:, h, :], start=True, stop=True), [Pt, Tc], [py], sig=(h == 3))
                Y = B["gY"].get()
                dve(lambda: V.tensor_tensor(out=Y[:], in0=py[:].rearrange("p (h f) -> p h f", f=128), in1=mA(lev), op=ALU.mult), [py, lmask], [Y])
                yield
                if lev < 6:
                    pm_ = nxt("pp")
                    for h in range(4):
                        pe(lambda: T.matmul(pm_[:, h * 128:(h + 1) * 128], Ttc[:, h, :], Y[:, h, :], start=(h == 0), stop=False, skip_group_check=True), [Ttc, Y], [pm_], sig=False)
                        pe(lambda: T.matmul(pm_[:, h * 128:(h + 1) * 128], ident_b[:], Tc[:, h, :], start=False, stop=(h == 3), skip_group_check=True), [ident_b, Tc], [pm_], sig=(h == 3))
                pmt = nxt("pp")
                for h in range(4):
                    pe(lambda: T.matmul(pmt[:, h * 128:(h + 1) * 128], Y[:, h, :], Ttc[:, h, :], start=(h == 0), stop=False, skip_group_check=True), [Ttc, Y], [pmt], sig=False)
                    pe(lambda: T.matmul(pmt[:, h * 128:(h + 1) * 128], ident_b[:], Ttc[:, h, :], start=False, stop=(h == 3), skip_group_check=True), [ident_b, Ttc], [pmt], sig=(h == 3))
                if lev < 6:
                    Tn = B["gT"].get()
                    act(lambda: A.copy(out=Tn[:].rearrange("p h f -> p (h f)"), in_=pm_[:]), [pm_], [Tn])
                    Tc = Tn
                Ttn = B["gTt"].get()
                if lev % 2 == 0:
                    dve(lambda: V.tensor_copy(out=Ttn[:].rearrange("p h f -> p (h f)"), in_=pmt[:]), [pmt], [Ttn])
                else:
                    act(lambda: A.copy(out=Ttn[:].rearrange("p h f -> p (h f)"), in_=pmt[:]), [pmt], [Ttn])
                Ttc = Ttn
            Tt = Ttc
            yield
            pu, pw = nxt("pp"), nxt("pp")
            for h in range(4):
                pe(lambda: T.matmul(pu[:, h * 128:(h + 1) * 128], Tt[:, h, :], vb[:, h, :], start=True, stop=True), [Tt, vb], [pu], sig=(h == 3))
            for h in range(4):
                pe(lambda: T.matmul(pw[:, h * 128:(h + 1) * 128], kb[:, h, :], Tt[:, h, :], start=True, stop=True), [Tt, kb], [pw], sig=(h == 3))
            u, wT = B["gu"].get(), B["gwT"].get()
            act(lambda: A.copy(out=u[:].rearrange("p h f -> p (h f)"), in_=pu[:]), [pu], [u])
            dve(lambda: V.tensor_copy(out=wT[:].rearrange("p h f -> p (h f)"), in_=pw[:]), [pw], [wT])
            yield
            pws = nxt("pp")
            for h in range(4):
                pe(lambda: T.matmul(pws[:, h * 128:(h + 1) * 128], wT[:, h, :], Sb[d][:, h, :], start=True, stop=True), [wT, Sb[d]], [pws], sig=(h == 3))
            vn = B["gvn"].get()
            dve(lambda: V.tensor_tensor(out=vn[:].rearrange("p h f -> p (h f)"), in0=u[:].rearrange("p h f -> p (h f)"), in1=pws[:], op=ALU.subtract), [u, pws], [vn])
            if nxt_tl is not None:
                ug_ready[d] = (nxt_tl, build_ug(nxt_tl))
            yield
            pqs, piv, pds = nxt("pp"), nxt("pp"), nxt("pp")
            for h in range(4):
                pe(lambda: T.matmul(pqs[:, h * 128:(h + 1) * 128], cT[:, h, tcs], Sb[d][:, h, :], start=True, stop=True), [cT, Sb[d]], [pqs], sig=(h == 3))
            for h in range(4):
                pe(lambda: T.matmul(piv[:, h * 128:(h + 1) * 128], intra[:, h, :], vn[:, h, :], start=True, stop=True), [intra, vn], [piv], sig=(h == 3))
            for h in range(4):
                pe(lambda: T.matmul(pds[:, h * 128:(h + 1) * 128], kd[:, h, :], vn[:, h, :], start=True, stop=True), [kd, vn], [pds], sig=(h == 3))
            for h in range(4):
                hs = slice(h * 128, (h + 1) * 128)
                dve(lambda: V.scalar_tensor_tensor(out=oacc[:, tl, hs], in0=pqs[:, hs], scalar=SA(2)[:, h:h + 1], in1=oacc[:, tl, hs],
                                                   op0=ALU.mult, op1=ALU.add), [pqs, svall, oacc], [oacc])
            dve(lambda: V.tensor_tensor(out=oacc[:, tl, :], in0=piv[:], in1=oacc[:, tl, :], op=ALU.add), [piv, oacc], [oacc])
            for h in range(4):
                dve(lambda: V.scalar_tensor_tensor(out=Sf[d][:, h, :], in0=Sf[d][:, h, :], scalar=sv[:, 3, h:h + 1], in1=pds[:, h * 128:(h + 1) * 128],
                                                   op0=ALU.mult, op1=ALU.add), [Sf[d], sv, pds], [Sf[d]])
            act(lambda: A.copy(out=Sb[d][:], in_=Sf[d][:]), [Sf[d]], [Sb[d]])
            yield

        try:
            for l in range(n_layers):
                layer(l, l == n_layers - 1)
        except StopBuild:
            pass
        k.finish()
        print("instructions:", k.nops)
    return nc


def _host_inputs(inp, core):
    f = np.float32
    bs = core % 4
    xs = inp["x_sample"][bs]
    xp = inp["x_prompt"][4 * core:4 * core + 4].reshape(1024, 1024)
    xall = np.concatenate([xs, xp], 0)
    xT = np.ascontiguousarray(xall.T.reshape(8, 128, 2048).transpose(1, 0, 2))
    cond2 = np.stack([inp["c_ctx"], inp["c"][bs]], 1)
    cond = np.ascontiguousarray(cond2.reshape(8, 128, 2).transpose(1, 0, 2))

    def fm(v, nch):
        return np.ascontiguousarray(v.reshape(v.shape[0], nch, 128).transpose(2, 0, 1))

    def bc(v):
        return np.ascontiguousarray(np.broadcast_to(v[None], (128,) + v.shape))

    w_in = inp["w_in"]
    o = np.cumsum([0, 256, 128, 32, 512, 512, 512, 512, 512, 8, 8])
    sl = lambda i: slice(o[i], o[i + 1])
    wtok = np.concatenate([w_in[:, :, sl(0)], w_in[:, :, sl(1)], w_in[:, :, sl(2)], w_in[:, :, sl(8)], w_in[:, :, sl(9)],
                           w_in[:, :, sl(3)], w_in[:, :, sl(7)]], 2)
    wfeat = np.concatenate([w_in[:, :, sl(4)], w_in[:, :, sl(5)], w_in[:, :, sl(6)]], 2)
    wqb = inp["w_qb"].reshape(NL, 256, 8, 96)
    wqb2 = np.concatenate([wqb[..., :64].reshape(NL, 256, 512), wqb[..., 64:].reshape(NL, 256, 256)], 2)
    wkvb = inp["w_kvb"].reshape(NL, 128, 8, 128)
    wkn = wkvb[..., :64].reshape(NL, 128, 512)
    wv = wkvb[..., 64:].reshape(NL, 128, 512)
    convw = np.ascontiguousarray(inp["conv_w"].reshape(NL, 3, 12, 128).transpose(3, 0, 1, 2).reshape(128, NL, 36))
    ar = np.arange(128)
    ufwd = (ar[:, None] <= ar[None, :]).astype(f)
    ubwd = (ar[:, None] >= ar[None, :]).astype(f)
    umask = np.stack([ufwd, ubwd], 1)
    p_, f_ = ar[:, None], ar[None, :]
    m1f = BIG * (f_ >= p_)
    nm2f = -BIG * (f_ < p_)
    m1b = BIG * (f_ <= p_)
    nm2b = -BIG * (f_ > p_)
    masks = np.stack([m1f, nm2f, m1b, nm2b], 1).astype(f)
    lm = []
    for lev in range(7):
        s_ = 1 << lev
        ml = ((p_ // (2 * s_)) == (f_ // (2 * s_))) & ((p_ % (2 * s_)) >= s_) & ((f_ % (2 * s_)) < s_)
        lm.append(ml.astype(f))
        lm.append(ml.T.astype(f))
    lmask = np.stack(lm, 1)
    t = np.arange(1024)
    inv = (10000.0 ** (-np.arange(8, dtype=f) / 8)).astype(f)
    ang = np.stack([(t // 64).astype(f)[:, None] * inv, (t % 64).astype(f)[:, None] * inv], 1)
    cs = np.concatenate([np.cos(ang).reshape(1024, 16), np.sin(ang).reshape(1024, 16)], 1).astype(f)
    rope = np.ascontiguousarray(cs.reshape(8, 128, 32).transpose(1, 0, 2))
    d = {
        "xT": xT, "cond": cond, "normw": fm(inp["norm_w"], 8),
        "wada": inp["w_ada"].reshape(NL, 8, 128, 24, 128).transpose(0, 3, 2, 1, 4), "bada": fm(inp["b_ada"], 24),
        "wtok": wtok, "wfeat": wfeat, "wqb": wqb2, "wkn": wkn, "wv": wv, "wout": inp["w_out"].reshape(NL, 8, 128, 8, 128).transpose(0, 3, 2, 1, 4),
        "qan": bc(inp["q_a_norm"]), "kvn": bc(inp["kv_a_norm"]), "onb": bc(inp["o_norm"]), "convw": convw,
        "alog": bc(inp["a_log"].reshape(NL, 8)), "dtb": bc(inp["dt_bias"].reshape(NL, 8)),
        "fnorm": np.ascontiguousarray(inp["final_norm"].reshape(8, 128).T),
        "cckv": inp["cache_ckv"][bs], "ckpe": inp["cache_kpe"][bs], "sgdn": inp["state_gdn"][bs].reshape(NL, 8, 128, 128),
        "ident": np.eye(128, dtype=f), "umask": umask, "masks": masks, "rope": rope, "lmask": lmask,
    }
    return {k_: np.ascontiguousarray(v, dtype=f) for k_, v in d.items()}


_NC_CACHE = {}


def kernel(**inputs):
    inp = {k_: np.asarray(v, dtype=np.float32) for k_, v in inputs.items()}
    if "nc" not in _NC_CACHE:
        _NC_CACHE["nc"] = build()
    nc = _NC_CACHE["nc"]
    in_maps = [_host_inputs(inp, c) for c in range(8)]
    res = run_bass_kernel_spmd(nc, in_maps, core_ids=list(range(8)))
    R = res.results
    y_prompt = np.zeros((32, 256, 1024), np.float32)
    y_sample = np.zeros((4, 1024, 1024), np.float32)
    new_ckv = np.zeros((32, NL, 256, 128), np.float32)
    new_kpe = np.zeros((32, NL, 256, 32), np.float32)
    new_state = np.zeros((32, NL, 2, 4, 128, 128), np.float32)
    for c in range(8):
        yT = np.asarray(R[c]["yT"])
        y = yT.transpose(2, 1, 0).reshape(2048, 1024)
        if c < 4:
            y_sample[c] = y[:1024]
        y_prompt[4 * c:4 * c + 4] = y[1024:].reshape(4, 256, 1024)
        new_ckv[4 * c:4 * c + 4] = np.asarray(R[c]["nckv"]).reshape(NL, 4, 256, 128).transpose(1, 0, 2, 3)
        new_kpe[4 * c:4 * c + 4] = np.asarray(R[c]["nkpe"]).reshape(NL, 4, 256, 32).transpose(1, 0, 2, 3)
        new_state[4 * c:4 * c + 4] = np.asarray(R[c]["nst"]).reshape(4, NL, 2, 4, 128, 128)
    return (y_prompt, y_sample, new_ckv, new_kpe, new_state)
```
